# Optimizing a Trainium2 kernel written in Bass

```python
import jax, jax.numpy as jnp
from jax import lax
import numpy as np

D_MODEL = 1024
BATCH = 4
SEQ = 4096
DEPTH = 2

CHUNK = 64
EPS = 1e-6
HEAD_DV = 64
RWKV_WIDTH = D_MODEL // 2
RWKV_HEADS = RWKV_WIDTH // HEAD_DV
RWKV_N = HEAD_DV
RWKV_DECAY_LORA = 64
RWKV_AAA_LORA = 64
RWKV_MV_LORA = 32
RWKV_GATE_LORA = 128
RWKV_GN_EPS = 64e-5
GLA_WIDTH = D_MODEL // 4
GLA_HEADS = GLA_WIDTH // HEAD_DV
GLA_DK = HEAD_DV // 2
GLA_GATE_LORA = 16
GLA_GATE_TAU = 16.0
RET_WIDTH = D_MODEL - RWKV_WIDTH - GLA_WIDTH
RET_HEADS = RET_WIDTH // HEAD_DV
RET_DK = HEAD_DV // 2
ROPE_BASE = 10000.0
MIX_WIDTH = RWKV_WIDTH + GLA_WIDTH + RET_WIDTH
IN_SIZES = (RWKV_WIDTH, RWKV_WIDTH, RWKV_WIDTH,
            GLA_HEADS * GLA_DK, GLA_HEADS * GLA_DK, GLA_WIDTH, GLA_WIDTH,
            RET_HEADS * RET_DK, RET_HEADS * RET_DK, RET_WIDTH, RET_WIDTH)
D_IN = sum(IN_SIZES)
D_FF = -(-8 * D_MODEL // (3 * 256)) * 256

kernel_name = 'hymba_rwkv7_gla_retnet_adaln_trunk'


def _token_shift(x):
    return jnp.pad(x, ((0, 0), (1, 0), (0, 0)))[:, :-1, :]


def _rmsnorm(x, g):
    x32 = x.astype(jnp.float32)
    y = x32 * lax.rsqrt(jnp.mean(x32 * x32, axis=-1, keepdims=True) + EPS)
    return (y * g.astype(jnp.float32)).astype(x.dtype)


def _head_layernorm(y, eps):
    mu = jnp.mean(y, axis=-1, keepdims=True)
    yc = y - mu
    return yc * lax.rsqrt(jnp.mean(yc * yc, axis=-1, keepdims=True) + eps)


def _rope_tables(seq):
    half = RET_DK // 2
    inv_freq = ROPE_BASE ** (-jnp.arange(half, dtype=jnp.float32) / half)
    ang = jnp.arange(seq, dtype=jnp.float32)[:, None] * inv_freq[None, :]
    return jnp.cos(ang), jnp.sin(ang)


def _rope(x, cos, sin):
    half = x.shape[-1] // 2
    x1, x2 = x[..., :half], x[..., half:]
    c, s = cos[None, :, None, :], sin[None, :, None, :]
    return jnp.concatenate([x1 * c - x2 * s, x2 * c + x1 * s], axis=-1)


def _rwkv7_mixer(h, r, k, v, v_first, mu_rkv, mu_x, w0, w1, w2, a0, a1, a2,
                 g1, g2, k_k, k_a, r_k, ln_g, ln_b, vres):
    f32 = jnp.float32
    h = h.astype(f32)
    B, S, _ = h.shape
    H, N = RWKV_HEADS, RWKV_N
    dh = _token_shift(h) - h
    xw = h + dh * mu_x[0]
    xa = h + dh * mu_x[1]
    xg = h + dh * mu_x[2]
    r = r + (_token_shift(r) - r) * mu_rkv[0]
    k = k + (_token_shift(k) - k) * mu_rkv[1]
    v = v + (_token_shift(v) - v) * mu_rkv[2]
    w = -jax.nn.softplus(-(w0 + jnp.tanh(xw @ w1) @ w2)) - 0.5
    decay = jnp.exp(-jnp.exp(w))
    a = jax.nn.sigmoid(a0 + (xa @ a1) @ a2)
    g = jax.nn.sigmoid(xg @ g1) @ g2
    kk = (k * k_k).reshape(B, S, H, N)
    kk = kk / jnp.maximum(jnp.sqrt(jnp.sum(kk * kk, axis=-1, keepdims=True)), 1e-12)
    k = k * (1.0 + (a - 1.0) * k_a)
    if vres is not None:
        mu_v, v0, v1, v2 = vres
        xv = h + dh * mu_v
        v = v + (v_first - v) * jax.nn.sigmoid(v0 + (xv @ v1) @ v2)

    def to_tm(t):
        return jnp.moveaxis(t.reshape(B, S, H, N), 1, 0)

    xs = (to_tm(r), to_tm(decay), to_tm(k), to_tm(v), jnp.moveaxis(kk, 1, 0), to_tm(a))

    def step(state, inp):
        r_t, w_t, k_t, v_t, kk_t, a_t = inp
        s_kk = jnp.einsum('bhvk,bhk->bhv', state, kk_t)
        state = (state * w_t[:, :, None, :]
                 - s_kk[..., None] * (kk_t * a_t)[:, :, None, :]
                 + v_t[..., None] * k_t[:, :, None, :])
        return state, jnp.einsum('bhvk,bhk->bhv', state, r_t)

    _, y = lax.scan(step, jnp.zeros((B, H, N, N), f32), xs)
    y = jnp.moveaxis(y, 0, 1)
    y = _head_layernorm(y, RWKV_GN_EPS).reshape(B, S, H * N) * ln_g + ln_b
    rh, kh, vh = (t.reshape(B, S, H, N) for t in (r, k, v))
    bonus = jnp.sum(rh * kh * r_k, axis=-1, keepdims=True) * vh
    return (y + bonus.reshape(B, S, H * N)) * g, v


def _gla_mixer(h, q, k, v, gate, a1, a2, ab, ln_g):
    f32 = jnp.float32
    h = h.astype(f32)
    B, S, _ = h.shape
    NC, H, DK, DV = S // CHUNK, GLA_HEADS, GLA_DK, HEAD_DV
    log_a = jax.nn.log_sigmoid((h @ a1) @ a2 + ab) / GLA_GATE_TAU

    def chunked(t, d):
        return t.reshape(B, NC, CHUNK, H, d).transpose(1, 0, 3, 2, 4)

    qc = chunked(q * DK ** -0.5, DK)
    kc = chunked(k, DK)
    vc = chunked(v, DV)
    bc = jnp.cumsum(chunked(log_a, DK), axis=3)
    b_end = bc[:, :, :, -1:, :]
    kv = jnp.einsum('nbhcd,nbhce->nbhde', kc * jnp.exp(b_end - bc), vc)

    def step(state, inp):
        kv_c, dec_c = inp
        return state * dec_c[..., None] + kv_c, state

    _, s_prev = lax.scan(step, jnp.zeros((B, H, DK, DV), f32), (kv, jnp.exp(b_end[:, :, :, 0, :])))
    inter = jnp.einsum('nbhcd,nbhde->nbhce', qc * jnp.exp(bc), s_prev)

    def intra(blk):
        q_, k_, v_, b_ = blk
        dec = jnp.exp(-jnp.abs(b_[:, :, :, None, :] - b_[:, :, None, :, :]))
        att = jnp.sum(q_[:, :, :, None, :] * k_[:, :, None, :, :] * dec, axis=-1)
        return jnp.einsum('bhnm,bhme->bhne', att, v_)

    o = inter + lax.map(intra, (qc, kc, vc, bc))
    o = o.transpose(1, 0, 3, 2, 4).reshape(B, S, H, DV)
    o = o * lax.rsqrt(jnp.mean(o * o, axis=-1, keepdims=True) + EPS) * ln_g
    return o.reshape(B, S, H * DV) * jax.nn.silu(gate)


def _retention_mixer(q, k, v, gate, cos, sin):
    f32 = jnp.float32
    B, S, _ = q.shape
    NC, H, DK, DV = S // CHUNK, RET_HEADS, RET_DK, HEAD_DV
    qh = _rope(q.reshape(B, S, H, DK), cos, sin) * DK ** -0.5
    kh = _rope(k.reshape(B, S, H, DK), cos, sin)
    log_gamma = jnp.log1p(-(2.0 ** (-5.0 - jnp.arange(H, dtype=f32))))
    pos = jnp.arange(CHUNK, dtype=f32)
    intra_dec = jnp.exp(log_gamma[:, None, None] * jnp.abs(pos[:, None] - pos[None, :]))
    k_dec = jnp.exp(log_gamma[None, :] * (CHUNK - 1.0 - pos)[:, None])
    q_dec = jnp.exp(log_gamma[None, :] * (pos + 1.0)[:, None])
    chunk_dec = jnp.exp(log_gamma * CHUNK)
    qc = qh.reshape(B, NC, CHUNK, H, DK)
    kc = kh.reshape(B, NC, CHUNK, H, DK)
    vc = v.reshape(B, NC, CHUNK, H, DV)
    scores = jnp.einsum('bnchd,bnmhd->bnhcm', qc, kc) * intra_dec
    intra = jnp.einsum('bnhcm,bnmhe->bnche', scores, vc)
    kv = jnp.einsum('bnmhd,bnmhe->nbhde', kc * k_dec[:, :, None], vc)

    def step(state, kv_c):
        return state * chunk_dec[:, None, None] + kv_c, state

    _, s_prev = lax.scan(step, jnp.zeros((B, H, DK, DV), f32), kv)
    inter = jnp.einsum('bnchd,nbhde->bnche', qc * q_dec[:, :, None], s_prev)
    o = _head_layernorm((intra + inter).reshape(B, S, H, DV), EPS)
    return o.reshape(B, S, H * DV) * jax.nn.silu(gate)


def setup_inputs(seed: int = 0) -> dict:
    key = jax.random.key(seed)
    ks = iter(jax.random.split(key, 48))
    f32 = jnp.float32

    def nrm(shape, scale):
        return jax.random.normal(next(ks), shape, f32) * scale

    def unif(shape, lo, hi):
        return jax.random.uniform(next(ks), shape, f32, lo, hi)

    L, D, RW = DEPTH, D_MODEL, RWKV_WIDTH
    return {
        'x': nrm((BATCH, SEQ, D), 1.0),
        'c': nrm((BATCH, D), 1.0),
        'ada_w': nrm((L, D, 6 * D), 0.02),
        'ada_b': nrm((L, 6 * D), 0.01),
        'norm1_g': 1.0 + nrm((L, D), 0.02),
        'norm2_g': 1.0 + nrm((L, D), 0.02),
        'w_in': nrm((L, D, D_IN), D ** -0.5),
        'w_out': nrm((L, MIX_WIDTH, D), MIX_WIDTH ** -0.5),
        'rk_mu_rkv': unif((L, 3, RW), 0.0, 1.0),
        'rk_mu_x': unif((L, 3, D), 0.0, 1.0),
        'rk_w0': unif((L, RW), -6.0, 1.0),
        'rk_w1': nrm((L, D, RWKV_DECAY_LORA), D ** -0.5),
        'rk_w2': nrm((L, RWKV_DECAY_LORA, RW), 0.1),
        'rk_a0': nrm((L, RW), 0.5),
        'rk_a1': nrm((L, D, RWKV_AAA_LORA), D ** -0.5),
        'rk_a2': nrm((L, RWKV_AAA_LORA, RW), 0.1),
        'rk_g1': nrm((L, D, RWKV_GATE_LORA), D ** -0.5),
        'rk_g2': nrm((L, RWKV_GATE_LORA, RW), RWKV_GATE_LORA ** -0.5),
        'rk_k_k': 0.85 + nrm((L, RW), 0.05),
        'rk_k_a': 1.0 + nrm((L, RW), 0.05),
        'rk_r_k': nrm((L, RWKV_HEADS, RWKV_N), 0.1),
        'rk_ln_g': 1.0 + nrm((L, RW), 0.02),
        'rk_ln_b': nrm((L, RW), 0.01),
        'rk_mu_v': unif((L - 1, D), 0.0, 1.0),
        'rk_v0': nrm((L - 1, RW), 0.5),
        'rk_v1': nrm((L - 1, D, RWKV_MV_LORA), D ** -0.5),
        'rk_v2': nrm((L - 1, RWKV_MV_LORA, RW), 0.1),
        'gla_a1': nrm((L, D, GLA_GATE_LORA), D ** -0.5),
        'gla_a2': nrm((L, GLA_GATE_LORA, GLA_HEADS * GLA_DK), GLA_GATE_LORA ** -0.5),
        'gla_ab': nrm((L, GLA_HEADS * GLA_DK), 0.1),
        'gla_ln_g': 1.0 + nrm((L, HEAD_DV), 0.02),
        'ffn_w_gate': nrm((L, D, D_FF), D ** -0.5),
        'ffn_w_up': nrm((L, D, D_FF), D ** -0.5),
        'ffn_w_down': nrm((L, D_FF, D), D_FF ** -0.5),
        'norm_f_g': 1.0 + nrm((D,), 0.02),
    }


def reference(x, c, ada_w, ada_b, norm1_g, norm2_g, w_in, w_out,
              rk_mu_rkv, rk_mu_x, rk_w0, rk_w1, rk_w2, rk_a0, rk_a1, rk_a2,
              rk_g1, rk_g2, rk_k_k, rk_k_a, rk_r_k, rk_ln_g, rk_ln_b,
              rk_mu_v, rk_v0, rk_v1, rk_v2,
              gla_a1, gla_a2, gla_ab, gla_ln_g,
              ffn_w_gate, ffn_w_up, ffn_w_down, norm_f_g):
    f32 = jnp.float32
    S = x.shape[1]
    cos, sin = _rope_tables(S)
    splits = np.cumsum(IN_SIZES)[:-1].tolist()
    cond = jax.nn.silu(c)
    v_first = None
    for l in range(DEPTH):
        mod = (cond @ ada_w[l] + ada_b[l])[:, None, :]
        sh1, sc1, gt1, sh2, sc2, gt2 = jnp.split(mod, 6, axis=-1)
        h = _rmsnorm(x, norm1_g[l]) * (1.0 + sc1) + sh1
        (rw_r, rw_k, rw_v, gl_q, gl_k, gl_v, gl_g,
         rt_q, rt_k, rt_v, rt_g) = jnp.split((h @ w_in[l]).astype(f32), splits, axis=-1)
        vres = None if l == 0 else (rk_mu_v[l - 1], rk_v0[l - 1], rk_v1[l - 1], rk_v2[l - 1])
        y_a, v_l = _rwkv7_mixer(h, rw_r, rw_k, rw_v, v_first, rk_mu_rkv[l], rk_mu_x[l],
                                rk_w0[l], rk_w1[l], rk_w2[l], rk_a0[l], rk_a1[l], rk_a2[l],
                                rk_g1[l], rk_g2[l], rk_k_k[l], rk_k_a[l], rk_r_k[l],
                                rk_ln_g[l], rk_ln_b[l], vres)
        if l == 0:
            v_first = v_l
        y_b = _gla_mixer(h, gl_q, gl_k, gl_v, gl_g, gla_a1[l], gla_a2[l], gla_ab[l], gla_ln_g[l])
        y_c = _retention_mixer(rt_q, rt_k, rt_v, rt_g, cos, sin)
        y = jnp.concatenate([y_a, y_b, y_c], axis=-1).astype(x.dtype) @ w_out[l]
        x = x + gt1 * y
        h = _rmsnorm(x, norm2_g[l]) * (1.0 + sc2) + sh2
        x = x + gt2 * ((jax.nn.silu(h @ ffn_w_gate[l]) * (h @ ffn_w_up[l])) @ ffn_w_down[l])
    return _rmsnorm(x, norm_f_g)
```

```python
import os
import numpy as np
from contextlib import ExitStack
import concourse.bass as bass
import concourse.mybir as mybir
from concourse.bass_utils import run_bass_kernel_spmd

F32 = mybir.dt.float32
BF16 = mybir.dt.bfloat16
AF = mybir.ActivationFunctionType
ALU = mybir.AluOpType
AX = mybir.AxisListType

D = 1024
SEQ = 4096
DEPTH = 2
DFF = 2816
NFF = DFF // 128
SEG = 1024
ST = 512
CH = 64
EPS = 1e-6
GN_EPS = 64e-5
CDEC = float(np.exp(-0.5))
FFG = 4
GPER = int(os.environ.get('GPER', '1'))
MIXMODE = int(os.environ.get('MIXMODE', '1'))
GIDLE = int(os.environ.get('GIDLE', '32'))
POOLENG = os.environ.get('POOLENG', 'dve')


class SemObj:
    __slots__ = ("h", "count")

    def __init__(self, h):
        self.h = h
        self.count = 0


class Slot:
    __slots__ = ("w", "r", "excl")

    def __init__(self, fence=None):
        self.w = None
        self.r = dict(fence) if fence else {}
        self.excl = False


class Tile:
    def __init__(self, t, fence=None):
        self.t = t
        self.s = Slot(fence)
        self._subs = {}
        self._fence = fence

    def sub(self, key):
        if key not in self._subs:
            self._subs[key] = Slot(self._fence)
        return self._subs[key]

    def all_slots(self):
        return [self.s] + list(self._subs.values())

    def __getitem__(self, k):
        return self.t[k]


class Eng:
    def __init__(self, e, sem, inorder=False):
        self.e = e
        self.sem = sem
        self.known = {}
        self.inorder = inorder


class Prog:
    def __init__(self, nc, es, n_dma_sems=24):
        self.nc = nc
        self.es = es
        mk = lambda n: SemObj(es.enter_context(nc.semaphore(n)))
        self.E = {
            "pe": Eng(nc.tensor, mk("s_pe"), inorder=True),
            "act": Eng(nc.scalar, mk("s_act")),
            "dve": Eng(nc.vector, mk("s_dve")),
            "pool": Eng(nc.gpsimd, mk("s_pool")),
            "sp": Eng(nc.sync, mk("s_sp")),
        }
        self.dsems = [mk("s_dma%d" % i) for i in range(n_dma_sems)]
        self.drr = 0
        self.fence = {}
        self.ninst = 0

    def sb(self, name, shape, dt, es=None):
        es = es or self.es
        self.nsb = getattr(self, "nsb", 0) + 1
        t = es.enter_context(self.nc.sbuf_tensor("sb%d_%s" % (self.nsb, name), list(shape), dt))
        return Tile(t, self.fence)

    def add_fence(self, tiles):
        f = dict(self.fence)
        for tl in tiles:
            for s in tl.all_slots():
                if s.w is not None:
                    f[s.w[0]] = max(f.get(s.w[0], 0), s.w[1])
                for so, v in s.r.items():
                    f[so] = max(f.get(so, 0), v)
        self.fence = f

    def _collect(self, reads, writes):
        need = {}
        for s in reads:
            if s.w is not None:
                need[s.w[0]] = max(need.get(s.w[0], 0), s.w[1])
            if s.excl:
                for so, v in s.r.items():
                    need[so] = max(need.get(so, 0), v)
        for s in writes:
            if s.w is not None:
                need[s.w[0]] = max(need.get(s.w[0], 0), s.w[1])
            for so, v in s.r.items():
                need[so] = max(need.get(so, 0), v)
        return need

    def _waits(self, E, need):
        for so, v in need.items():
            if v <= 0:
                continue
            if so is E.sem and E.inorder:
                continue
            if E.known.get(so, 0) < v:
                E.e.wait_ge(so.h, v)
                E.known[so] = v

    @staticmethod
    def _slots(xs):
        return [x.s if isinstance(x, Tile) else x for x in xs]

    stopped = False

    def op(self, eng, fn, r=(), w=()):
        if self.stopped:
            return None
        E = self.E[eng]
        r = self._slots(r)
        w = self._slots(w)
        self._waits(E, self._collect(r, w))
        inst = fn(E.e)
        E.sem.count += 1
        inst.then_inc(E.sem.h, 1)
        v = E.sem.count
        for s in r:
            s.r[E.sem] = v
        for s in w:
            s.w = (E.sem, v)
            s.r = {}
        self.ninst += 1
        return inst

    def dma(self, eng, out, in_, r=(), w=()):
        if self.stopped:
            return None
        E = self.E[eng]
        r = self._slots(r)
        w = self._slots(w)
        so = self.dsems[self.drr]
        self.drr = (self.drr + 1) % len(self.dsems)
        need = self._collect(r, w)
        if so.count > 0:
            need[so] = max(need.get(so, 0), so.count)
        self._waits(E, need)
        inst = E.e.dma_start(out=out, in_=in_)
        so.count += 16
        inst.then_inc(so.h, 16)
        for s in r:
            s.r[so] = so.count
        for s in w:
            s.w = (so, so.count)
            s.r = {}
        self.ninst += 1
        return inst


def _consts():
    c = {}
    c["ident"] = np.eye(128, dtype=np.float32)
    bd = np.zeros((128, 128), np.float32)
    bd[:64, :64] = 1.0
    bd[64:, 64:] = 1.0
    c["ones_bd"] = bd
    c["ones_mean"] = np.full((128, 128), 1.0 / D, np.float32)
    s = np.arange(64)[:, None]
    t = np.arange(64)[None, :]
    m_ap = np.concatenate([(s < t), (s <= t)], axis=1).astype(np.float32)
    c["m_ap"] = np.concatenate([m_ap, m_ap], axis=0)
    m_low = (t < s).astype(np.float32)
    c["m_low"] = np.concatenate([m_low, m_low], axis=0)
    ge = (t >= s).astype(np.float32)
    lt = (t < s).astype(np.float32)
    ge2 = np.concatenate([ge, ge], axis=1)
    lt2 = np.concatenate([lt, lt], axis=1)
    c["m_ge"] = np.concatenate([ge2, ge2], axis=0)
    c["m_lt"] = np.concatenate([lt2, lt2], axis=0)
    sm = np.ones((128, ST), np.float32)
    sm[:, ::CH] = 0.0
    c["scanmask"] = sm
    lg = np.log1p(-(2.0 ** (-5.0 - np.arange(4, dtype=np.float64))))
    pos = np.arange(64, dtype=np.float64)
    dk = 32
    retD = np.zeros((128, 2, 64), np.float64)
    qdec = np.zeros((128, 2, 64), np.float64)
    kdec = np.zeros((128, 2, 64), np.float64)
    cdec = np.zeros((128, 2), np.float64)
    for u in range(2):
        for hp in range(2):
            h = 2 * u + hp
            dm = np.exp(lg[h] * np.abs(pos[:, None] - pos[None, :])) * dk ** -0.5
            retD[64 * hp:64 * hp + 64, u, :] = dm
            qdec[64 * hp:64 * hp + 32, u, :] = (np.exp(lg[h] * (pos + 1.0)) * dk ** -0.5)[None, :]
            kdec[64 * hp:64 * hp + 32, u, :] = np.exp(lg[h] * (63.0 - pos))[None, :]
            cdec[64 * hp:64 * hp + 32, u] = np.exp(lg[h] * 64.0)
    c["retD"] = retD.reshape(128, 128).astype(np.float32)
    c["qdec"] = qdec.reshape(128, 128).astype(np.float32)
    c["kdec"] = kdec.reshape(128, 128).astype(np.float32)
    c["cdec"] = cdec.astype(np.float32)
    half = 16
    inv_freq = (10000.0 ** (-np.arange(half, dtype=np.float32) / half)).astype(np.float32)
    ang = np.arange(SEQ, dtype=np.float32)[:, None] * inv_freq[None, :]
    cos = np.cos(ang.astype(np.float64)).T
    sin = np.sin(ang.astype(np.float64)).T
    ropeC = np.zeros((128, SEQ), np.float64)
    ropeS = np.zeros((128, SEQ), np.float64)
    for hp in range(2):
        for j in range(2):
            rows = slice(64 * hp + 16 * j, 64 * hp + 16 * j + 16)
            ropeC[rows] = cos
            ropeS[rows] = -sin if j == 0 else sin
    c["ropeC"] = ropeC.astype(np.float32)
    c["ropeS"] = ropeS.astype(np.float32)
    return c


def _pcol_layout():
    off = {}
    n = 0

    def add(name, k):
        nonlocal n
        off[name] = n
        n += k
    for l in range(DEPTH):
        add(("n1g", l), 8)
        add(("n2g", l), 8)
        for j in range(3):
            add(("mux", l, j), 8)
        add(("muv", l), 8)
        for nm in ("w0", "a0", "kk", "ka", "rk", "lng", "lnb", "v0"):
            add((nm, l), 4)
        add(("glab", l), 2)
        add(("glng", l), 1)
        add(("adab", l), 48)
    add(("nfg",), 8)
    add(("cT",), 8)
    return off, n


def _col8(v):
    return np.ascontiguousarray(np.asarray(v, np.float32).reshape(8, 128).T)


def _col4(v):
    return np.ascontiguousarray(np.asarray(v, np.float32).reshape(4, 128).T)


def _build_pcol(inp, b):
    off, n = _pcol_layout()
    t = np.zeros((128, n), np.float32)
    for l in range(DEPTH):
        t[:, off[("n1g", l)]:off[("n1g", l)] + 8] = _col8(inp["norm1_g"][l])
        t[:, off[("n2g", l)]:off[("n2g", l)] + 8] = _col8(inp["norm2_g"][l])
        for j in range(3):
            t[:, off[("mux", l, j)]:off[("mux", l, j)] + 8] = _col8(inp["rk_mu_x"][l, j])
        if l >= 1:
            t[:, off[("muv", l)]:off[("muv", l)] + 8] = _col8(inp["rk_mu_v"][l - 1])
            t[:, off[("v0", l)]:off[("v0", l)] + 4] = _col4(inp["rk_v0"][l - 1])
        for nm, key in (("w0", "rk_w0"), ("a0", "rk_a0"), ("kk", "rk_k_k"), ("ka", "rk_k_a"),
                        ("lng", "rk_ln_g"), ("lnb", "rk_ln_b")):
            t[:, off[(nm, l)]:off[(nm, l)] + 4] = _col4(inp[key][l])
        t[:, off[("rk", l)]:off[("rk", l)] + 4] = _col4(np.asarray(inp["rk_r_k"][l]).reshape(512))
        gab = np.asarray(inp["gla_ab"][l], np.float32)
        for u in range(2):
            for hh in range(2):
                t[64 * hh:64 * hh + 32, off[("glab", l)] + u] = gab[64 * u + 32 * hh:64 * u + 32 * hh + 32]
        lg_ = np.asarray(inp["gla_ln_g"][l], np.float32)
        t[:, off[("glng", l)]] = np.concatenate([lg_, lg_])
        t[:, off[("adab", l)]:off[("adab", l)] + 48] = np.asarray(inp["ada_b"][l], np.float32).reshape(48, 128).T
    t[:, off[("nfg",)]:off[("nfg",)] + 8] = _col8(inp["norm_f_g"])
    t[:, off[("cT",)]:off[("cT",)] + 8] = _col8(inp["c"][b])
    return t


class _Stop(Exception):
    pass


def build_program(nseg=SEQ // SEG, nlayer=DEPTH, dbg=None, stop=None):
    nc = bass.Bass("TRN2", target_bir_lowering=False)
    ntok = nseg * SEG
    poff, pn = _pcol_layout()
    cst = _consts()

    def din(name, shape):
        return nc.dram_tensor(name, list(shape), F32, kind="ExternalInput").ap()

    x_d = din("x", [SEQ, D])
    pcol_d = din("pcol", [128, pn])
    ada_w = din("ada_w", [DEPTH, D, 6 * D])
    w_in = din("w_in", [DEPTH, D, 3072])
    w_out = din("w_out", [DEPTH, D, D])
    mu_rkv = din("rk_mu_rkv", [DEPTH, 3, 512])
    rk_w1 = din("rk_w1", [DEPTH, D, 64])
    rk_a1 = din("rk_a1", [DEPTH, D, 64])
    rk_g1 = din("rk_g1", [DEPTH, D, 128])
    rk_v1 = din("rk_v1", [DEPTH - 1, D, 32])
    gla_a1 = din("gla_a1", [DEPTH, D, 16])
    rk_w2 = din("rk_w2", [DEPTH, 64, 512])
    rk_a2 = din("rk_a2", [DEPTH, 64, 512])
    rk_g2 = din("rk_g2", [DEPTH, 128, 512])
    rk_v2 = din("rk_v2", [DEPTH - 1, 32, 512])
    gla_a2 = din("gla_a2", [DEPTH, 16, 128])
    gla_lng = din("gla_ln_g", [DEPTH, 64])
    wg_d = din("ffn_w_gate", [DEPTH, D, DFF])
    wu_d = din("ffn_w_up", [DEPTH, D, DFF])
    wd_d = din("ffn_w_down", [DEPTH, DFF, D])
    cd = {k: din("c_" + k, v.shape) for k, v in cst.items()}
    out_d = nc.dram_tensor("out", [SEQ, D], F32, kind="ExternalOutput").ap()
    wab_d = nc.dram_tensor("wab_scr", [DEPTH * 4, 128, 8 * 768], BF16, kind="Internal").ap()
    lw_d = nc.dram_tensor("lw_scr", [DEPTH, 128, 2 * 8 * 304], BF16, kind="Internal").ap()
    wo_d = nc.dram_tensor("wo_scr", [DEPTH, 128, 8 * D], BF16, kind="Internal").ap()
    dbg_d = None
    if dbg is not None:
        dbg_d = nc.dram_tensor("dbg", [128, dbg[1]], F32, kind="ExternalOutput").ap()

    es = ExitStack()
    with es:
        P = Prog(nc, es)
        op, dma = P.op, P.dma

        psb = [Tile(es.enter_context(nc.psum_tensor("ps%d" % i, [128, 512], F32))) for i in range(8)]
        for t_ in psb:
            t_.s.excl = True
        prr = [0]

        held = set()

        def ps(hold=False):
            for _ in range(16):
                i = prr[0]
                prr[0] = (prr[0] + 1) % 8
                if i not in held:
                    if hold:
                        held.add(i)
                    return psb[i]
            raise RuntimeError("no free psum bank")

        def release(t):
            held.discard(psb.index(t))

        xT = P.sb("xT", [128, 8, SEG], F32)
        hT = P.sb("hT", [128, 8, SEG + 1], BF16)
        yT = P.sb("yT", [128, 8, SEG], BF16)
        vfirst = P.sb("vfirst", [128, 4, SEG], BF16)
        pcol = P.sb("pcol", [128, pn], F32)
        drv = P.sb("drv", [128, DEPTH, 64], F32)
        mod = P.sb("mod", [128, DEPTH, 48], F32)
        identf = P.sb("identf", [128, 128], F32)
        identb = P.sb("identb", [128, 128], BF16)
        ones_bd = P.sb("ones_bd", [128, 128], BF16)
        ones_mean = P.sb("ones_mean", [128, 128], BF16)
        m_ap = P.sb("m_ap", [128, 128], F32)
        m_low = P.sb("m_low", [128, 64], F32)
        m_ge = P.sb("m_ge", [128, 128], F32)
        m_lt = P.sb("m_lt", [128, 128], F32)
        scanmask = P.sb("scanmask", [128, ST], F32)
        retD = P.sb("retD", [128, 128], F32)
        qdec = P.sb("qdec", [128, 128], F32)
        kdec = P.sb("kdec", [128, 128], F32)
        cdec = P.sb("cdec", [128, 2], F32)
        up_gl = P.sb("up_gl", [48, DEPTH, 2, 128], BF16)
        epsc = P.sb("epsc", [128, 4], F32)
        glng = P.sb("glng", [128, DEPTH, 64], F32)
        wab_slots = [Slot() for _ in range(DEPTH * 4)]
        lw_slots = [Slot() for _ in range(DEPTH)]
        wo_slots = [Slot() for _ in range(DEPTH)]
        up_wa = P.sb("up_wa", [128, DEPTH, 512], BF16)
        up_g = P.sb("up_g", [128, DEPTH, 512], BF16)
        up_c = P.sb("up_c", [48, DEPTH, 512], BF16)
        Mf = P.sb("Mf", [128, DEPTH, 4, 64], F32)
        Mb = P.sb("Mb", [128, DEPTH, 4, 64], BF16)
        Sg = P.sb("Sg", [128, DEPTH, 2, 64], F32)
        Sr = P.sb("Sr", [128, DEPTH, 2, 64], F32)
        hcar = P.sb("hcar", [128, DEPTH, 8], BF16)
        mids = [P.sb("midA", [128, SEG], BF16), P.sb("midG", [128, SEG], BF16), P.sb("midC", [48, SEG], BF16)]

        def pc(key, k=1, j=0):
            o = poff[key] + j
            return pcol[:, o:o + k]

        dma("sp", pcol[:], pcol_d, w=[pcol])
        dma("sp", identf[:], cd["ident"], w=[identf])
        dma("pool", identb[:], cd["ident"], w=[identb])
        dma("pool", ones_bd[:], cd["ones_bd"], w=[ones_bd])
        dma("pool", ones_mean[:], cd["ones_mean"], w=[ones_mean])
        for tl, nm in ((m_ap, "m_ap"), (m_low, "m_low"), (m_ge, "m_ge"), (m_lt, "m_lt"), (scanmask, "scanmask"),
                       (retD, "retD"), (qdec, "qdec"), (kdec, "kdec"), (cdec, "cdec")):
            dma("sp", tl[:], cd[nm], w=[tl])
        for l in range(DEPTH):
            dma("sp", glng[:, l, :], gla_lng[l:l + 1, :].partition_broadcast(128), w=[glng])
        op("dve", lambda e: e.memset(epsc[:, 0:1], EPS), w=[epsc])
        op("dve", lambda e: e.memset(epsc[:, 1:2], GN_EPS), w=[epsc])
        op("dve", lambda e: e.memset(epsc[:, 2:3], 1e-24), w=[epsc])
        op("dve", lambda e: e.memset(epsc[:, 3:4], 1.0), w=[epsc])
        for tl in (Mf, Mb, Sg, Sr, hcar, up_gl):
            op("dve", lambda e, tl=tl: e.memset(tl[:], 0.0), w=[tl])

        for l in range(nlayer):
            dma("pool", up_wa[0:64, l, :], rk_w2[l], w=[up_wa])
            dma("pool", up_wa[64:128, l, :], rk_a2[l], w=[up_wa])
            dma("pool", up_g[:, l, :], rk_g2[l], w=[up_g])
            if l >= 1:
                dma("pool", up_c[0:32, l, :], rk_v2[l - 1], w=[up_c])
            for u_ in range(2):
                for hh_ in range(2):
                    dma("pool", up_gl[32:48, l, u_, 64 * hh_:64 * hh_ + 32], gla_a2[l, :, 64 * u_ + 32 * hh_:64 * u_ + 32 * hh_ + 32], w=[up_gl])

        with ExitStack() as s0:
            condb = P.sb("condb", [128, 8], BF16, s0)
            omu = P.sb("omu", [128, DEPTH, 4, 8], F32, s0)
            ldwf = P.sb("ldwf", [128, 8, 304], F32, s0)
            adaw = [P.sb("adaw%d" % i, [128, 8, 512], BF16, s0) for i in range(2)]
            lwab = P.sb("lwab", [128, 2, 8, 304], BF16, s0)
            wst = P.sb("wst", [128, 8, 384], F32, s0)
            murow = P.sb("murow", [128, 384], F32, s0)
            omurow = P.sb("omurow", [128, 384], F32, s0)
            WABs = P.sb("WABs", [128, 8, 2, 384], BF16, s0)
            scoped = [condb, omu, ldwf, lwab, wst, murow, omurow, WABs] + adaw
            op("act", lambda e: e.activation(out=condb[:], in_=pc(("cT",), 8), func=AF.Silu), r=[pcol], w=[condb])
            for l in range(nlayer):
                mps = ps()
                for piece in range(12):
                    aw = adaw[piece % 2]
                    dma("pool", aw[:], ada_w[l, :, piece * 512:(piece + 1) * 512].rearrange("(kc p) n -> p kc n", p=128), w=[aw])
                    for j in range(4):
                        jc = piece * 4 + j
                        for kc in range(8):
                            op("pe", lambda e, kc=kc, j=j, jc=jc, aw=aw: e.matmul(
                                mps[:, jc:jc + 1], aw[:, kc, j * 128:(j + 1) * 128], condb[:, kc:kc + 1],
                                start=(kc == 0), stop=(kc == 7)), r=[aw, condb], w=[mps])
                op("dve", lambda e, l=l, mps=mps: e.tensor_tensor(out=mod[:, l, :], in0=mps[:, 0:48], in1=pc(("adab", l), 48), op=ALU.add),
                   r=[mps, pcol], w=[mod])
                op("dve", lambda e, l=l: e.scalar_tensor_tensor(out=drv[:, l, 0:8], in0=mod[:, l, 8:16], scalar=1.0, in1=pc(("n1g", l), 8),
                                                                op0=ALU.add, op1=ALU.mult), r=[mod, pcol], w=[drv])
                op("dve", lambda e, l=l: e.scalar_tensor_tensor(out=drv[:, l, 8:16], in0=mod[:, l, 32:40], scalar=1.0, in1=pc(("n2g", l), 8),
                                                                op0=ALU.add, op1=ALU.mult), r=[mod, pcol], w=[drv])
                op("dve", lambda e, l=l: e.tensor_scalar(out=drv[:, l, 16:20], in0=pc(("ka", l), 4), scalar1=-1.0, scalar2=1.0,
                                                         op0=ALU.mult, op1=ALU.add), r=[pcol], w=[drv])
                for j in range(4):
                    src = pc(("mux", l, j), 8) if j < 3 else pc(("muv", l), 8)
                    op("dve", lambda e, l=l, j=j, src=src: e.tensor_scalar(out=omu[:, l, j, :], in0=src, scalar1=-1.0, scalar2=1.0,
                                                                           op0=ALU.mult, op1=ALU.add), r=[pcol], w=[omu])
                op("dve", lambda e: e.memset(ldwf[:], 0.0), w=[ldwf])
                rr = lambda a: a.rearrange("(kc p) n -> p kc n", p=128)
                dma("sp", ldwf[:, :, 0:64], rr(rk_w1[l]), w=[ldwf])
                dma("sp", ldwf[:, :, 64:128], rr(rk_a1[l]), w=[ldwf])
                dma("sp", ldwf[:, :, 128:256], rr(rk_g1[l]), w=[ldwf])
                if l >= 1:
                    dma("sp", ldwf[:, :, 256:288], rr(rk_v1[l - 1]), w=[ldwf])
                dma("sp", ldwf[:, :, 288:304], rr(gla_a1[l]), w=[ldwf])
                for kc in range(8):
                    for j, (c0, c1) in enumerate(((0, 64), (64, 128), (128, 256), (256, 288))):
                        mu_ap = (pc(("mux", l, j), 8) if j < 3 else pc(("muv", l), 8))[:, kc:kc + 1]
                        op("dve", lambda e, kc=kc, c0=c0, c1=c1, mu_ap=mu_ap, l=l: e.tensor_scalar(
                            out=lwab[:, 1, kc, c0:c1], in0=ldwf[:, kc, c0:c1], scalar1=mu_ap, scalar2=None, op0=ALU.mult),
                            r=[ldwf, pcol], w=[lwab])
                        op("dve", lambda e, kc=kc, c0=c0, c1=c1, j=j, l=l: e.tensor_scalar(
                            out=lwab[:, 0, kc, c0:c1], in0=ldwf[:, kc, c0:c1], scalar1=omu[:, l, j, kc:kc + 1], scalar2=None, op0=ALU.mult),
                            r=[ldwf, omu], w=[lwab])
                    op("dve", lambda e, kc=kc, l=l: e.tensor_copy(out=lwab[:, 0, kc, 288:304], in_=ldwf[:, kc, 288:304]), r=[ldwf], w=[lwab])
                    op("dve", lambda e, kc=kc, l=l: e.memset(lwab[:, 1, kc, 288:304], 0.0), w=[lwab])
                dma("sp", lw_d[l], lwab[:].rearrange("q a k n -> q (a k n)"), r=[lwab], w=[lw_slots[l]])
                for hf in range(2):
                    aw = adaw[hf]
                    dma("pool", aw[:], w_out[l, :, hf * 512:(hf + 1) * 512].rearrange("(kc q) n -> q kc n", q=128), w=[aw])
                    dma("sp", wo_d[l].rearrange("q (k n) -> q k n", k=8)[:, :, hf * 512:(hf + 1) * 512], aw[:], r=[aw], w=[wo_slots[l]])
                for p in range(4):
                    for j in range(3):
                        dma("sp", wst[:, :, j * 128:(j + 1) * 128],
                            w_in[l, :, j * 512 + p * 128:j * 512 + (p + 1) * 128].rearrange("(kc q) n -> q kc n", q=128), w=[wst])
                        dma("sp", murow[:, j * 128:(j + 1) * 128], mu_rkv[l, j:j + 1, p * 128:(p + 1) * 128].partition_broadcast(128), w=[murow])
                    op("dve", lambda e: e.tensor_scalar(out=omurow[:], in0=murow[:], scalar1=-1.0, scalar2=1.0, op0=ALU.mult, op1=ALU.add),
                       r=[murow], w=[omurow])
                    op("dve", lambda e: e.tensor_tensor(out=WABs[:, :, 1, :], in0=wst[:], in1=murow[:].unsqueeze(1).to_broadcast([128, 8, 384]), op=ALU.mult),
                       r=[wst, murow], w=[WABs])
                    op("dve", lambda e: e.tensor_tensor(out=WABs[:, :, 0, :], in0=wst[:], in1=omurow[:].unsqueeze(1).to_broadcast([128, 8, 384]), op=ALU.mult),
                       r=[wst, omurow], w=[WABs])
                    dma("sp", wab_d[l * 4 + p], WABs[:].rearrange("q k a n -> q (k a n)"), r=[WABs], w=[wab_slots[l * 4 + p]])
            P.add_fence(scoped)

        def rmsnorm_to_hT(gm, sh, es_l):
            sqb = P.sb("sqb", [128, 8, ST], BF16, es_l)
            rstd = P.sb("rstd", [128, ST], F32, es_l)
            tmpf = [P.sb("tmpf%d" % i, [128, ST], F32, es_l) for i in range(2)]
            for st in range(SEG // ST):
                ts = slice(st * ST, (st + 1) * ST)
                mps = ps()
                for c in range(8):
                    op("act", lambda e, c=c: e.activation(out=sqb[:, c, :], in_=xT[:, c, ts], func=AF.Square), r=[xT], w=[sqb.sub(c)])
                    op("pe", lambda e, c=c: e.matmul(mps[:], ones_mean[:], sqb[:, c, :], start=(c == 0), stop=(c == 7)),
                       r=[ones_mean, sqb.sub(c)], w=[mps])
                op("act", lambda e: e.activation(out=rstd[:], in_=mps[:], func=AF.Ln, bias=epsc[:, 0:1], scale=1.0), r=[mps, epsc], w=[rstd])
                op("act", lambda e: e.activation(out=rstd[:], in_=rstd[:], func=AF.Exp, scale=-0.5), r=[rstd], w=[rstd])
                for c in range(8):
                    tf = tmpf[c % 2]
                    op("dve", lambda e, c=c, tf=tf: e.scalar_tensor_tensor(out=tf[:], in0=xT[:, c, ts], scalar=gm[:, c:c + 1], in1=rstd[:],
                                                                          op0=ALU.mult, op1=ALU.mult), r=[xT, rstd, drv, pcol], w=[tf])
                    dst = hT[:, c, 1 + st * ST:1 + (st + 1) * ST]
                    if sh is not None:
                        op("act", lambda e, c=c, tf=tf, dst=dst: e.activation(out=dst, in_=tf[:], func=AF.Identity, bias=sh[:, c:c + 1], scale=1.0),
                           r=[tf, mod], w=[hT.sub(st)])
                    else:
                        op("act", lambda e, tf=tf, dst=dst: e.activation(out=dst, in_=tf[:], func=AF.Copy), r=[tf], w=[hT.sub(st)])
            return [sqb, rstd] + tmpf

        def hslots():
            return [hT.sub(i) for i in range(SEG // ST)] + [hT.sub("c0")]

        STR = int(os.environ.get('STR', '256'))
        NCR = STR // CH

        def run_streams(gens, periods=None):
            live = list(gens)
            per = dict((id(g_), (periods[i] if periods else 1)) for i, g_ in enumerate(gens))
            rnd = 0
            while live:
                for g_ in list(live):
                    if rnd % per[id(g_)] != 0 and len(live) > 1:
                        continue
                    try:
                        next(g_)
                    except StopIteration:
                        live.remove(g_)
                rnd += 1

        def rwkv_pair(l, p, es_u, tiles):
            def T(name, shape, dt):
                t = P.sb(name, shape, dt, es_u)
                tiles.append(t)
                return t
            WAB = T("WAB", [128, 8, 2, 384], BF16)
            dma("sp", WAB[:].rearrange("q k a n -> q (k a n)"), wab_d[l * 4 + p], r=[wab_slots[l * 4 + p]], w=[WAB])
            cs = slice(p * 128, (p + 1) * 128)
            col = lambda nm: pcol[:, poff[(nm, l)] + p:poff[(nm, l)] + p + 1]
            HP = (slice(0, 64), slice(64, 128))

            def v3(ap):
                return ap.rearrange("q (c t) -> q c t", c=NCR)

            def stream(sidx):
                f = lambda n: T(n, [128, STR], F32)
                sig, cum, a_t, g_t, r_t, k_t, v_t, kk_t, kt_t, tA, tB, eG, eI, eX, eE, bon = [f(n) for n in
                    ("sig", "cum", "a_t", "g_t", "r_t", "k_t", "v_t", "kk_t", "kt_t", "tA", "tB", "eG", "eI", "eX", "eE", "bon")]
                tbf = T("tbf", [128, STR], BF16)
                RK = T("RK", [128, NCR, 2, 64], BF16)
                LK = T("LK", [128, NCR, 2, 64], BF16)
                EF = T("EF", [128, 2, STR], BF16)
                vbf = T("vbf", [128, STR], BF16)
                Et = T("Et", [128, NCR, 2, 64], BF16)
                Vt = T("Vt", [128, NCR, 64], BF16)
                APs = T("APs", [128, NCR, 128], BF16)
                BQs = T("BQs", [128, NCR, 128], BF16)
                Apl = [T("Apl%d" % i, [128, NCR, 64], BF16) for i in range(2)]
                ATr = [T("ATr%d" % i, [128, NCR, 64], BF16) for i in range(2)]
                X = [T("X%d" % i, [128, NCR, 128], BF16) for i in range(2)]
                Gt = T("Gt", [128, NCR, 64], BF16)
                Yt = T("Yt", [128, NCR, 64], BF16)
                ysq = T("ysq", [128, STR], F32)
                st1 = T("st1", [128, NCR, 4], F32)
                yn = T("yn", [128, NCR, 64], BF16)
                yield
                for st in range(sidx, SEG // STR, 2):
                    ts = slice(st * STR, (st + 1) * STR)
                    cur = slice(1 + st * STR, 1 + (st + 1) * STR)
                    prv = slice(st * STR, (st + 1) * STR)
                    st5 = (st * STR) // ST
                    hs = hslots()
                    for j, dst in enumerate((r_t, k_t, v_t)):
                        pp = ps()
                        for kc in range(8):
                            op("pe", lambda e, j=j, pp=pp, kc=kc: e.matmul(pp[:, 0:STR], WAB[:, kc, 0, j * 128:(j + 1) * 128], hT[:, kc, cur],
                                                                          start=(kc == 0), stop=False), r=[WAB] + hs, w=[pp])
                            op("pe", lambda e, j=j, pp=pp, kc=kc: e.matmul(pp[:, 0:STR], WAB[:, kc, 1, j * 128:(j + 1) * 128], hT[:, kc, prv],
                                                                          start=False, stop=(kc == 7)), r=[WAB] + hs, w=[pp])
                        op("act", lambda e, pp=pp, dst=dst: e.activation(out=dst[:], in_=pp[:, 0:STR], func=AF.Copy), r=[pp], w=[dst])
                        yield
                    pw, pa, pg = ps(), ps(), ps()
                    op("pe", lambda e: e.matmul(pw[:, 0:STR], up_wa[0:64, l, cs], mids[0][0:64, ts], start=True, stop=True), r=[up_wa, mids[0].sub(st5)], w=[pw])
                    op("pe", lambda e: e.matmul(pa[:, 0:STR], up_wa[64:128, l, cs], mids[0][64:128, ts], start=True, stop=True), r=[up_wa, mids[0].sub(st5)], w=[pa])
                    op("pe", lambda e: e.matmul(pg[:, 0:STR], up_g[:, l, cs], mids[1][:, ts], start=True, stop=True), r=[up_g, mids[1].sub(st5)], w=[pg])
                    op("act", lambda e: e.activation(out=sig[:], in_=pw[:, 0:STR], func=AF.Sigmoid, bias=col("w0"), scale=1.0), r=[pw, pcol], w=[sig])
                    op("act", lambda e: e.activation(out=a_t[:], in_=pa[:, 0:STR], func=AF.Sigmoid, bias=col("a0"), scale=1.0), r=[pa, pcol], w=[a_t])
                    op("act", lambda e: e.activation(out=g_t[:], in_=pg[:, 0:STR], func=AF.Copy), r=[pg], w=[g_t])
                    if l >= 1:
                        pvg = ps()
                        op("pe", lambda e: e.matmul(pvg[:, 0:STR], up_c[0:32, l, cs], mids[2][0:32, ts], start=True, stop=True), r=[up_c, mids[2].sub(st5)], w=[pvg])
                        op("act", lambda e: e.activation(out=tA[:], in_=pvg[:, 0:STR], func=AF.Sigmoid, bias=col("v0"), scale=1.0), r=[pvg, pcol], w=[tA])
                        yield
                        op(POOLENG, lambda e: e.tensor_tensor(out=tB[:], in0=vfirst[:, p, ts], in1=v_t[:], op=ALU.subtract), r=[vfirst.sub((p, st)), v_t], w=[tB])
                        op(POOLENG, lambda e: e.tensor_tensor(out=tB[:], in0=tB[:], in1=tA[:], op=ALU.mult), r=[tB, tA], w=[tB])
                        op(POOLENG, lambda e: e.tensor_tensor(out=v_t[:], in0=v_t[:], in1=tB[:], op=ALU.add), r=[v_t, tB], w=[v_t])
                    else:
                        yield
                        op("act", lambda e: e.activation(out=vfirst[:, p, ts], in_=v_t[:], func=AF.Copy), r=[v_t], w=[vfirst.sub((p, st))])
                    op("act", lambda e: e.activation(out=vbf[:], in_=v_t[:], func=AF.Copy), r=[v_t], w=[vbf])
                    yield
                    op("dve", lambda e: e.tensor_tensor_scan(out=cum[:], data0=scanmask[:, 0:STR], data1=sig[:], initial=0.0, op0=ALU.mult, op1=ALU.add),
                       r=[scanmask, sig], w=[cum])
                    op(POOLENG, lambda e: e.tensor_tensor(out=tA[:], in0=cum[:], in1=sig[:], op=ALU.subtract), r=[cum, sig], w=[tA])
                    op(POOLENG, lambda e: e.tensor_tensor(out=v3(tB[:]), in0=v3(cum[:])[:, :, 63:64].to_broadcast([128, NCR, 64]), in1=v3(cum[:]),
                                                        op=ALU.subtract), r=[cum], w=[tB])
                    yield
                    op("act", lambda e: e.activation(out=eG[:], in_=cum[:], func=AF.Exp, scale=-CDEC), r=[cum], w=[eG])
                    op("act", lambda e: e.activation(out=eI[:], in_=cum[:], func=AF.Exp, scale=CDEC), r=[cum], w=[eI])
                    op("act", lambda e: e.activation(out=eX[:], in_=tA[:], func=AF.Exp, scale=-CDEC), r=[tA], w=[eX])
                    op("act", lambda e: e.activation(out=eE[:], in_=tB[:], func=AF.Exp, scale=-CDEC), r=[tB], w=[eE])
                    op("act", lambda e: e.activation(out=kk_t[:], in_=k_t[:], func=AF.Copy, scale=col("kk")), r=[k_t, pcol], w=[kk_t])
                    op("act", lambda e: e.activation(out=tbf[:], in_=kk_t[:], func=AF.Square), r=[kk_t], w=[tbf])
                    pss = ps()
                    op("pe", lambda e: e.matmul(pss[:, 0:STR], ones_bd[:], tbf[:], start=True, stop=True), r=[ones_bd, tbf], w=[pss])
                    op("dve", lambda e: e.tensor_scalar(out=tA[:], in0=pss[:, 0:STR], scalar1=1e-24, scalar2=None, op0=ALU.max), r=[pss], w=[tA])
                    yield
                    op("act", lambda e: e.activation(out=tA[:], in_=tA[:], func=AF.Ln), r=[tA], w=[tA])
                    op("act", lambda e: e.activation(out=tA[:], in_=tA[:], func=AF.Exp, scale=-0.5), r=[tA], w=[tA])
                    op("act", lambda e: e.activation(out=tB[:], in_=a_t[:], func=AF.Identity, scale=col("ka"), bias=drv[:, l, 16 + p:17 + p]),
                       r=[a_t, pcol, drv], w=[tB])
                    op(POOLENG, lambda e: e.tensor_tensor(out=kt_t[:], in0=k_t[:], in1=tB[:], op=ALU.mult), r=[k_t, tB], w=[kt_t])
                    op("dve", lambda e: e.scalar_tensor_tensor(out=tbf[:], in0=r_t[:], scalar=col("rk"), in1=kt_t[:], op0=ALU.mult, op1=ALU.mult),
                       r=[r_t, kt_t, pcol], w=[tbf])
                    pbn = ps()
                    op("pe", lambda e: e.matmul(pbn[:, 0:STR], ones_bd[:], tbf[:], start=True, stop=True), r=[ones_bd, tbf], w=[pbn])
                    op("dve", lambda e: e.tensor_tensor(out=bon[:], in0=pbn[:, 0:STR], in1=v_t[:], op=ALU.mult), r=[pbn, v_t], w=[bon])
                    yield
                    op("dve", lambda e: e.tensor_tensor(out=kk_t[:], in0=kk_t[:], in1=tA[:], op=ALU.mult), r=[kk_t, tA], w=[kk_t])
                    op(POOLENG, lambda e: e.tensor_tensor(out=tA[:], in0=kk_t[:], in1=a_t[:], op=ALU.mult), r=[kk_t, a_t], w=[tA])
                    op("dve", lambda e: e.tensor_tensor(out=RK[:, :, 0, :], in0=v3(kk_t[:]), in1=v3(eX[:]), op=ALU.mult), r=[kk_t, eX], w=[RK])
                    op("dve", lambda e: e.tensor_tensor(out=RK[:, :, 1, :], in0=v3(r_t[:]), in1=v3(eG[:]), op=ALU.mult), r=[r_t, eG], w=[RK])
                    yield
                    op("dve", lambda e: e.scalar_tensor_tensor(out=LK[:, :, 0, :], in0=v3(tA[:]), scalar=-1.0, in1=v3(eI[:]), op0=ALU.mult, op1=ALU.mult),
                       r=[tA, eI], w=[LK])
                    op("dve", lambda e: e.tensor_tensor(out=LK[:, :, 1, :], in0=v3(kt_t[:]), in1=v3(eI[:]), op=ALU.mult), r=[kt_t, eI], w=[LK])
                    op("dve", lambda e: e.scalar_tensor_tensor(out=EF[:, 0, :], in0=tA[:], scalar=-1.0, in1=eE[:], op0=ALU.mult, op1=ALU.mult),
                       r=[tA, eE], w=[EF])
                    op("dve", lambda e: e.tensor_tensor(out=EF[:, 1, :], in0=kt_t[:], in1=eE[:], op=ALU.mult), r=[kt_t, eE], w=[EF])
                    yield
                    pt1, pt2 = ps(), ps()
                    pt1b = pt1[:].bitcast(BF16)
                    pt2b = pt2[:].bitcast(BF16)
                    for c in range(NCR):
                        for h in range(2):
                            hp = HP[h]
                            cc = slice(c * 64, (c + 1) * 64)
                            op("pe", lambda e, c=c, hp=hp, cc=cc: e.transpose(pt1b[hp, c * 128:c * 128 + 64], EF[hp, 0, cc], identb[hp, hp]),
                               r=[EF, identb], w=[pt1])
                            op("pe", lambda e, c=c, hp=hp, cc=cc: e.transpose(pt1b[hp, c * 128 + 64:c * 128 + 128], EF[hp, 1, cc], identb[hp, hp]),
                               r=[EF, identb], w=[pt1])
                            op("pe", lambda e, c=c, hp=hp, cc=cc: e.transpose(pt2b[hp, c * 64:c * 64 + 64], vbf[hp, cc], identb[hp, hp]),
                               r=[vbf, identb], w=[pt2])
                            op("pe", lambda e, c=c, hp=hp: e.transpose(pt2b[hp, 512 + c * 64:512 + c * 64 + 64], RK[hp, c, 0, :], identb[hp, hp]),
                               r=[RK, identb], w=[pt2])
                    op("act", lambda e: e.activation(out=Et[:].rearrange("q c j s -> q (c j s)"), in_=pt1b[:, 0:NCR * 128], func=AF.Copy), r=[pt1], w=[Et])
                    op("dve", lambda e: e.tensor_copy(out=Vt[:].rearrange("q c s -> q (c s)"), in_=pt2b[:, 0:NCR * 64]), r=[pt2], w=[Vt])
                    op("dve", lambda e: e.tensor_copy(out=X[0][:, :, 0:64], in_=pt2b[:, 512:512 + NCR * 64].rearrange("q (c s) -> q c s", c=NCR)), r=[pt2], w=[X[0].sub("k")])
                    yield
                    pap, pbq, pal = ps(), ps(), ps()
                    for c in range(NCR):
                        for h in range(2):
                            hp = HP[h]
                            op("pe", lambda e, c=c, hp=hp: e.matmul(
                                pap[hp, c * 128:(c + 1) * 128], LK[hp, c, 0, :], RK[hp, c, :, :].rearrange("q j s -> q (j s)"),
                                start=True, stop=True), r=[LK, RK], w=[pap])
                            op("pe", lambda e, c=c, hp=hp: e.matmul(
                                pbq[hp, c * 128:(c + 1) * 128], LK[hp, c, 1, :], RK[hp, c, :, :].rearrange("q j s -> q (j s)"),
                                start=True, stop=True), r=[LK, RK], w=[pbq])
                            op("pe", lambda e, c=c, hp=hp: e.matmul(pal[hp, c * 64:(c + 1) * 64], RK[hp, c, 0, :], LK[hp, c, 0, :],
                                                                    start=True, stop=True), r=[LK, RK], w=[pal])
                    mapb = m_ap[:].unsqueeze(1).to_broadcast([128, NCR, 128])
                    op("dve", lambda e: e.tensor_tensor(out=APs[:], in0=pap[:, 0:NCR * 128].rearrange("q (c s) -> q c s", c=NCR), in1=mapb, op=ALU.mult), r=[pap, m_ap], w=[APs])
                    op("dve", lambda e: e.tensor_tensor(out=BQs[:], in0=pbq[:, 0:NCR * 128].rearrange("q (c s) -> q c s", c=NCR), in1=mapb, op=ALU.mult), r=[pbq, m_ap], w=[BQs])
                    op("dve", lambda e: e.tensor_tensor(out=Apl[0][:], in0=v3(pal[:, 0:NCR * 64]), in1=m_low[:].unsqueeze(1).to_broadcast([128, NCR, 64]), op=ALU.mult),
                       r=[pal, m_low], w=[Apl[0]])
                    yield
                    pbv = ps()
                    for c in range(NCR):
                        for h in range(2):
                            hp = HP[h]
                            op("pe", lambda e, c=c, hp=hp: e.matmul(pbv[hp, c * 64:(c + 1) * 64], BQs[hp, c, 0:64], Vt[hp, c, :], start=True, stop=True),
                               r=[BQs, Vt], w=[pbv])
                    op("act", lambda e: e.activation(out=X[0][:, :, 64:128], in_=v3(pbv[:, 0:NCR * 64]), func=AF.Copy), r=[pbv], w=[X[0].sub("v")])
                    yield
                    xc = 0
                    ac = 0

                    def atr(lev, ac_, hp, c):
                        return APs[hp, c, 0:64] if lev == 0 else ATr[ac_][hp, c, :]
                    for lev in range(6):
                        px = ps()
                        xi, xo = X[xc], X[1 - xc]
                        asl = [APs] if lev == 0 else [ATr[ac]]
                        for c in range(NCR):
                            for h in range(2):
                                hp = HP[h]
                                op("pe", lambda e, c=c, hp=hp, xi=xi, ac=ac, lev=lev, px=px: e.matmul(
                                    px[hp, c * 128:(c + 1) * 128], atr(lev, ac, hp, c), xi[hp, c, :], start=True, stop=True),
                                   r=asl + [xi.sub("k"), xi.sub("v")], w=[px])
                        op("dve", lambda e, xi=xi, xo=xo, px=px: e.tensor_tensor(
                            out=xo[:], in0=px[:, 0:NCR * 128].rearrange("q (c s) -> q c s", c=NCR), in1=xi[:], op=ALU.add),
                           r=[px, xi.sub("k"), xi.sub("v")], w=[xo.sub("k"), xo.sub("v")])
                        xc = 1 - xc
                        if lev < 5:
                            pq2 = ps()
                            for c in range(NCR):
                                for h in range(2):
                                    hp = HP[h]
                                    if lev < 4:
                                        op("pe", lambda e, c=c, hp=hp, ac=ac, lev=lev, pq2=pq2: e.matmul(pq2[hp, c * 64:(c + 1) * 64], atr(lev, ac, hp, c), Apl[ac][hp, c, :],
                                                                                                    start=True, stop=True), r=asl + [Apl[ac]], w=[pq2])
                                    op("pe", lambda e, c=c, hp=hp, ac=ac, lev=lev, pq2=pq2: e.matmul(pq2[hp, 256 + c * 64:256 + (c + 1) * 64], Apl[ac][hp, c, :], atr(lev, ac, hp, c),
                                                                                                start=True, stop=True), r=asl + [Apl[ac]], w=[pq2])
                            nac = 1 - ac
                            if lev < 4:
                                op("act", lambda e, nac=nac, pq2=pq2: e.activation(out=Apl[nac][:], in_=v3(pq2[:, 0:NCR * 64]), func=AF.Copy), r=[pq2], w=[Apl[nac]])
                            op("act", lambda e, nac=nac, pq2=pq2: e.activation(out=ATr[nac][:], in_=v3(pq2[:, 256:256 + NCR * 64]), func=AF.Copy), r=[pq2], w=[ATr[nac]])
                            ac = nac
                        yield
                    XF = X[xc]
                    xfs = [XF.sub("k"), XF.sub("v")]
                    pgy = ps()
                    for c in range(NCR):
                        for h in range(2):
                            hp = HP[h]
                            op("pe", lambda e, c=c, hp=hp: e.matmul(pgy[hp, c * 64:(c + 1) * 64], XF[hp, c, 0:64], Et[hp, c, 0, :], start=True, stop=True),
                               r=xfs + [Et], w=[pgy])
                            op("pe", lambda e, c=c, hp=hp: e.matmul(pgy[hp, 256 + c * 64:256 + (c + 1) * 64], XF[hp, c, 0:64], APs[hp, c, 64:128], start=True, stop=True),
                               r=xfs + [APs], w=[pgy])
                    op("act", lambda e: e.activation(out=Gt[:], in_=v3(pgy[:, 0:NCR * 64]), func=AF.Copy), r=[pgy], w=[Gt])
                    op("dve", lambda e: e.tensor_tensor(out=Yt[:], in0=v3(pgy[:, 256:256 + NCR * 64]), in1=RK[:, :, 1, :], op=ALU.add), r=[pgy, RK], w=[Yt])
                    yield
                    py = ps(hold=True)
                    msl = Mf.sub((l, p))
                    mbs = Mb.sub((l, p))
                    for c in range(NCR):
                        pm = ps()
                        for h in range(2):
                            hp = HP[h]
                            yo = py[hp, c * 64:(c + 1) * 64]
                            op("pe", lambda e, c=c, hp=hp, yo=yo: e.matmul(yo, APs[hp, c, 64:128], XF[hp, c, 64:128], start=True, stop=False),
                               r=[APs] + xfs, w=[py])
                            op("pe", lambda e, c=c, hp=hp, yo=yo: e.matmul(yo, BQs[hp, c, 64:128], Vt[hp, c, :], start=False, stop=False),
                               r=[BQs, Vt], w=[py])
                            op("pe", lambda e, c=c, hp=hp, yo=yo: e.matmul(yo, Yt[hp, c, :], Mb[hp, l, p, :], start=False, stop=True),
                               r=[Yt, mbs], w=[py])
                            mo = pm[hp, 0:64]
                            op("pe", lambda e, c=c, hp=hp, mo=mo: e.matmul(mo, Et[hp, c, 0, :], XF[hp, c, 64:128], start=True, stop=False),
                               r=[Et] + xfs, w=[pm])
                            op("pe", lambda e, c=c, hp=hp, mo=mo: e.matmul(mo, Et[hp, c, 1, :], Vt[hp, c, :], start=False, stop=False),
                               r=[Et, Vt], w=[pm])
                            op("pe", lambda e, c=c, hp=hp, mo=mo: e.matmul(mo, Gt[hp, c, :], Mb[hp, l, p, :], start=False, stop=True),
                               r=[Gt, mbs], w=[pm])
                        op("dve", lambda e, c=c, pm=pm: e.scalar_tensor_tensor(out=Mb[:, l, p, :], in0=Mf[:, l, p, :], scalar=eG[:, c * 64 + 63:c * 64 + 64],
                                                                               in1=pm[:, 0:64], op0=ALU.mult, op1=ALU.add), r=[msl, eG, pm], w=[mbs])
                        op("dve", lambda e, c=c, pm=pm: e.scalar_tensor_tensor(out=Mf[:, l, p, :], in0=Mf[:, l, p, :], scalar=eG[:, c * 64 + 63:c * 64 + 64],
                                                                               in1=pm[:, 0:64], op0=ALU.mult, op1=ALU.add), r=[msl, eG, pm], w=[msl])
                    release(py)
                    py3 = v3(py[:, 0:NCR * 64])
                    op("dve", lambda e: e.tensor_reduce(out=st1[:, :, 0], in_=py3, axis=AX.X, op=ALU.add), r=[py], w=[st1])
                    op("act", lambda e: e.activation(out=ysq[:], in_=py[:, 0:STR], func=AF.Square), r=[py], w=[ysq])
                    op("dve", lambda e: e.tensor_reduce(out=st1[:, :, 1], in_=v3(ysq[:]), axis=AX.X, op=ALU.add), r=[ysq], w=[st1])
                    op("dve", lambda e: e.tensor_scalar(out=st1[:, :, 0], in0=st1[:, :, 0], scalar1=1.0 / 64, scalar2=None, op0=ALU.mult), r=[st1], w=[st1])
                    op("dve", lambda e: e.tensor_tensor(out=st1[:, :, 2], in0=st1[:, :, 0], in1=st1[:, :, 0], op=ALU.mult), r=[st1], w=[st1])
                    op("dve", lambda e: e.scalar_tensor_tensor(out=st1[:, :, 1], in0=st1[:, :, 1], scalar=1.0 / 64, in1=st1[:, :, 2], op0=ALU.mult, op1=ALU.subtract),
                       r=[st1], w=[st1])
                    op("act", lambda e: e.activation(out=st1[:, :, 1], in_=st1[:, :, 1], func=AF.Ln, bias=epsc[:, 1:2], scale=1.0), r=[st1, epsc], w=[st1])
                    op("act", lambda e: e.activation(out=st1[:, :, 1], in_=st1[:, :, 1], func=AF.Exp, scale=-0.5), r=[st1], w=[st1])
                    op("dve", lambda e: e.tensor_tensor(out=v3(ysq[:]), in0=py3, in1=st1[:, :, 0:1].to_broadcast([128, NCR, 64]), op=ALU.subtract),
                       r=[py, st1], w=[ysq])
                    op("dve", lambda e: e.tensor_tensor(out=yn[:], in0=v3(ysq[:]), in1=st1[:, :, 1:2].to_broadcast([128, NCR, 64]), op=ALU.mult),
                       r=[ysq, st1], w=[yn])
                    yield
                    pto = ps()
                    ptob = pto[:].bitcast(BF16)
                    for c in range(NCR):
                        for h in range(2):
                            hp = HP[h]
                            op("pe", lambda e, c=c, hp=hp: e.transpose(ptob[hp, c * 64:(c + 1) * 64], yn[hp, c, :], identb[hp, hp]), r=[yn, identb], w=[pto])
                    op("act", lambda e: e.activation(out=tA[:], in_=ptob[:, 0:STR], func=AF.Identity, scale=col("lng"), bias=col("lnb")), r=[pto, pcol], w=[tA])
                    op("dve", lambda e: e.tensor_tensor(out=tA[:], in0=tA[:], in1=bon[:], op=ALU.add), r=[tA, bon], w=[tA])
                    op("dve", lambda e: e.tensor_tensor(out=yT[:, p, ts], in0=tA[:], in1=g_t[:], op=ALU.mult), r=[tA, g_t], w=[yT.sub((p, st))])
                    yield
            return [stream(0), stream(1)]

        def glaret_unit(l, u, is_ret, es_u, tiles):
            def T(name, shape, dt):
                t = P.sb(name, shape, dt, es_u)
                tiles.append(t)
                return t
            yidx = 4 + (2 if is_ret else 0) + u
            base = 2304 if is_ret else 1536
            qc0 = base + 64 * u
            kc0 = base + 128 + 64 * u
            vc0 = base + 256 + 128 * u
            gc0 = base + 512 + 128 * u
            ncol = 768 if is_ret else 512
            Wt = T("Wt", [128, 8, ncol], BF16)
            rr = lambda c0, n: w_in[l, :, c0:c0 + n].rearrange("(kc q) n -> q kc n", q=128)
            op("dve", lambda e: e.memset(Wt[:], 0.0), w=[Wt])
            for hh in range(2):
                dma("pool", Wt[:, :, 64 * hh:64 * hh + 32], rr(qc0 + 32 * hh, 32), w=[Wt])
                dma("pool", Wt[:, :, 128 + 64 * hh:128 + 64 * hh + 32], rr(kc0 + 32 * hh, 32), w=[Wt])
            dma("pool", Wt[:, :, 256:384], rr(vc0, 128), w=[Wt])
            dma("pool", Wt[:, :, 384:512], rr(gc0, 128), w=[Wt])
            if is_ret:
                for j, c0 in enumerate((qc0, kc0)):
                    for hh in range(2):
                        for jj in range(2):
                            dma("pool", Wt[:, :, 512 + 128 * j + 64 * hh + 16 * jj:512 + 128 * j + 64 * hh + 16 * jj + 16],
                                rr(c0 + 32 * hh + 16 * (1 - jj), 16), w=[Wt])
            for _ in range(GIDLE):
                yield
            f = lambda n: T(n, [128, ST], F32)
            tA, tB, g_fm = [f(n) for n in ("gtA", "gtB", "g_fm")]
            if not is_ret:
                e1, e2, e3 = [f(n) for n in ("ge1", "ge2", "ge3")]
            if is_ret:
                rC = T("rC", [128, ST], F32)
                rS = T("rS", [128, ST], F32)
            qa = T("qa", [128, ST], BF16)
            qb = T("qb", [128, ST], BF16)
            ka = T("ka", [128, ST], BF16)
            kb = T("kb", [128, ST], BF16)
            kend = T("kend", [128, ST], BF16)
            vbf = T("gvbf", [128, ST], BF16)
            kendT = T("kendT", [128, 8, 64], BF16)
            Vt = T("gVt", [128, 8, 64], BF16)
            P1 = T("P1", [128, 8, 64], BF16)
            P2 = T("P2", [128, 8, 64], BF16)
            Sball = T("Sball", [128, 8, 64], BF16)
            decc = T("decc", [128, 8], F32)
            ysq = T("gysq", [128, ST], F32)
            st2 = T("st2", [128, 8, 4], F32)
            yn = T("gyn", [128, 8, 64], BF16)
            Sst = (Sr if is_ret else Sg)
            ssl = Sst.sub((l, u))
            HP = (slice(0, 64), slice(64, 128))
            HD = HP

            def v3(ap):
                return ap.rearrange("q (c t) -> q c t", c=8)

            for st in range(SEG // ST):
                ts = slice(st * ST, (st + 1) * ST)
                cur = slice(1 + st * ST, 1 + (st + 1) * ST)
                hs = hslots()
                pq, pk, pv, pg = ps(), ps(), ps(), ps()
                for j, pp in enumerate((pq, pk, pv, pg)):
                    for kc in range(8):
                        op("pe", lambda e, kc=kc, j=j, pp=pp: e.matmul(pp[:], Wt[:, kc, j * 128:(j + 1) * 128], hT[:, kc, cur], start=(kc == 0), stop=(kc == 7)),
                           r=[Wt] + hs, w=[pp])
                op("act", lambda e: e.activation(out=vbf[:], in_=pv[:], func=AF.Copy), r=[pv], w=[vbf])
                op("act", lambda e: e.activation(out=g_fm[:], in_=pg[:], func=AF.Silu), r=[pg], w=[g_fm])
                chk("g_proj")
                if not is_ret:
                    pz = ps()
                    op("pe", lambda e: e.matmul(pz[:], up_gl[32:48, l, u, :], mids[2][32:48, ts], start=True, stop=True),
                       r=[up_gl, mids[2].sub(st)], w=[pz])
                    gb = pcol[:, poff[("glab", l)] + u:poff[("glab", l)] + u + 1]
                    op("act", lambda e: e.activation(out=tA[:], in_=pz[:], func=AF.Sigmoid, bias=gb, scale=1.0), r=[pz, pcol], w=[tA])
                    op("act", lambda e: e.activation(out=tA[:], in_=tA[:], func=AF.Ln), r=[tA], w=[tA])
                    op("dve", lambda e: e.tensor_tensor_scan(out=tB[:], data0=scanmask[:], data1=tA[:], initial=0.0, op0=ALU.mult, op1=ALU.add),
                       r=[scanmask, tA], w=[tB])
                    op("act", lambda e: e.activation(out=e1[:], in_=tB[:], func=AF.Exp, scale=1.0 / 16), r=[tB], w=[e1])
                    op("act", lambda e: e.activation(out=e2[:], in_=tB[:], func=AF.Exp, scale=-1.0 / 16), r=[tB], w=[e2])
                    op("dve", lambda e: e.tensor_tensor(out=v3(tA[:]), in0=v3(tB[:])[:, :, 63:64].to_broadcast([128, 8, 64]), in1=v3(tB[:]), op=ALU.subtract),
                       r=[tB], w=[tA])
                    op("act", lambda e: e.activation(out=e3[:], in_=tA[:], func=AF.Exp, scale=1.0 / 16), r=[tA], w=[e3])
                    op("dve", lambda e: e.tensor_copy(out=decc[:], in_=v3(e1[:])[:, :, 63]), r=[e1], w=[decc])
                    sc = 32 ** -0.5
                    op("dve", lambda e: e.scalar_tensor_tensor(out=qa[:], in0=pq[:], scalar=sc, in1=e1[:], op0=ALU.mult, op1=ALU.mult), r=[pq, e1], w=[qa])
                    op("dve", lambda e: e.scalar_tensor_tensor(out=qb[:], in0=pq[:], scalar=sc, in1=e2[:], op0=ALU.mult, op1=ALU.mult), r=[pq, e2], w=[qb])
                    op("dve", lambda e: e.tensor_tensor(out=ka[:], in0=pk[:], in1=e2[:], op=ALU.mult), r=[pk, e2], w=[ka])
                    op("dve", lambda e: e.tensor_tensor(out=kb[:], in0=pk[:], in1=e1[:], op=ALU.mult), r=[pk, e1], w=[kb])
                    op("dve", lambda e: e.tensor_tensor(out=kend[:], in0=pk[:], in1=e3[:], op=ALU.mult), r=[pk, e3], w=[kend])
                else:
                    pqs, pks = ps(), ps()
                    for j, pp in enumerate((pqs, pks)):
                        for kc in range(8):
                            op("pe", lambda e, kc=kc, j=j, pp=pp: e.matmul(pp[:], Wt[:, kc, 512 + j * 128:512 + (j + 1) * 128], hT[:, kc, cur], start=(kc == 0), stop=(kc == 7)),
                               r=[Wt] + hs, w=[pp])
                    g0 = seg_tok0[0] + st * ST
                    dma("sp", rC[:], cd["ropeC"][:, g0:g0 + ST], w=[rC])
                    dma("sp", rS[:], cd["ropeS"][:, g0:g0 + ST], w=[rS])
                    for (px, pxs, dst) in ((pq, pqs, qa), (pk, pks, ka)):
                        op("dve", lambda e, px=px: e.tensor_tensor(out=tA[:], in0=px[:], in1=rC[:], op=ALU.mult), r=[px, rC], w=[tA])
                        op("dve", lambda e, pxs=pxs: e.tensor_tensor(out=tB[:], in0=pxs[:], in1=rS[:], op=ALU.mult), r=[pxs, rS], w=[tB])
                        op("dve", lambda e, dst=dst: e.tensor_tensor(out=dst[:], in0=tA[:], in1=tB[:], op=ALU.add), r=[tA, tB], w=[dst])
                    op("dve", lambda e: e.tensor_tensor(out=v3(qb[:]), in0=v3(qa[:]), in1=qdec[:, 64 * u:64 * u + 64].unsqueeze(1).to_broadcast([128, 8, 64]), op=ALU.mult),
                       r=[qa, qdec], w=[qb])
                    op("dve", lambda e: e.tensor_tensor(out=v3(kend[:]), in0=v3(ka[:]), in1=kdec[:, 64 * u:64 * u + 64].unsqueeze(1).to_broadcast([128, 8, 64]), op=ALU.mult),
                       r=[ka, kdec], w=[kend])
                chk("g_dec")
                yield
                ptv = ps()
                ptvb = ptv[:].bitcast(BF16)
                for c in range(8):
                    cc = slice(c * 64, (c + 1) * 64)
                    for hh in range(2):
                        hp, hd = HP[hh], HD[hh]
                        op("pe", lambda e, c=c, hp=hp, cc=cc: e.transpose(ptvb[hp, c * 64:(c + 1) * 64], vbf[hp, cc], identb[hp, hp]), r=[vbf, identb], w=[ptv])
                        op("pe", lambda e, c=c, hp=hp, cc=cc: e.transpose(ptvb[hp, 512 + c * 64:512 + (c + 1) * 64], kend[hp, cc], identb[hp, hp]), r=[kend, identb], w=[ptv])
                op("act", lambda e: e.activation(out=Vt[:].rearrange("q a b -> q (a b)"), in_=ptvb[:, 0:512], func=AF.Copy), r=[ptv], w=[Vt])
                op("dve", lambda e: e.tensor_copy(out=kendT[:].rearrange("q a b -> q (a b)"), in_=ptvb[:, 512:1024]), r=[ptv], w=[kendT])
                chk("g_tr")
                yield
                p1 = ps()
                p2 = ps() if not is_ret else None
                for c in range(8):
                    cc = slice(c * 64, (c + 1) * 64)
                    for hh in range(2):
                        hp, hd = HP[hh], HD[hh]
                        op("pe", lambda e, hp=hp, hd=hd, cc=cc: e.matmul(p1[hp, cc], ka[hd, cc], qa[hd, cc], start=True, stop=True), r=[ka, qa], w=[p1])
                        if not is_ret:
                            op("pe", lambda e, hp=hp, hd=hd, cc=cc: e.matmul(p2[hp, cc], kb[hd, cc], qb[hd, cc], start=True, stop=True), r=[kb, qb], w=[p2])
                if not is_ret:
                    op("dve", lambda e: e.tensor_tensor(out=P1[:], in0=v3(p1[:]), in1=m_ap[:, 64:128].unsqueeze(1).to_broadcast([128, 8, 64]), op=ALU.mult), r=[p1, m_ap], w=[P1])
                    op("dve", lambda e: e.tensor_tensor(out=P2[:], in0=v3(p2[:]), in1=m_low[:].unsqueeze(1).to_broadcast([128, 8, 64]), op=ALU.mult), r=[p2, m_low], w=[P2])
                else:
                    op("dve", lambda e: e.tensor_tensor(out=P1[:], in0=v3(p1[:]), in1=retD[:, 64 * u:64 * u + 64].unsqueeze(1).to_broadcast([128, 8, 64]), op=ALU.mult),
                       r=[p1, retD], w=[P1])
                chk("g_sc")
                yield
                pkv = ps()
                for c in range(8):
                    for hh in range(2):
                        hp, hd = HP[hh], HD[hh]
                        op("pe", lambda e, c=c, hp=hp, hd=hd: e.matmul(pkv[hd, c * 64:(c + 1) * 64], kendT[hp, c, :], Vt[hp, c, :], start=True, stop=True), r=[kendT, Vt], w=[pkv])
                for c in range(8):
                    for hh in range(2):
                        hd = HD[hh]
                        op("act", lambda e, c=c, hd=hd: e.activation(out=Sball[hd, c, :], in_=Sst[hd, l, u, :], func=AF.Copy), r=[ssl], w=[Sball])
                        dsc = cdec[hd, u:u + 1] if is_ret else decc[hd, c:c + 1]
                        op("dve", lambda e, c=c, dsc=dsc, hd=hd: e.scalar_tensor_tensor(out=Sst[hd, l, u, :], in0=Sst[hd, l, u, :], scalar=dsc, in1=pkv[hd, c * 64:(c + 1) * 64],
                                                                                       op0=ALU.mult, op1=ALU.add), r=[ssl, pkv, decc, cdec], w=[ssl])
                chk("g_kv")
                yield
                po = ps()
                qi = qa if not is_ret else qb
                for c in range(8):
                    cc = slice(c * 64, (c + 1) * 64)
                    for hh in range(2):
                        hp, hd = HP[hh], HD[hh]
                        oo = po[hp, cc]
                        op("pe", lambda e, oo=oo, hp=hp, c=c: e.matmul(oo, P1[hp, c, :], Vt[hp, c, :], start=True, stop=False), r=[P1, Vt], w=[po])
                        if not is_ret:
                            op("pe", lambda e, oo=oo, hp=hp, c=c: e.matmul(oo, P2[hp, c, :], Vt[hp, c, :], start=False, stop=False), r=[P2, Vt], w=[po])
                        op("pe", lambda e, oo=oo, hd=hd, cc=cc, c=c: e.matmul(oo, qi[hd, cc], Sball[hd, c, :], start=False, stop=True), r=[qi, Sball], w=[po])
                chk("g_o")
                po3 = v3(po[:])
                if is_ret:
                    op("dve", lambda e: e.tensor_reduce(out=st2[:, :, 0], in_=po3, axis=AX.X, op=ALU.add), r=[po], w=[st2])
                    op("dve", lambda e: e.tensor_scalar(out=st2[:, :, 0], in0=st2[:, :, 0], scalar1=1.0 / 64, scalar2=None, op0=ALU.mult), r=[st2], w=[st2])
                    op("dve", lambda e: e.tensor_tensor(out=v3(tA[:]), in0=po3, in1=st2[:, :, 0:1].to_broadcast([128, 8, 64]), op=ALU.subtract), r=[po, st2], w=[tA])
                else:
                    op("act", lambda e: e.activation(out=tA[:], in_=po[:], func=AF.Copy), r=[po], w=[tA])
                op("act", lambda e: e.activation(out=ysq[:], in_=tA[:], func=AF.Square), r=[tA], w=[ysq])
                op("dve", lambda e: e.tensor_reduce(out=st2[:, :, 1], in_=v3(ysq[:]), axis=AX.X, op=ALU.add), r=[ysq], w=[st2])
                op("act", lambda e: e.activation(out=st2[:, :, 1], in_=st2[:, :, 1], func=AF.Ln, bias=epsc[:, 0:1], scale=1.0 / 64), r=[st2, epsc], w=[st2])
                op("act", lambda e: e.activation(out=st2[:, :, 1], in_=st2[:, :, 1], func=AF.Exp, scale=-0.5), r=[st2], w=[st2])
                op("dve", lambda e: e.tensor_tensor(out=yn[:], in0=v3(tA[:]), in1=st2[:, :, 1:2].to_broadcast([128, 8, 64]), op=ALU.mult), r=[tA, st2], w=[yn])
                chk("g_norm")
                yield
                pto = ps()
                ptob = pto[:].bitcast(BF16)
                for c in range(8):
                    for hh in range(2):
                        hp = HP[hh]
                        op("pe", lambda e, c=c, hp=hp: e.transpose(ptob[hp, c * 64:(c + 1) * 64], yn[hp, c, :], identb[hp, hp]), r=[yn, identb], w=[pto])
                if not is_ret:
                    gl = pcol[:, poff[("glng", l)]:poff[("glng", l)] + 1]
                    op("dve", lambda e: e.scalar_tensor_tensor(out=yT[:, yidx, ts], in0=ptob[:, 0:512], scalar=gl, in1=g_fm[:], op0=ALU.mult, op1=ALU.mult),
                       r=[pto, pcol, g_fm], w=[yT.sub((yidx, st))])
                else:
                    op("dve", lambda e: e.tensor_tensor(out=yT[:, yidx, ts], in0=ptob[:, 0:512], in1=g_fm[:], op=ALU.mult), r=[pto, g_fm], w=[yT.sub((yidx, st))])
                yield

        seg_tok0 = [0]
        out_slots = []

        def chk(tag):
            if stop == tag:
                P.stopped = True
        try:
          chk("setup")
          for seg in range(nseg):
              t0 = seg * SEG
              seg_tok0[0] = t0
              with ExitStack() as s1:
                  xst = [P.sb("xst%d" % i, [128, D], F32, s1) for i in range(2)]
                  for tt in range(SEG // 128):
                      xs = xst[tt % 2]
                      dma("sp", xs[:], x_d[t0 + tt * 128:t0 + (tt + 1) * 128, :], w=[xs])
                      for half in range(2):
                          pp = ps()
                          for j in range(4):
                              c = half * 4 + j
                              op("pe", lambda e, c=c, j=j, xs=xs, pp=pp: e.transpose(pp[:, j * 128:(j + 1) * 128], xs[:, c * 128:(c + 1) * 128], identf[:]),
                                 r=[xs, identf], w=[pp])
                          eng = "act" if half == 0 else "dve"
                          if eng == "act":
                              op("act", lambda e, half=half, tt=tt, pp=pp: e.activation(out=xT[:, half * 4:(half + 1) * 4, tt * 128:(tt + 1) * 128],
                                                                                        in_=pp[:].rearrange("q (c t) -> q c t", c=4), func=AF.Copy), r=[pp], w=[xT])
                          else:
                              op("dve", lambda e, half=half, tt=tt, pp=pp: e.tensor_copy(out=xT[:, half * 4:(half + 1) * 4, tt * 128:(tt + 1) * 128],
                                                                                         in_=pp[:].rearrange("q (c t) -> q c t", c=4)), r=[pp], w=[xT])
                  P.add_fence(xst)
              chk("xload")

              for l in range(nlayer):
                  with ExitStack() as s2:
                      if seg == 0:
                          op("dve", lambda e: e.memset(hT[:, :, 0:1], 0.0), w=[hT.sub("c0")])
                      else:
                          op("dve", lambda e: e.tensor_copy(out=hT[:, :, 0:1], in_=hcar[:, l, :].unsqueeze(2)), r=[hcar.sub(l)], w=[hT.sub("c0")])
                      tl = rmsnorm_to_hT(drv[:, l, 0:8], mod[:, l, 0:8], s2)
                      op("dve", lambda e: e.tensor_copy(out=hcar[:, l, :].unsqueeze(2), in_=hT[:, :, SEG:SEG + 1]), r=hslots(), w=[hcar.sub(l)])
                      P.add_fence(tl)
                  chk("norm1")
                  s2b = ExitStack()
                  lw = P.sb("lw", [128, 2, 8, 304], BF16, s2b)
                  dma("sp", lw[:].rearrange("q a k n -> q (a k n)"), lw_d[l], r=[lw_slots[l]], w=[lw])
                  for st in range(SEG // ST):
                      cur = slice(1 + st * ST, 1 + (st + 1) * ST)
                      prv = slice(st * ST, (st + 1) * ST)
                      ts = slice(st * ST, (st + 1) * ST)
                      hs = hslots()
                      for ci, (c0, c1) in enumerate(((0, 128), (128, 256), (256, 304))):
                          m = c1 - c0
                          pp = ps()
                          for kc in range(8):
                              op("pe", lambda e, kc=kc, pp=pp, c0=c0, c1=c1, m=m: e.matmul(pp[0:m, :], lw[:, 0, kc, c0:c1], hT[:, kc, cur], start=(kc == 0), stop=False),
                                 r=[lw] + hs, w=[pp])
                              op("pe", lambda e, kc=kc, pp=pp, c0=c0, c1=c1, m=m: e.matmul(pp[0:m, :], lw[:, 1, kc, c0:c1], hT[:, kc, prv], start=False, stop=(kc == 7)),
                                 r=[lw] + hs, w=[pp])
                          if ci == 0:
                              op("act", lambda e, pp=pp: e.activation(out=mids[0][0:64, ts], in_=pp[0:64, :], func=AF.Tanh), r=[pp], w=[mids[0].sub(st)])
                              op("act", lambda e, pp=pp: e.activation(out=mids[0][64:128, ts], in_=pp[64:128, :], func=AF.Copy), r=[pp], w=[mids[0].sub(st)])
                          elif ci == 1:
                              op("act", lambda e, pp=pp: e.activation(out=mids[1][:, ts], in_=pp[:], func=AF.Sigmoid), r=[pp], w=[mids[1].sub(st)])
                          else:
                              op("act", lambda e, pp=pp: e.activation(out=mids[2][0:48, ts], in_=pp[0:48, :], func=AF.Copy), r=[pp], w=[mids[2].sub(st)])
                  P.add_fence([lw])
                  s2b.close()
                  chk("lora")
                  if MIXMODE == 2:
                      for p in (0, 2):
                          with ExitStack() as s3:
                              tl_ = []
                              run_streams(rwkv_pair(l, p, s3, tl_) + rwkv_pair(l, p + 1, s3, tl_))
                              P.add_fence(tl_)
                      for is_ret in (False, True):
                          with ExitStack() as s3:
                              tl_ = []
                              run_streams([glaret_unit(l, u, is_ret, s3, tl_) for u in range(2)])
                              P.add_fence(tl_)
                  elif MIXMODE == 1:
                      for p in range(4):
                          with ExitStack() as s3:
                              tl_ = []
                              gens = rwkv_pair(l, p, s3, tl_) + [glaret_unit(l, p % 2, p >= 2, s3, tl_)]
                              run_streams(gens, periods=[1, 1, GPER])
                              P.add_fence(tl_)
                          chk("rwkv%d" % p)
                  else:
                      for p in range(4):
                          with ExitStack() as s3:
                              tl_ = []
                              run_streams(rwkv_pair(l, p, s3, tl_))
                              P.add_fence(tl_)
                          chk("rwkv%d" % p)
                      for is_ret in (False, True):
                          with ExitStack() as s3:
                              tl_ = []
                              run_streams([glaret_unit(l, u, is_ret, s3, tl_) for u in range(2)])
                              P.add_fence(tl_)
                  with ExitStack() as s6:
                      ngrp = (NFF + FFG - 1) // FFG
                      wgu = [P.sb("wgu%d" % i, [128, 8, 2, FFG * 128], BF16, s6) for i in range(2)]
                      wdn = [P.sb("wdn%d" % i, [128, FFG, D], BF16, s6) for i in range(2)]
                      sg = [P.sb("sg%d" % i, [128, ST], F32, s6) for i in range(2)]
                      aT = [P.sb("aT%d" % i, [128, FFG, ST], BF16, s6) for i in range(2)]
                      ai = 0

                      def ffn_load(g):
                          f0 = g * FFG
                          nf = min(FFG, NFF - f0)
                          wg_t, wd_t = wgu[g % 2], wdn[g % 2]
                          dma("pool", wg_t[:, :, 0, 0:nf * 128], wg_d[l, :, f0 * 128:(f0 + nf) * 128].rearrange("(kc q) n -> q kc n", q=128), w=[wg_t])
                          dma("pool", wg_t[:, :, 1, 0:nf * 128], wu_d[l, :, f0 * 128:(f0 + nf) * 128].rearrange("(kc q) n -> q kc n", q=128), w=[wg_t])
                          dma("pool", wd_t[:, 0:nf, :], wd_d[l, f0 * 128:(f0 + nf) * 128, :].rearrange("(f q) n -> q f n", q=128), w=[wd_t])
                      ffn_load(0)
                      ffn_load(1)
                      with ExitStack() as s4:
                          wo = P.sb("wo", [128, 8, D], BF16, s4)
                          dma("sp", wo[:].rearrange("q k n -> q (k n)"), wo_d[l], r=[wo_slots[l]], w=[wo])
                          ysl = yT.all_slots()
                          for st in range(SEG // ST):
                              ts = slice(st * ST, (st + 1) * ST)
                              for c in range(8):
                                  pp = ps()
                                  for kc in range(8):
                                      op("pe", lambda e, kc=kc, c=c, pp=pp: e.matmul(pp[:], wo[:, kc, c * 128:(c + 1) * 128], yT[:, kc, ts], start=(kc == 0), stop=(kc == 7)),
                                         r=[wo] + ysl, w=[pp])
                                  op("dve", lambda e, c=c, pp=pp: e.scalar_tensor_tensor(out=xT[:, c, ts], in0=pp[:], scalar=mod[:, l, 16 + c:17 + c], in1=xT[:, c, ts],
                                                                                         op0=ALU.mult, op1=ALU.add), r=[pp, mod, xT], w=[xT])
                          P.add_fence([wo])
                      chk("wout")
                      with ExitStack() as s5:
                          tl = rmsnorm_to_hT(drv[:, l, 8:16], mod[:, l, 24:32], s5)
                          P.add_fence(tl)
                      hs = hslots()
                      for g in range(ngrp):
                          f0 = g * FFG
                          nf = min(FFG, NFF - f0)
                          wg_t, wd_t = wgu[g % 2], wdn[g % 2]
                          if g >= 2:
                              ffn_load(g)
                          for st in range(SEG // ST):
                              tok = slice(1 + st * ST, 1 + (st + 1) * ST)
                              tx = slice(st * ST, (st + 1) * ST)
                              at = aT[ai % 2]
                              ai += 1
                              for fi in range(nf):
                                  pg_, pu_ = ps(), ps()
                                  for kc in range(8):
                                      op("pe", lambda e, kc=kc, fi=fi, pg_=pg_: e.matmul(pg_[:], wg_t[:, kc, 0, fi * 128:(fi + 1) * 128], hT[:, kc, tok],
                                                                                        start=(kc == 0), stop=(kc == 7)), r=[wg_t] + hs, w=[pg_])
                                  for kc in range(8):
                                      op("pe", lambda e, kc=kc, fi=fi, pu_=pu_: e.matmul(pu_[:], wg_t[:, kc, 1, fi * 128:(fi + 1) * 128], hT[:, kc, tok],
                                                                                        start=(kc == 0), stop=(kc == 7)), r=[wg_t] + hs, w=[pu_])
                                  sgt = sg[fi % 2]
                                  op("act", lambda e, sgt=sgt, pg_=pg_: e.activation(out=sgt[:], in_=pg_[:], func=AF.Silu), r=[pg_], w=[sgt])
                                  op("dve", lambda e, sgt=sgt, pu_=pu_, at=at, fi=fi: e.tensor_tensor(out=at[:, fi, :], in0=pu_[:], in1=sgt[:], op=ALU.mult),
                                     r=[pu_, sgt], w=[at.sub(fi)])
                              for c in range(8):
                                  pd = ps()
                                  for fi in range(nf):
                                      op("pe", lambda e, c=c, fi=fi, at=at, pd=pd: e.matmul(pd[:], wd_t[:, fi, c * 128:(c + 1) * 128], at[:, fi, :],
                                                                                           start=(fi == 0), stop=(fi == nf - 1)), r=[wd_t, at.sub(fi)], w=[pd])
                                  op("dve", lambda e, c=c, pd=pd: e.scalar_tensor_tensor(out=xT[:, c, tx], in0=pd[:], scalar=mod[:, l, 40 + c:41 + c], in1=xT[:, c, tx],
                                                                                         op0=ALU.mult, op1=ALU.add), r=[pd, mod, xT], w=[xT])
                      P.add_fence(wgu + wdn + sg + aT)

              with ExitStack() as s7:
                  sqb = P.sb("fsqb", [128, 8, ST], BF16, s7)
                  rstd = P.sb("frstd", [128, ST], F32, s7)
                  tfall = P.sb("tfall", [128, 8, ST], F32, s7)
                  ost = [P.sb("ost%d" % i, [128, D], F32, s7) for i in range(2)]
                  gmf = pc(("nfg",), 8)
                  oi = 0
                  for st in range(SEG // ST):
                      ts = slice(st * ST, (st + 1) * ST)
                      mps = ps()
                      for c in range(8):
                          op("act", lambda e, c=c: e.activation(out=sqb[:, c, :], in_=xT[:, c, ts], func=AF.Square), r=[xT], w=[sqb.sub(c)])
                          op("pe", lambda e, c=c: e.matmul(mps[:], ones_mean[:], sqb[:, c, :], start=(c == 0), stop=(c == 7)),
                             r=[ones_mean, sqb.sub(c)], w=[mps])
                      op("act", lambda e: e.activation(out=rstd[:], in_=mps[:], func=AF.Ln, bias=epsc[:, 0:1], scale=1.0), r=[mps, epsc], w=[rstd])
                      op("act", lambda e: e.activation(out=rstd[:], in_=rstd[:], func=AF.Exp, scale=-0.5), r=[rstd], w=[rstd])
                      for c in range(8):
                          op("dve", lambda e, c=c: e.scalar_tensor_tensor(out=tfall[:, c, :], in0=xT[:, c, ts], scalar=gmf[:, c:c + 1], in1=rstd[:],
                                                                          op0=ALU.mult, op1=ALU.mult), r=[xT, rstd, pcol], w=[tfall])
                      for tt in range(ST // 128):
                          o_t = ost[oi % 2]
                          oi += 1
                          for half in range(2):
                              pp = ps()
                              for j in range(4):
                                  c = half * 4 + j
                                  op("pe", lambda e, c=c, j=j, pp=pp, tt=tt: e.transpose(pp[:, j * 128:(j + 1) * 128], tfall[:, c, tt * 128:(tt + 1) * 128], identf[:]),
                                     r=[tfall, identf], w=[pp])
                              op("act", lambda e, half=half, o_t=o_t, pp=pp: e.activation(out=o_t[:, half * 512:(half + 1) * 512], in_=pp[:], func=AF.Copy), r=[pp], w=[o_t])
                          r0 = t0 + st * ST + tt * 128
                          dma("sp", out_d[r0:r0 + 128, :], o_t[:], r=[o_t])
                  P.add_fence([sqb, rstd, tfall] + ost)

        except _Stop:
            pass
        P.stopped = False
        if dbg_d is not None:
            src = {"yT": yT, "hT": hT, "xT": xT, "vfirst": vfirst, "mod": mod, "midA": mids[0], "midG": mids[1], "midC": mids[2],
                   "Mf": Mf, "Sg": Sg, "Sr": Sr, "drv": drv}[dbg[0]]
            ap = src[:]
            if len(ap.shape) == 3:
                ap = ap.rearrange("q a b -> q (a b)")
            elif len(ap.shape) == 4:
                ap = ap.rearrange("q a b c -> q (a b c)")
            npart = ap.shape[0]
            dma("pool", dbg_d[0:npart, 0:ap.shape[1]], ap, r=src.all_slots())
        sp = P.E["sp"]
        for so in P.dsems:
            if so.count > 0:
                sp.e.wait_ge(so.h, so.count)
    return nc


_CACHE = {}


def _in_map(inputs, b, consts):
    m = {
        "x": np.ascontiguousarray(inputs["x"][b], dtype=np.float32),
        "pcol": _build_pcol(inputs, b),
    }
    for k in ("ada_w", "w_in", "w_out", "rk_mu_rkv", "rk_w1", "rk_a1", "rk_g1", "rk_v1", "gla_a1", "rk_w2", "rk_a2", "rk_g2",
              "rk_v2", "gla_a2", "gla_ln_g", "ffn_w_gate", "ffn_w_up", "ffn_w_down"):
        m[k] = np.ascontiguousarray(inputs[k], dtype=np.float32)
    for k, v in consts.items():
        m["c_" + k] = v
    return m


def kernel(**inputs):
    inputs = {k: np.asarray(v) for k, v in inputs.items()}
    nb = inputs["x"].shape[0]
    if "nc" not in _CACHE:
        _CACHE["nc"] = build_program()
    nc = _CACHE["nc"]
    consts = _consts()
    in_maps = [_in_map(inputs, b, consts) for b in range(nb)]
    res = run_bass_kernel_spmd(nc, in_maps, core_ids=list(range(nb)))
    out = np.stack([np.asarray(res.results[b]["out"], dtype=np.float32) for b in range(nb)], axis=0)
    return out
```

```python
import os
import numpy as np
from contextlib import ExitStack
import concourse.bass as bass
import concourse.mybir as mybir
from concourse.bass_utils import run_bass_kernel_spmd

F32 = mybir.dt.float32
BF16 = mybir.dt.bfloat16
AF = mybir.ActivationFunctionType
ALU = mybir.AluOpType
AX = mybir.AxisListType

D = 1024
SEQ = 4096
DEPTH = 2
DFF = 2816
NFF = DFF // 128
SEG = 1024
ST = 512
CH = 64
EPS = 1e-6
GN_EPS = 64e-5
CDEC = float(np.exp(-0.5))
FFG = 4
GPER = int(os.environ.get('GPER', '1'))
MIXMODE = int(os.environ.get('MIXMODE', '1'))
GIDLE = int(os.environ.get('GIDLE', '28'))
POOLENG = os.environ.get('POOLENG', 'dve')


class SemObj:
    __slots__ = ("h", "count")

    def __init__(self, h):
        self.h = h
        self.count = 0


class Slot:
    __slots__ = ("w", "r", "excl")

    def __init__(self, fence=None):
        self.w = None
        self.r = dict(fence) if fence else {}
        self.excl = False


class Tile:
    def __init__(self, t, fence=None):
        self.t = t
        self.s = Slot(fence)
        self._subs = {}
        self._fence = fence

    def sub(self, key):
        if key not in self._subs:
            self._subs[key] = Slot(self._fence)
        return self._subs[key]

    def all_slots(self):
        return [self.s] + list(self._subs.values())

    def __getitem__(self, k):
        return self.t[k]


class Eng:
    def __init__(self, e, sem, inorder=False):
        self.e = e
        self.sem = sem
        self.known = {}
        self.inorder = inorder


class Prog:
    def __init__(self, nc, es, n_dma_sems=24):
        self.nc = nc
        self.es = es
        mk = lambda n: SemObj(es.enter_context(nc.semaphore(n)))
        self.E = {
            "pe": Eng(nc.tensor, mk("s_pe"), inorder=True),
            "act": Eng(nc.scalar, mk("s_act")),
            "dve": Eng(nc.vector, mk("s_dve")),
            "pool": Eng(nc.gpsimd, mk("s_pool")),
            "sp": Eng(nc.sync, mk("s_sp")),
        }
        self.dsems = [mk("s_dma%d" % i) for i in range(n_dma_sems)]
        self.drr = 0
        self.fence = {}
        self.ninst = 0

    def sb(self, name, shape, dt, es=None):
        es = es or self.es
        self.nsb = getattr(self, "nsb", 0) + 1
        t = es.enter_context(self.nc.sbuf_tensor("sb%d_%s" % (self.nsb, name), list(shape), dt))
        return Tile(t, self.fence)

    def add_fence(self, tiles):
        f = dict(self.fence)
        for tl in tiles:
            for s in tl.all_slots():
                if s.w is not None:
                    f[s.w[0]] = max(f.get(s.w[0], 0), s.w[1])
                for so, v in s.r.items():
                    f[so] = max(f.get(so, 0), v)
        self.fence = f

    def _collect(self, reads, writes):
        need = {}
        for s in reads:
            if s.w is not None:
                need[s.w[0]] = max(need.get(s.w[0], 0), s.w[1])
            if s.excl:
                for so, v in s.r.items():
                    need[so] = max(need.get(so, 0), v)
        for s in writes:
            if s.w is not None:
                need[s.w[0]] = max(need.get(s.w[0], 0), s.w[1])
            for so, v in s.r.items():
                need[so] = max(need.get(so, 0), v)
        return need

    def _waits(self, E, need):
        for so, v in need.items():
            if v <= 0:
                continue
            if so is E.sem and E.inorder:
                continue
            if E.known.get(so, 0) < v:
                E.e.wait_ge(so.h, v)
                E.known[so] = v

    @staticmethod
    def _slots(xs):
        return [x.s if isinstance(x, Tile) else x for x in xs]

    stopped = False

    def op(self, eng, fn, r=(), w=()):
        if self.stopped:
            return None
        E = self.E[eng]
        r = self._slots(r)
        w = self._slots(w)
        self._waits(E, self._collect(r, w))
        inst = fn(E.e)
        E.sem.count += 1
        inst.then_inc(E.sem.h, 1)
        v = E.sem.count
        for s in r:
            s.r[E.sem] = v
        for s in w:
            s.w = (E.sem, v)
            s.r = {}
        self.ninst += 1
        return inst

    def dma(self, eng, out, in_, r=(), w=()):
        if self.stopped:
            return None
        E = self.E[eng]
        r = self._slots(r)
        w = self._slots(w)
        so = self.dsems[self.drr]
        self.drr = (self.drr + 1) % len(self.dsems)
        need = self._collect(r, w)
        if so.count > 0:
            need[so] = max(need.get(so, 0), so.count)
        self._waits(E, need)
        inst = E.e.dma_start(out=out, in_=in_)
        so.count += 16
        inst.then_inc(so.h, 16)
        for s in r:
            s.r[so] = so.count
        for s in w:
            s.w = (so, so.count)
            s.r = {}
        self.ninst += 1
        return inst


def _consts():
    c = {}
    c["ident"] = np.eye(128, dtype=np.float32)
    bd = np.zeros((128, 128), np.float32)
    bd[:64, :64] = 1.0
    bd[64:, 64:] = 1.0
    c["ones_bd"] = bd
    c["ones_mean"] = np.full((128, 128), 1.0 / D, np.float32)
    s = np.arange(64)[:, None]
    t = np.arange(64)[None, :]
    m_ap = np.concatenate([(s < t), (s <= t)], axis=1).astype(np.float32)
    c["m_ap"] = np.concatenate([m_ap, m_ap], axis=0)
    m_low = (t < s).astype(np.float32)
    c["m_low"] = np.concatenate([m_low, m_low], axis=0)
    ge = (t >= s).astype(np.float32)
    lt = (t < s).astype(np.float32)
    ge2 = np.concatenate([ge, ge], axis=1)
    lt2 = np.concatenate([lt, lt], axis=1)
    c["m_ge"] = np.concatenate([ge2, ge2], axis=0)
    c["m_lt"] = np.concatenate([lt2, lt2], axis=0)
    sm = np.ones((128, ST), np.float32)
    sm[:, ::CH] = 0.0
    c["scanmask"] = sm
    lg = np.log1p(-(2.0 ** (-5.0 - np.arange(4, dtype=np.float64))))
    pos = np.arange(64, dtype=np.float64)
    dk = 32
    retD = np.zeros((128, 2, 64), np.float64)
    qdec = np.zeros((128, 2, 64), np.float64)
    kdec = np.zeros((128, 2, 64), np.float64)
    cdec = np.zeros((128, 2), np.float64)
    for u in range(2):
        for hp in range(2):
            h = 2 * u + hp
            dm = np.exp(lg[h] * np.abs(pos[:, None] - pos[None, :])) * dk ** -0.5
            retD[64 * hp:64 * hp + 64, u, :] = dm
            qdec[64 * hp:64 * hp + 32, u, :] = (np.exp(lg[h] * (pos + 1.0)) * dk ** -0.5)[None, :]
            kdec[64 * hp:64 * hp + 32, u, :] = np.exp(lg[h] * (63.0 - pos))[None, :]
            cdec[64 * hp:64 * hp + 32, u] = np.exp(lg[h] * 64.0)
    c["retD"] = retD.reshape(128, 128).astype(np.float32)
    c["qdec"] = qdec.reshape(128, 128).astype(np.float32)
    c["kdec"] = kdec.reshape(128, 128).astype(np.float32)
    c["cdec"] = cdec.astype(np.float32)
    half = 16
    inv_freq = (10000.0 ** (-np.arange(half, dtype=np.float32) / half)).astype(np.float32)
    ang = np.arange(SEQ, dtype=np.float32)[:, None] * inv_freq[None, :]
    cos = np.cos(ang.astype(np.float64)).T
    sin = np.sin(ang.astype(np.float64)).T
    ropeC = np.zeros((128, SEQ), np.float64)
    ropeS = np.zeros((128, SEQ), np.float64)
    for hp in range(2):
        for j in range(2):
            rows = slice(64 * hp + 16 * j, 64 * hp + 16 * j + 16)
            ropeC[rows] = cos
            ropeS[rows] = -sin if j == 0 else sin
    c["ropeC"] = ropeC.astype(np.float32)
    c["ropeS"] = ropeS.astype(np.float32)
    return c


def _pcol_layout():
    off = {}
    n = 0

    def add(name, k):
        nonlocal n
        off[name] = n
        n += k
    for l in range(DEPTH):
        add(("n1g", l), 8)
        add(("n2g", l), 8)
        for j in range(3):
            add(("mux", l, j), 8)
        add(("muv", l), 8)
        for nm in ("w0", "a0", "kk", "ka", "rk", "lng", "lnb", "v0"):
            add((nm, l), 4)
        add(("glab", l), 2)
        add(("glng", l), 1)
        add(("adab", l), 48)
    add(("nfg",), 8)
    add(("cT",), 8)
    return off, n


def _col8(v):
    return np.ascontiguousarray(np.asarray(v, np.float32).reshape(8, 128).T)


def _col4(v):
    return np.ascontiguousarray(np.asarray(v, np.float32).reshape(4, 128).T)


def _build_pcol(inp, b):
    off, n = _pcol_layout()
    t = np.zeros((128, n), np.float32)
    for l in range(DEPTH):
        t[:, off[("n1g", l)]:off[("n1g", l)] + 8] = _col8(inp["norm1_g"][l])
        t[:, off[("n2g", l)]:off[("n2g", l)] + 8] = _col8(inp["norm2_g"][l])
        for j in range(3):
            t[:, off[("mux", l, j)]:off[("mux", l, j)] + 8] = _col8(inp["rk_mu_x"][l, j])
        if l >= 1:
            t[:, off[("muv", l)]:off[("muv", l)] + 8] = _col8(inp["rk_mu_v"][l - 1])
            t[:, off[("v0", l)]:off[("v0", l)] + 4] = _col4(inp["rk_v0"][l - 1])
        for nm, key in (("w0", "rk_w0"), ("a0", "rk_a0"), ("kk", "rk_k_k"), ("ka", "rk_k_a"),
                        ("lng", "rk_ln_g"), ("lnb", "rk_ln_b")):
            t[:, off[(nm, l)]:off[(nm, l)] + 4] = _col4(inp[key][l])
        t[:, off[("rk", l)]:off[("rk", l)] + 4] = _col4(np.asarray(inp["rk_r_k"][l]).reshape(512))
        gab = np.asarray(inp["gla_ab"][l], np.float32)
        for u in range(2):
            for hh in range(2):
                t[64 * hh:64 * hh + 32, off[("glab", l)] + u] = gab[64 * u + 32 * hh:64 * u + 32 * hh + 32]
        lg_ = np.asarray(inp["gla_ln_g"][l], np.float32)
        t[:, off[("glng", l)]] = np.concatenate([lg_, lg_])
        t[:, off[("adab", l)]:off[("adab", l)] + 48] = np.asarray(inp["ada_b"][l], np.float32).reshape(48, 128).T
    t[:, off[("nfg",)]:off[("nfg",)] + 8] = _col8(inp["norm_f_g"])
    t[:, off[("cT",)]:off[("cT",)] + 8] = _col8(inp["c"][b])
    return t


class _Stop(Exception):
    pass


def build_program(nseg=SEQ // SEG, nlayer=DEPTH, dbg=None, stop=None):
    nc = bass.Bass("TRN2", target_bir_lowering=False)
    ntok = nseg * SEG
    poff, pn = _pcol_layout()
    cst = _consts()

    def din(name, shape):
        return nc.dram_tensor(name, list(shape), F32, kind="ExternalInput").ap()

    x_d = din("x", [SEQ, D])
    pcol_d = din("pcol", [128, pn])
    ada_w = din("ada_w", [DEPTH, D, 6 * D])
    w_in = din("w_in", [DEPTH, D, 3072])
    w_out = din("w_out", [DEPTH, D, D])
    mu_rkv = din("rk_mu_rkv", [DEPTH, 3, 512])
    rk_w1 = din("rk_w1", [DEPTH, D, 64])
    rk_a1 = din("rk_a1", [DEPTH, D, 64])
    rk_g1 = din("rk_g1", [DEPTH, D, 128])
    rk_v1 = din("rk_v1", [DEPTH - 1, D, 32])
    gla_a1 = din("gla_a1", [DEPTH, D, 16])
    rk_w2 = din("rk_w2", [DEPTH, 64, 512])
    rk_a2 = din("rk_a2", [DEPTH, 64, 512])
    rk_g2 = din("rk_g2", [DEPTH, 128, 512])
    rk_v2 = din("rk_v2", [DEPTH - 1, 32, 512])
    gla_a2 = din("gla_a2", [DEPTH, 16, 128])
    gla_lng = din("gla_ln_g", [DEPTH, 64])
    wg_d = din("ffn_w_gate", [DEPTH, D, DFF])
    wu_d = din("ffn_w_up", [DEPTH, D, DFF])
    wd_d = din("ffn_w_down", [DEPTH, DFF, D])
    cd = {k: din("c_" + k, v.shape) for k, v in cst.items()}
    out_d = nc.dram_tensor("out", [SEQ, D], F32, kind="ExternalOutput").ap()
    wab_d = nc.dram_tensor("wab_scr", [DEPTH * 4, 128, 8 * 768], BF16, kind="Internal").ap()
    lw_d = nc.dram_tensor("lw_scr", [DEPTH, 128, 2 * 8 * 304], BF16, kind="Internal").ap()
    wo_d = nc.dram_tensor("wo_scr", [DEPTH, 128, 8 * D], BF16, kind="Internal").ap()
    dbg_d = None
    if dbg is not None:
        dbg_d = nc.dram_tensor("dbg", [128, dbg[1]], F32, kind="ExternalOutput").ap()

    es = ExitStack()
    with es:
        P = Prog(nc, es)
        op, dma = P.op, P.dma

        psb = [Tile(es.enter_context(nc.psum_tensor("ps%d" % i, [128, 512], F32))) for i in range(8)]
        for t_ in psb:
            t_.s.excl = True
        prr = [0]

        held = set()

        def ps(hold=False):
            for _ in range(16):
                i = prr[0]
                prr[0] = (prr[0] + 1) % 8
                if i not in held:
                    if hold:
                        held.add(i)
                    return psb[i]
            raise RuntimeError("no free psum bank")

        def release(t):
            held.discard(psb.index(t))

        xT = P.sb("xT", [128, 8, SEG], F32)
        hT = P.sb("hT", [128, 8, SEG + 1], BF16)
        yT = P.sb("yT", [128, 8, SEG], BF16)
        vfirst = P.sb("vfirst", [128, 4, SEG], BF16)
        pcol = P.sb("pcol", [128, pn], F32)
        drv = P.sb("drv", [128, DEPTH, 64], F32)
        mod = P.sb("mod", [128, DEPTH, 48], F32)
        identf = P.sb("identf", [128, 128], F32)
        identb = P.sb("identb", [128, 128], BF16)
        ones_bd = P.sb("ones_bd", [128, 128], BF16)
        ones_mean = P.sb("ones_mean", [128, 128], BF16)
        m_ap = P.sb("m_ap", [128, 128], F32)
        m_low = P.sb("m_low", [128, 64], F32)
        m_ge = P.sb("m_ge", [128, 128], F32)
        m_lt = P.sb("m_lt", [128, 128], F32)
        scanmask = P.sb("scanmask", [128, ST], F32)
        retD = P.sb("retD", [128, 128], F32)
        qdec = P.sb("qdec", [128, 128], F32)
        kdec = P.sb("kdec", [128, 128], F32)
        cdec = P.sb("cdec", [128, 2], F32)
        up_gl = P.sb("up_gl", [48, DEPTH, 2, 128], BF16)
        epsc = P.sb("epsc", [128, 4], F32)
        glng = P.sb("glng", [128, DEPTH, 64], F32)
        wab_slots = [Slot() for _ in range(DEPTH * 4)]
        lw_slots = [Slot() for _ in range(DEPTH)]
        wo_slots = [Slot() for _ in range(DEPTH)]
        up_wa = P.sb("up_wa", [128, DEPTH, 512], BF16)
        up_g = P.sb("up_g", [128, DEPTH, 512], BF16)
        up_c = P.sb("up_c", [48, DEPTH, 512], BF16)
        Mf = P.sb("Mf", [128, DEPTH, 4, 64], F32)
        Mb = P.sb("Mb", [128, DEPTH, 4, 64], BF16)
        Sg = P.sb("Sg", [128, DEPTH, 2, 64], F32)
        Sr = P.sb("Sr", [128, DEPTH, 2, 64], F32)
        hcar = P.sb("hcar", [128, DEPTH, 8], BF16)
        mids = [P.sb("midA", [128, SEG], BF16), P.sb("midG", [128, SEG], BF16), P.sb("midC", [48, SEG], BF16)]

        def pc(key, k=1, j=0):
            o = poff[key] + j
            return pcol[:, o:o + k]

        dma("sp", pcol[:], pcol_d, w=[pcol])
        dma("sp", identf[:], cd["ident"], w=[identf])
        dma("pool", identb[:], cd["ident"], w=[identb])
        dma("pool", ones_bd[:], cd["ones_bd"], w=[ones_bd])
        dma("pool", ones_mean[:], cd["ones_mean"], w=[ones_mean])
        for tl, nm in ((m_ap, "m_ap"), (m_low, "m_low"), (m_ge, "m_ge"), (m_lt, "m_lt"), (scanmask, "scanmask"),
                       (retD, "retD"), (qdec, "qdec"), (kdec, "kdec"), (cdec, "cdec")):
            dma("sp", tl[:], cd[nm], w=[tl])
        for l in range(DEPTH):
            dma("sp", glng[:, l, :], gla_lng[l:l + 1, :].partition_broadcast(128), w=[glng])
        op("dve", lambda e: e.memset(epsc[:, 0:1], EPS), w=[epsc])
        op("dve", lambda e: e.memset(epsc[:, 1:2], GN_EPS), w=[epsc])
        op("dve", lambda e: e.memset(epsc[:, 2:3], 1e-24), w=[epsc])
        op("dve", lambda e: e.memset(epsc[:, 3:4], 1.0), w=[epsc])
        for tl in (Mf, Mb, Sg, Sr, hcar, up_gl):
            op("dve", lambda e, tl=tl: e.memset(tl[:], 0.0), w=[tl])

        for l in range(nlayer):
            dma("pool", up_wa[0:64, l, :], rk_w2[l], w=[up_wa])
            dma("pool", up_wa[64:128, l, :], rk_a2[l], w=[up_wa])
            dma("pool", up_g[:, l, :], rk_g2[l], w=[up_g])
            if l >= 1:
                dma("pool", up_c[0:32, l, :], rk_v2[l - 1], w=[up_c])
            for u_ in range(2):
                for hh_ in range(2):
                    dma("pool", up_gl[32:48, l, u_, 64 * hh_:64 * hh_ + 32], gla_a2[l, :, 64 * u_ + 32 * hh_:64 * u_ + 32 * hh_ + 32], w=[up_gl])

        with ExitStack() as s0:
            condb = P.sb("condb", [128, 8], BF16, s0)
            omu = P.sb("omu", [128, DEPTH, 4, 8], F32, s0)
            ldwf = P.sb("ldwf", [128, 8, 304], F32, s0)
            adaw = [P.sb("adaw%d" % i, [128, 8, 512], BF16, s0) for i in range(2)]
            lwab = P.sb("lwab", [128, 2, 8, 304], BF16, s0)
            wst = P.sb("wst", [128, 8, 384], F32, s0)
            murow = P.sb("murow", [128, 384], F32, s0)
            omurow = P.sb("omurow", [128, 384], F32, s0)
            WABs = P.sb("WABs", [128, 8, 2, 384], BF16, s0)
            scoped = [condb, omu, ldwf, lwab, wst, murow, omurow, WABs] + adaw
            op("act", lambda e: e.activation(out=condb[:], in_=pc(("cT",), 8), func=AF.Silu), r=[pcol], w=[condb])
            for l in range(nlayer):
                mps = ps()
                for piece in range(12):
                    aw = adaw[piece % 2]
                    dma("pool", aw[:], ada_w[l, :, piece * 512:(piece + 1) * 512].rearrange("(kc p) n -> p kc n", p=128), w=[aw])
                    for j in range(4):
                        jc = piece * 4 + j
                        for kc in range(8):
                            op("pe", lambda e, kc=kc, j=j, jc=jc, aw=aw: e.matmul(
                                mps[:, jc:jc + 1], aw[:, kc, j * 128:(j + 1) * 128], condb[:, kc:kc + 1],
                                start=(kc == 0), stop=(kc == 7)), r=[aw, condb], w=[mps])
                op("dve", lambda e, l=l, mps=mps: e.tensor_tensor(out=mod[:, l, :], in0=mps[:, 0:48], in1=pc(("adab", l), 48), op=ALU.add),
                   r=[mps, pcol], w=[mod])
                op("dve", lambda e, l=l: e.scalar_tensor_tensor(out=drv[:, l, 0:8], in0=mod[:, l, 8:16], scalar=1.0, in1=pc(("n1g", l), 8),
                                                                op0=ALU.add, op1=ALU.mult), r=[mod, pcol], w=[drv])
                op("dve", lambda e, l=l: e.scalar_tensor_tensor(out=drv[:, l, 8:16], in0=mod[:, l, 32:40], scalar=1.0, in1=pc(("n2g", l), 8),
                                                                op0=ALU.add, op1=ALU.mult), r=[mod, pcol], w=[drv])
                op("dve", lambda e, l=l: e.tensor_scalar(out=drv[:, l, 16:20], in0=pc(("ka", l), 4), scalar1=-1.0, scalar2=1.0,
                                                         op0=ALU.mult, op1=ALU.add), r=[pcol], w=[drv])
                for j in range(4):
                    src = pc(("mux", l, j), 8) if j < 3 else pc(("muv", l), 8)
                    op("dve", lambda e, l=l, j=j, src=src: e.tensor_scalar(out=omu[:, l, j, :], in0=src, scalar1=-1.0, scalar2=1.0,
                                                                           op0=ALU.mult, op1=ALU.add), r=[pcol], w=[omu])
                op("dve", lambda e: e.memset(ldwf[:], 0.0), w=[ldwf])
                rr = lambda a: a.rearrange("(kc p) n -> p kc n", p=128)
                dma("sp", ldwf[:, :, 0:64], rr(rk_w1[l]), w=[ldwf])
                dma("sp", ldwf[:, :, 64:128], rr(rk_a1[l]), w=[ldwf])
                dma("sp", ldwf[:, :, 128:256], rr(rk_g1[l]), w=[ldwf])
                if l >= 1:
                    dma("sp", ldwf[:, :, 256:288], rr(rk_v1[l - 1]), w=[ldwf])
                dma("sp", ldwf[:, :, 288:304], rr(gla_a1[l]), w=[ldwf])
                for kc in range(8):
                    for j, (c0, c1) in enumerate(((0, 64), (64, 128), (128, 256), (256, 288))):
                        mu_ap = (pc(("mux", l, j), 8) if j < 3 else pc(("muv", l), 8))[:, kc:kc + 1]
                        op("dve", lambda e, kc=kc, c0=c0, c1=c1, mu_ap=mu_ap, l=l: e.tensor_scalar(
                            out=lwab[:, 1, kc, c0:c1], in0=ldwf[:, kc, c0:c1], scalar1=mu_ap, scalar2=None, op0=ALU.mult),
                            r=[ldwf, pcol], w=[lwab])
                        op("dve", lambda e, kc=kc, c0=c0, c1=c1, j=j, l=l: e.tensor_scalar(
                            out=lwab[:, 0, kc, c0:c1], in0=ldwf[:, kc, c0:c1], scalar1=omu[:, l, j, kc:kc + 1], scalar2=None, op0=ALU.mult),
                            r=[ldwf, omu], w=[lwab])
                    op("dve", lambda e, kc=kc, l=l: e.tensor_copy(out=lwab[:, 0, kc, 288:304], in_=ldwf[:, kc, 288:304]), r=[ldwf], w=[lwab])
                    op("dve", lambda e, kc=kc, l=l: e.memset(lwab[:, 1, kc, 288:304], 0.0), w=[lwab])
                dma("sp", lw_d[l], lwab[:].rearrange("q a k n -> q (a k n)"), r=[lwab], w=[lw_slots[l]])
                for hf in range(2):
                    aw = adaw[hf]
                    dma("pool", aw[:], w_out[l, :, hf * 512:(hf + 1) * 512].rearrange("(kc q) n -> q kc n", q=128), w=[aw])
                    dma("sp", wo_d[l].rearrange("q (k n) -> q k n", k=8)[:, :, hf * 512:(hf + 1) * 512], aw[:], r=[aw], w=[wo_slots[l]])
                for p in range(4):
                    for j in range(3):
                        dma("sp", wst[:, :, j * 128:(j + 1) * 128],
                            w_in[l, :, j * 512 + p * 128:j * 512 + (p + 1) * 128].rearrange("(kc q) n -> q kc n", q=128), w=[wst])
                        dma("sp", murow[:, j * 128:(j + 1) * 128], mu_rkv[l, j:j + 1, p * 128:(p + 1) * 128].partition_broadcast(128), w=[murow])
                    op("dve", lambda e: e.tensor_scalar(out=omurow[:], in0=murow[:], scalar1=-1.0, scalar2=1.0, op0=ALU.mult, op1=ALU.add),
                       r=[murow], w=[omurow])
                    op("dve", lambda e: e.tensor_tensor(out=WABs[:, :, 1, :], in0=wst[:], in1=murow[:].unsqueeze(1).to_broadcast([128, 8, 384]), op=ALU.mult),
                       r=[wst, murow], w=[WABs])
                    op("dve", lambda e: e.tensor_tensor(out=WABs[:, :, 0, :], in0=wst[:], in1=omurow[:].unsqueeze(1).to_broadcast([128, 8, 384]), op=ALU.mult),
                       r=[wst, omurow], w=[WABs])
                    dma("sp", wab_d[l * 4 + p], WABs[:].rearrange("q k a n -> q (k a n)"), r=[WABs], w=[wab_slots[l * 4 + p]])
            P.add_fence(scoped)

        def rmsnorm_to_hT(gm, sh, es_l):
            sqb = P.sb("sqb", [128, 8, ST], BF16, es_l)
            rstd = P.sb("rstd", [128, ST], F32, es_l)
            tmpf = [P.sb("tmpf%d" % i, [128, ST], F32, es_l) for i in range(2)]
            for st in range(SEG // ST):
                ts = slice(st * ST, (st + 1) * ST)
                mps = ps()
                for c in range(8):
                    op("act", lambda e, c=c: e.activation(out=sqb[:, c, :], in_=xT[:, c, ts], func=AF.Square), r=[xT], w=[sqb.sub(c)])
                    op("pe", lambda e, c=c: e.matmul(mps[:], ones_mean[:], sqb[:, c, :], start=(c == 0), stop=(c == 7)),
                       r=[ones_mean, sqb.sub(c)], w=[mps])
                op("act", lambda e: e.activation(out=rstd[:], in_=mps[:], func=AF.Ln, bias=epsc[:, 0:1], scale=1.0), r=[mps, epsc], w=[rstd])
                op("act", lambda e: e.activation(out=rstd[:], in_=rstd[:], func=AF.Exp, scale=-0.5), r=[rstd], w=[rstd])
                for c in range(8):
                    tf = tmpf[c % 2]
                    op("dve", lambda e, c=c, tf=tf: e.scalar_tensor_tensor(out=tf[:], in0=xT[:, c, ts], scalar=gm[:, c:c + 1], in1=rstd[:],
                                                                          op0=ALU.mult, op1=ALU.mult), r=[xT, rstd, drv, pcol], w=[tf])
                    dst = hT[:, c, 1 + st * ST:1 + (st + 1) * ST]
                    if sh is not None:
                        op("act", lambda e, c=c, tf=tf, dst=dst: e.activation(out=dst, in_=tf[:], func=AF.Identity, bias=sh[:, c:c + 1], scale=1.0),
                           r=[tf, mod], w=[hT.sub(st)])
                    else:
                        op("act", lambda e, tf=tf, dst=dst: e.activation(out=dst, in_=tf[:], func=AF.Copy), r=[tf], w=[hT.sub(st)])
            return [sqb, rstd] + tmpf

        def hslots():
            return [hT.sub(i) for i in range(SEG // ST)] + [hT.sub("c0")]

        STR = int(os.environ.get('STR', '256'))
        NCR = STR // CH

        def run_streams(gens, periods=None):
            live = list(gens)
            per = dict((id(g_), (periods[i] if periods else 1)) for i, g_ in enumerate(gens))
            rnd = 0
            while live:
                for g_ in list(live):
                    if rnd % per[id(g_)] != 0 and len(live) > 1:
                        continue
                    try:
                        next(g_)
                    except StopIteration:
                        live.remove(g_)
                rnd += 1

        def rwkv_pair(l, p, es_u, tiles):
            def T(name, shape, dt):
                t = P.sb(name, shape, dt, es_u)
                tiles.append(t)
                return t
            WAB = T("WAB", [128, 8, 2, 384], BF16)
            dma("sp", WAB[:].rearrange("q k a n -> q (k a n)"), wab_d[l * 4 + p], r=[wab_slots[l * 4 + p]], w=[WAB])
            cs = slice(p * 128, (p + 1) * 128)
            col = lambda nm: pcol[:, poff[(nm, l)] + p:poff[(nm, l)] + p + 1]
            HP = (slice(0, 64), slice(64, 128))

            def v3(ap):
                return ap.rearrange("q (c t) -> q c t", c=NCR)

            def stream(sidx):
                f = lambda n: T(n, [128, STR], F32)
                sig, cum, a_t, g_t, r_t, k_t, v_t, kk_t, kt_t, tA, tB, eG, eI, eX, eE, bon = [f(n) for n in
                    ("sig", "cum", "a_t", "g_t", "r_t", "k_t", "v_t", "kk_t", "kt_t", "tA", "tB", "eG", "eI", "eX", "eE", "bon")]
                tbf = T("tbf", [128, STR], BF16)
                RK = T("RK", [128, NCR, 2, 64], BF16)
                LK = T("LK", [128, NCR, 2, 64], BF16)
                EF = T("EF", [128, 2, STR], BF16)
                vbf = T("vbf", [128, STR], BF16)
                Et = T("Et", [128, NCR, 2, 64], BF16)
                Vt = T("Vt", [128, NCR, 64], BF16)
                APs = T("APs", [128, NCR, 128], BF16)
                BQs = T("BQs", [128, NCR, 128], BF16)
                Apl = [T("Apl%d" % i, [128, NCR, 64], BF16) for i in range(2)]
                ATr = [T("ATr%d" % i, [128, NCR, 64], BF16) for i in range(2)]
                X = [T("X%d" % i, [128, NCR, 128], BF16) for i in range(2)]
                Gt = T("Gt", [128, NCR, 64], BF16)
                Yt = T("Yt", [128, NCR, 64], BF16)
                ysq = T("ysq", [128, STR], F32)
                st1 = T("st1", [128, NCR, 4], F32)
                yn = T("yn", [128, NCR, 64], BF16)
                yield
                for st in range(sidx, SEG // STR, 2):
                    ts = slice(st * STR, (st + 1) * STR)
                    cur = slice(1 + st * STR, 1 + (st + 1) * STR)
                    prv = slice(st * STR, (st + 1) * STR)
                    st5 = (st * STR) // ST
                    hs = hslots()
                    for j, dst in enumerate((r_t, k_t, v_t)):
                        pp = ps()
                        for kc in range(8):
                            op("pe", lambda e, j=j, pp=pp, kc=kc: e.matmul(pp[:, 0:STR], WAB[:, kc, 0, j * 128:(j + 1) * 128], hT[:, kc, cur],
                                                                          start=(kc == 0), stop=False), r=[WAB] + hs, w=[pp])
                            op("pe", lambda e, j=j, pp=pp, kc=kc: e.matmul(pp[:, 0:STR], WAB[:, kc, 1, j * 128:(j + 1) * 128], hT[:, kc, prv],
                                                                          start=False, stop=(kc == 7)), r=[WAB] + hs, w=[pp])
                        op("act", lambda e, pp=pp, dst=dst: e.activation(out=dst[:], in_=pp[:, 0:STR], func=AF.Copy), r=[pp], w=[dst])
                        yield
                    pw, pa, pg = ps(), ps(), ps()
                    op("pe", lambda e: e.matmul(pw[:, 0:STR], up_wa[0:64, l, cs], mids[0][0:64, ts], start=True, stop=True), r=[up_wa, mids[0].sub(st5)], w=[pw])
                    op("pe", lambda e: e.matmul(pa[:, 0:STR], up_wa[64:128, l, cs], mids[0][64:128, ts], start=True, stop=True), r=[up_wa, mids[0].sub(st5)], w=[pa])
                    op("pe", lambda e: e.matmul(pg[:, 0:STR], up_g[:, l, cs], mids[1][:, ts], start=True, stop=True), r=[up_g, mids[1].sub(st5)], w=[pg])
                    op("act", lambda e: e.activation(out=sig[:], in_=pw[:, 0:STR], func=AF.Sigmoid, bias=col("w0"), scale=1.0), r=[pw, pcol], w=[sig])
                    op("act", lambda e: e.activation(out=a_t[:], in_=pa[:, 0:STR], func=AF.Sigmoid, bias=col("a0"), scale=1.0), r=[pa, pcol], w=[a_t])
                    op("act", lambda e: e.activation(out=g_t[:], in_=pg[:, 0:STR], func=AF.Copy), r=[pg], w=[g_t])
                    if l >= 1:
                        pvg = ps()
                        op("pe", lambda e: e.matmul(pvg[:, 0:STR], up_c[0:32, l, cs], mids[2][0:32, ts], start=True, stop=True), r=[up_c, mids[2].sub(st5)], w=[pvg])
                        op("act", lambda e: e.activation(out=tA[:], in_=pvg[:, 0:STR], func=AF.Sigmoid, bias=col("v0"), scale=1.0), r=[pvg, pcol], w=[tA])
                        yield
                        op(POOLENG, lambda e: e.tensor_tensor(out=tB[:], in0=vfirst[:, p, ts], in1=v_t[:], op=ALU.subtract), r=[vfirst.sub((p, st)), v_t], w=[tB])
                        op(POOLENG, lambda e: e.tensor_tensor(out=tB[:], in0=tB[:], in1=tA[:], op=ALU.mult), r=[tB, tA], w=[tB])
                        op(POOLENG, lambda e: e.tensor_tensor(out=v_t[:], in0=v_t[:], in1=tB[:], op=ALU.add), r=[v_t, tB], w=[v_t])
                    else:
                        yield
                        op("act", lambda e: e.activation(out=vfirst[:, p, ts], in_=v_t[:], func=AF.Copy), r=[v_t], w=[vfirst.sub((p, st))])
                    op("act", lambda e: e.activation(out=vbf[:], in_=v_t[:], func=AF.Copy), r=[v_t], w=[vbf])
                    yield
                    op("dve", lambda e: e.tensor_tensor_scan(out=cum[:], data0=scanmask[:, 0:STR], data1=sig[:], initial=0.0, op0=ALU.mult, op1=ALU.add),
                       r=[scanmask, sig], w=[cum])
                    op(POOLENG, lambda e: e.tensor_tensor(out=tA[:], in0=cum[:], in1=sig[:], op=ALU.subtract), r=[cum, sig], w=[tA])
                    op(POOLENG, lambda e: e.tensor_tensor(out=v3(tB[:]), in0=v3(cum[:])[:, :, 63:64].to_broadcast([128, NCR, 64]), in1=v3(cum[:]),
                                                        op=ALU.subtract), r=[cum], w=[tB])
                    yield
                    op("act", lambda e: e.activation(out=eG[:], in_=cum[:], func=AF.Exp, scale=-CDEC), r=[cum], w=[eG])
                    op("act", lambda e: e.activation(out=eI[:], in_=cum[:], func=AF.Exp, scale=CDEC), r=[cum], w=[eI])
                    op("act", lambda e: e.activation(out=eX[:], in_=tA[:], func=AF.Exp, scale=-CDEC), r=[tA], w=[eX])
                    op("act", lambda e: e.activation(out=eE[:], in_=tB[:], func=AF.Exp, scale=-CDEC), r=[tB], w=[eE])
                    op("act", lambda e: e.activation(out=kk_t[:], in_=k_t[:], func=AF.Copy, scale=col("kk")), r=[k_t, pcol], w=[kk_t])
                    op("act", lambda e: e.activation(out=tbf[:], in_=kk_t[:], func=AF.Square), r=[kk_t], w=[tbf])
                    pss = ps()
                    op("pe", lambda e: e.matmul(pss[:, 0:STR], ones_bd[:], tbf[:], start=True, stop=True), r=[ones_bd, tbf], w=[pss])
                    op("dve", lambda e: e.tensor_scalar(out=tA[:], in0=pss[:, 0:STR], scalar1=1e-24, scalar2=None, op0=ALU.max), r=[pss], w=[tA])
                    yield
                    op("act", lambda e: e.activation(out=tA[:], in_=tA[:], func=AF.Ln), r=[tA], w=[tA])
                    op("act", lambda e: e.activation(out=tA[:], in_=tA[:], func=AF.Exp, scale=-0.5), r=[tA], w=[tA])
                    op("act", lambda e: e.activation(out=tB[:], in_=a_t[:], func=AF.Identity, scale=col("ka"), bias=drv[:, l, 16 + p:17 + p]),
                       r=[a_t, pcol, drv], w=[tB])
                    op(POOLENG, lambda e: e.tensor_tensor(out=kt_t[:], in0=k_t[:], in1=tB[:], op=ALU.mult), r=[k_t, tB], w=[kt_t])
                    op("dve", lambda e: e.scalar_tensor_tensor(out=tbf[:], in0=r_t[:], scalar=col("rk"), in1=kt_t[:], op0=ALU.mult, op1=ALU.mult),
                       r=[r_t, kt_t, pcol], w=[tbf])
                    pbn = ps()
                    op("pe", lambda e: e.matmul(pbn[:, 0:STR], ones_bd[:], tbf[:], start=True, stop=True), r=[ones_bd, tbf], w=[pbn])
                    op("dve", lambda e: e.tensor_tensor(out=bon[:], in0=pbn[:, 0:STR], in1=v_t[:], op=ALU.mult), r=[pbn, v_t], w=[bon])
                    yield
                    op("dve", lambda e: e.tensor_tensor(out=kk_t[:], in0=kk_t[:], in1=tA[:], op=ALU.mult), r=[kk_t, tA], w=[kk_t])
                    op(POOLENG, lambda e: e.tensor_tensor(out=tA[:], in0=kk_t[:], in1=a_t[:], op=ALU.mult), r=[kk_t, a_t], w=[tA])
                    op("dve", lambda e: e.tensor_tensor(out=RK[:, :, 0, :], in0=v3(kk_t[:]), in1=v3(eX[:]), op=ALU.mult), r=[kk_t, eX], w=[RK])
                    op("dve", lambda e: e.tensor_tensor(out=RK[:, :, 1, :], in0=v3(r_t[:]), in1=v3(eG[:]), op=ALU.mult), r=[r_t, eG], w=[RK])
                    yield
                    op("dve", lambda e: e.scalar_tensor_tensor(out=LK[:, :, 0, :], in0=v3(tA[:]), scalar=-1.0, in1=v3(eI[:]), op0=ALU.mult, op1=ALU.mult),
                       r=[tA, eI], w=[LK])
                    op("dve", lambda e: e.tensor_tensor(out=LK[:, :, 1, :], in0=v3(kt_t[:]), in1=v3(eI[:]), op=ALU.mult), r=[kt_t, eI], w=[LK])
                    op("dve", lambda e: e.scalar_tensor_tensor(out=EF[:, 0, :], in0=tA[:], scalar=-1.0, in1=eE[:], op0=ALU.mult, op1=ALU.mult),
                       r=[tA, eE], w=[EF])
                    op("dve", lambda e: e.tensor_tensor(out=EF[:, 1, :], in0=kt_t[:], in1=eE[:], op=ALU.mult), r=[kt_t, eE], w=[EF])
                    yield
                    pt1, pt2 = ps(), ps()
                    pt1b = pt1[:].bitcast(BF16)
                    pt2b = pt2[:].bitcast(BF16)
                    for c in range(NCR):
                        for h in range(2):
                            hp = HP[h]
                            cc = slice(c * 64, (c + 1) * 64)
                            op("pe", lambda e, c=c, hp=hp, cc=cc: e.transpose(pt1b[hp, c * 128:c * 128 + 64], EF[hp, 0, cc], identb[hp, hp]),
                               r=[EF, identb], w=[pt1])
                            op("pe", lambda e, c=c, hp=hp, cc=cc: e.transpose(pt1b[hp, c * 128 + 64:c * 128 + 128], EF[hp, 1, cc], identb[hp, hp]),
                               r=[EF, identb], w=[pt1])
                            op("pe", lambda e, c=c, hp=hp, cc=cc: e.transpose(pt2b[hp, c * 64:c * 64 + 64], vbf[hp, cc], identb[hp, hp]),
                               r=[vbf, identb], w=[pt2])
                            op("pe", lambda e, c=c, hp=hp: e.transpose(pt2b[hp, 512 + c * 64:512 + c * 64 + 64], RK[hp, c, 0, :], identb[hp, hp]),
                               r=[RK, identb], w=[pt2])
                    op("act", lambda e: e.activation(out=Et[:].rearrange("q c j s -> q (c j s)"), in_=pt1b[:, 0:NCR * 128], func=AF.Copy), r=[pt1], w=[Et])
                    op("dve", lambda e: e.tensor_copy(out=Vt[:].rearrange("q c s -> q (c s)"), in_=pt2b[:, 0:NCR * 64]), r=[pt2], w=[Vt])
                    op("dve", lambda e: e.tensor_copy(out=X[0][:, :, 0:64], in_=pt2b[:, 512:512 + NCR * 64].rearrange("q (c s) -> q c s", c=NCR)), r=[pt2], w=[X[0].sub("k")])
                    yield
                    pap, pbq, pal = ps(), ps(), ps()
                    for c in range(NCR):
                        for h in range(2):
                            hp = HP[h]
                            op("pe", lambda e, c=c, hp=hp: e.matmul(
                                pap[hp, c * 128:(c + 1) * 128], LK[hp, c, 0, :], RK[hp, c, :, :].rearrange("q j s -> q (j s)"),
                                start=True, stop=True), r=[LK, RK], w=[pap])
                            op("pe", lambda e, c=c, hp=hp: e.matmul(
                                pbq[hp, c * 128:(c + 1) * 128], LK[hp, c, 1, :], RK[hp, c, :, :].rearrange("q j s -> q (j s)"),
                                start=True, stop=True), r=[LK, RK], w=[pbq])
                            op("pe", lambda e, c=c, hp=hp: e.matmul(pal[hp, c * 64:(c + 1) * 64], RK[hp, c, 0, :], LK[hp, c, 0, :],
                                                                    start=True, stop=True), r=[LK, RK], w=[pal])
                    mapb = m_ap[:].unsqueeze(1).to_broadcast([128, NCR, 128])
                    op("dve", lambda e: e.tensor_tensor(out=APs[:], in0=pap[:, 0:NCR * 128].rearrange("q (c s) -> q c s", c=NCR), in1=mapb, op=ALU.mult), r=[pap, m_ap], w=[APs])
                    op("dve", lambda e: e.tensor_tensor(out=BQs[:], in0=pbq[:, 0:NCR * 128].rearrange("q (c s) -> q c s", c=NCR), in1=mapb, op=ALU.mult), r=[pbq, m_ap], w=[BQs])
                    op("dve", lambda e: e.tensor_tensor(out=Apl[0][:], in0=v3(pal[:, 0:NCR * 64]), in1=m_low[:].unsqueeze(1).to_broadcast([128, NCR, 64]), op=ALU.mult),
                       r=[pal, m_low], w=[Apl[0]])
                    yield
                    pbv = ps()
                    for c in range(NCR):
                        for h in range(2):
                            hp = HP[h]
                            op("pe", lambda e, c=c, hp=hp: e.matmul(pbv[hp, c * 64:(c + 1) * 64], BQs[hp, c, 0:64], Vt[hp, c, :], start=True, stop=True),
                               r=[BQs, Vt], w=[pbv])
                    op("act", lambda e: e.activation(out=X[0][:, :, 64:128], in_=v3(pbv[:, 0:NCR * 64]), func=AF.Copy), r=[pbv], w=[X[0].sub("v")])
                    yield
                    xc = 0
                    ac = 0

                    def atr(lev, ac_, hp, c):
                        return APs[hp, c, 0:64] if lev == 0 else ATr[ac_][hp, c, :]
                    for lev in range(6):
                        px = ps()
                        xi, xo = X[xc], X[1 - xc]
                        asl = [APs] if lev == 0 else [ATr[ac]]
                        for c in range(NCR):
                            for h in range(2):
                                hp = HP[h]
                                op("pe", lambda e, c=c, hp=hp, xi=xi, ac=ac, lev=lev, px=px: e.matmul(
                                    px[hp, c * 128:(c + 1) * 128], atr(lev, ac, hp, c), xi[hp, c, :], start=True, stop=True),
                                   r=asl + [xi.sub("k"), xi.sub("v")], w=[px])
                        op("dve", lambda e, xi=xi, xo=xo, px=px: e.tensor_tensor(
                            out=xo[:], in0=px[:, 0:NCR * 128].rearrange("q (c s) -> q c s", c=NCR), in1=xi[:], op=ALU.add),
                           r=[px, xi.sub("k"), xi.sub("v")], w=[xo.sub("k"), xo.sub("v")])
                        xc = 1 - xc
                        if lev < 5:
                            pq2 = ps()
                            for c in range(NCR):
                                for h in range(2):
                                    hp = HP[h]
                                    if lev < 4:
                                        op("pe", lambda e, c=c, hp=hp, ac=ac, lev=lev, pq2=pq2: e.matmul(pq2[hp, c * 64:(c + 1) * 64], atr(lev, ac, hp, c), Apl[ac][hp, c, :],
                                                                                                    start=True, stop=True), r=asl + [Apl[ac]], w=[pq2])
                                    op("pe", lambda e, c=c, hp=hp, ac=ac, lev=lev, pq2=pq2: e.matmul(pq2[hp, 256 + c * 64:256 + (c + 1) * 64], Apl[ac][hp, c, :], atr(lev, ac, hp, c),
                                                                                                start=True, stop=True), r=asl + [Apl[ac]], w=[pq2])
                            nac = 1 - ac
                            if lev < 4:
                                op("act", lambda e, nac=nac, pq2=pq2: e.activation(out=Apl[nac][:], in_=v3(pq2[:, 0:NCR * 64]), func=AF.Copy), r=[pq2], w=[Apl[nac]])
                            op("act", lambda e, nac=nac, pq2=pq2: e.activation(out=ATr[nac][:], in_=v3(pq2[:, 256:256 + NCR * 64]), func=AF.Copy), r=[pq2], w=[ATr[nac]])
                            ac = nac
                        yield
                    XF = X[xc]
                    xfs = [XF.sub("k"), XF.sub("v")]
                    pgy = ps()
                    for c in range(NCR):
                        for h in range(2):
                            hp = HP[h]
                            op("pe", lambda e, c=c, hp=hp: e.matmul(pgy[hp, c * 64:(c + 1) * 64], XF[hp, c, 0:64], Et[hp, c, 0, :], start=True, stop=True),
                               r=xfs + [Et], w=[pgy])
                            op("pe", lambda e, c=c, hp=hp: e.matmul(pgy[hp, 256 + c * 64:256 + (c + 1) * 64], XF[hp, c, 0:64], APs[hp, c, 64:128], start=True, stop=True),
                               r=xfs + [APs], w=[pgy])
                    op("act", lambda e: e.activation(out=Gt[:], in_=v3(pgy[:, 0:NCR * 64]), func=AF.Copy), r=[pgy], w=[Gt])
                    op("dve", lambda e: e.tensor_tensor(out=Yt[:], in0=v3(pgy[:, 256:256 + NCR * 64]), in1=RK[:, :, 1, :], op=ALU.add), r=[pgy, RK], w=[Yt])
                    yield
                    py = ps(hold=True)
                    msl = Mf.sub((l, p))
                    mbs = Mb.sub((l, p))
                    for c in range(NCR):
                        pm = ps()
                        for h in range(2):
                            hp = HP[h]
                            yo = py[hp, c * 64:(c + 1) * 64]
                            op("pe", lambda e, c=c, hp=hp, yo=yo: e.matmul(yo, APs[hp, c, 64:128], XF[hp, c, 64:128], start=True, stop=False),
                               r=[APs] + xfs, w=[py])
                            op("pe", lambda e, c=c, hp=hp, yo=yo: e.matmul(yo, BQs[hp, c, 64:128], Vt[hp, c, :], start=False, stop=False),
                               r=[BQs, Vt], w=[py])
                            op("pe", lambda e, c=c, hp=hp, yo=yo: e.matmul(yo, Yt[hp, c, :], Mb[hp, l, p, :], start=False, stop=True),
                               r=[Yt, mbs], w=[py])
                            mo = pm[hp, 0:64]
                            op("pe", lambda e, c=c, hp=hp, mo=mo: e.matmul(mo, Et[hp, c, 0, :], XF[hp, c, 64:128], start=True, stop=False),
                               r=[Et] + xfs, w=[pm])
                            op("pe", lambda e, c=c, hp=hp, mo=mo: e.matmul(mo, Et[hp, c, 1, :], Vt[hp, c, :], start=False, stop=False),
                               r=[Et, Vt], w=[pm])
                            op("pe", lambda e, c=c, hp=hp, mo=mo: e.matmul(mo, Gt[hp, c, :], Mb[hp, l, p, :], start=False, stop=True),
                               r=[Gt, mbs], w=[pm])
                        op("dve", lambda e, c=c, pm=pm: e.scalar_tensor_tensor(out=Mb[:, l, p, :], in0=Mf[:, l, p, :], scalar=eG[:, c * 64 + 63:c * 64 + 64],
                                                                               in1=pm[:, 0:64], op0=ALU.mult, op1=ALU.add), r=[msl, eG, pm], w=[mbs])
                        op("dve", lambda e, c=c, pm=pm: e.scalar_tensor_tensor(out=Mf[:, l, p, :], in0=Mf[:, l, p, :], scalar=eG[:, c * 64 + 63:c * 64 + 64],
                                                                               in1=pm[:, 0:64], op0=ALU.mult, op1=ALU.add), r=[msl, eG, pm], w=[msl])
                    release(py)
                    py3 = v3(py[:, 0:NCR * 64])
                    op("dve", lambda e: e.tensor_reduce(out=st1[:, :, 0], in_=py3, axis=AX.X, op=ALU.add), r=[py], w=[st1])
                    op("act", lambda e: e.activation(out=ysq[:], in_=py[:, 0:STR], func=AF.Square), r=[py], w=[ysq])
                    op("dve", lambda e: e.tensor_reduce(out=st1[:, :, 1], in_=v3(ysq[:]), axis=AX.X, op=ALU.add), r=[ysq], w=[st1])
                    op("dve", lambda e: e.tensor_scalar(out=st1[:, :, 0], in0=st1[:, :, 0], scalar1=1.0 / 64, scalar2=None, op0=ALU.mult), r=[st1], w=[st1])
                    op("dve", lambda e: e.tensor_tensor(out=st1[:, :, 2], in0=st1[:, :, 0], in1=st1[:, :, 0], op=ALU.mult), r=[st1], w=[st1])
                    op("dve", lambda e: e.scalar_tensor_tensor(out=st1[:, :, 1], in0=st1[:, :, 1], scalar=1.0 / 64, in1=st1[:, :, 2], op0=ALU.mult, op1=ALU.subtract),
                       r=[st1], w=[st1])
                    op("act", lambda e: e.activation(out=st1[:, :, 1], in_=st1[:, :, 1], func=AF.Ln, bias=epsc[:, 1:2], scale=1.0), r=[st1, epsc], w=[st1])
                    op("act", lambda e: e.activation(out=st1[:, :, 1], in_=st1[:, :, 1], func=AF.Exp, scale=-0.5), r=[st1], w=[st1])
                    op("dve", lambda e: e.tensor_tensor(out=v3(ysq[:]), in0=py3, in1=st1[:, :, 0:1].to_broadcast([128, NCR, 64]), op=ALU.subtract),
                       r=[py, st1], w=[ysq])
                    op("dve", lambda e: e.tensor_tensor(out=yn[:], in0=v3(ysq[:]), in1=st1[:, :, 1:2].to_broadcast([128, NCR, 64]), op=ALU.mult),
                       r=[ysq, st1], w=[yn])
                    yield
                    pto = ps()
                    ptob = pto[:].bitcast(BF16)
                    for c in range(NCR):
                        for h in range(2):
                            hp = HP[h]
                            op("pe", lambda e, c=c, hp=hp: e.transpose(ptob[hp, c * 64:(c + 1) * 64], yn[hp, c, :], identb[hp, hp]), r=[yn, identb], w=[pto])
                    op("act", lambda e: e.activation(out=tA[:], in_=ptob[:, 0:STR], func=AF.Identity, scale=col("lng"), bias=col("lnb")), r=[pto, pcol], w=[tA])
                    op("dve", lambda e: e.tensor_tensor(out=tA[:], in0=tA[:], in1=bon[:], op=ALU.add), r=[tA, bon], w=[tA])
                    op("dve", lambda e: e.tensor_tensor(out=yT[:, p, ts], in0=tA[:], in1=g_t[:], op=ALU.mult), r=[tA, g_t], w=[yT.sub((p, st))])
                    yield
            return [stream(0), stream(1)]

        def glaret_unit(l, u, is_ret, es_u, tiles):
            def T(name, shape, dt):
                t = P.sb(name, shape, dt, es_u)
                tiles.append(t)
                return t
            yidx = 4 + (2 if is_ret else 0) + u
            base = 2304 if is_ret else 1536
            qc0 = base + 64 * u
            kc0 = base + 128 + 64 * u
            vc0 = base + 256 + 128 * u
            gc0 = base + 512 + 128 * u
            ncol = 768 if is_ret else 512
            Wt = T("Wt", [128, 8, ncol], BF16)
            rr = lambda c0, n: w_in[l, :, c0:c0 + n].rearrange("(kc q) n -> q kc n", q=128)
            op("dve", lambda e: e.memset(Wt[:, :, 0:256], 0.0), w=[Wt])
            if is_ret:
                op("dve", lambda e: e.memset(Wt[:, :, 512:768], 0.0), w=[Wt])
            for hh in range(2):
                dma("pool", Wt[:, :, 64 * hh:64 * hh + 32], rr(qc0 + 32 * hh, 32), w=[Wt])
                dma("pool", Wt[:, :, 128 + 64 * hh:128 + 64 * hh + 32], rr(kc0 + 32 * hh, 32), w=[Wt])
            dma("pool", Wt[:, :, 256:384], rr(vc0, 128), w=[Wt])
            dma("pool", Wt[:, :, 384:512], rr(gc0, 128), w=[Wt])
            if is_ret:
                for j, c0 in enumerate((qc0, kc0)):
                    for hh in range(2):
                        for jj in range(2):
                            dma("pool", Wt[:, :, 512 + 128 * j + 64 * hh + 16 * jj:512 + 128 * j + 64 * hh + 16 * jj + 16],
                                rr(c0 + 32 * hh + 16 * (1 - jj), 16), w=[Wt])
            for _ in range(GIDLE):
                yield
            f = lambda n: T(n, [128, ST], F32)
            tA, tB, g_fm = [f(n) for n in ("gtA", "gtB", "g_fm")]
            if not is_ret:
                e1, e2, e3 = [f(n) for n in ("ge1", "ge2", "ge3")]
            if is_ret:
                rC = T("rC", [128, ST], F32)
                rS = T("rS", [128, ST], F32)
            qa = T("qa", [128, ST], BF16)
            qb = T("qb", [128, ST], BF16)
            ka = T("ka", [128, ST], BF16)
            kb = T("kb", [128, ST], BF16)
            kend = T("kend", [128, ST], BF16)
            vbf = T("gvbf", [128, ST], BF16)
            kendT = T("kendT", [128, 8, 64], BF16)
            Vt = T("gVt", [128, 8, 64], BF16)
            P1 = T("P1", [128, 8, 64], BF16)
            P2 = T("P2", [128, 8, 64], BF16)
            Sball = T("Sball", [128, 8, 64], BF16)
            decc = T("decc", [128, 8], F32)
            ysq = T("gysq", [128, ST], F32)
            st2 = T("st2", [128, 8, 4], F32)
            yn = T("gyn", [128, 8, 64], BF16)
            Sst = (Sr if is_ret else Sg)
            ssl = Sst.sub((l, u))
            HP = (slice(0, 64), slice(64, 128))
            HD = HP

            def v3(ap):
                return ap.rearrange("q (c t) -> q c t", c=8)

            for st in range(SEG // ST):
                ts = slice(st * ST, (st + 1) * ST)
                cur = slice(1 + st * ST, 1 + (st + 1) * ST)
                hs = hslots()
                pq, pk, pv, pg = ps(), ps(), ps(), ps()
                for j, pp in enumerate((pq, pk, pv, pg)):
                    for kc in range(8):
                        op("pe", lambda e, kc=kc, j=j, pp=pp: e.matmul(pp[:], Wt[:, kc, j * 128:(j + 1) * 128], hT[:, kc, cur], start=(kc == 0), stop=(kc == 7)),
                           r=[Wt] + hs, w=[pp])
                op("act", lambda e: e.activation(out=vbf[:], in_=pv[:], func=AF.Copy), r=[pv], w=[vbf])
                op("act", lambda e: e.activation(out=g_fm[:], in_=pg[:], func=AF.Silu), r=[pg], w=[g_fm])
                chk("g_proj")
                if not is_ret:
                    pz = ps()
                    op("pe", lambda e: e.matmul(pz[:], up_gl[32:48, l, u, :], mids[2][32:48, ts], start=True, stop=True),
                       r=[up_gl, mids[2].sub(st)], w=[pz])
                    gb = pcol[:, poff[("glab", l)] + u:poff[("glab", l)] + u + 1]
                    op("act", lambda e: e.activation(out=tA[:], in_=pz[:], func=AF.Sigmoid, bias=gb, scale=1.0), r=[pz, pcol], w=[tA])
                    op("act", lambda e: e.activation(out=tA[:], in_=tA[:], func=AF.Ln), r=[tA], w=[tA])
                    op("dve", lambda e: e.tensor_tensor_scan(out=tB[:], data0=scanmask[:], data1=tA[:], initial=0.0, op0=ALU.mult, op1=ALU.add),
                       r=[scanmask, tA], w=[tB])
                    op("act", lambda e: e.activation(out=e1[:], in_=tB[:], func=AF.Exp, scale=1.0 / 16), r=[tB], w=[e1])
                    op("act", lambda e: e.activation(out=e2[:], in_=tB[:], func=AF.Exp, scale=-1.0 / 16), r=[tB], w=[e2])
                    op("dve", lambda e: e.tensor_tensor(out=v3(tA[:]), in0=v3(tB[:])[:, :, 63:64].to_broadcast([128, 8, 64]), in1=v3(tB[:]), op=ALU.subtract),
                       r=[tB], w=[tA])
                    op("act", lambda e: e.activation(out=e3[:], in_=tA[:], func=AF.Exp, scale=1.0 / 16), r=[tA], w=[e3])
                    op("dve", lambda e: e.tensor_copy(out=decc[:], in_=v3(e1[:])[:, :, 63]), r=[e1], w=[decc])
                    sc = 32 ** -0.5
                    op("dve", lambda e: e.scalar_tensor_tensor(out=qa[:], in0=pq[:], scalar=sc, in1=e1[:], op0=ALU.mult, op1=ALU.mult), r=[pq, e1], w=[qa])
                    op("dve", lambda e: e.scalar_tensor_tensor(out=qb[:], in0=pq[:], scalar=sc, in1=e2[:], op0=ALU.mult, op1=ALU.mult), r=[pq, e2], w=[qb])
                    op("dve", lambda e: e.tensor_tensor(out=ka[:], in0=pk[:], in1=e2[:], op=ALU.mult), r=[pk, e2], w=[ka])
                    op("dve", lambda e: e.tensor_tensor(out=kb[:], in0=pk[:], in1=e1[:], op=ALU.mult), r=[pk, e1], w=[kb])
                    op("dve", lambda e: e.tensor_tensor(out=kend[:], in0=pk[:], in1=e3[:], op=ALU.mult), r=[pk, e3], w=[kend])
                else:
                    pqs, pks = ps(), ps()
                    for j, pp in enumerate((pqs, pks)):
                        for kc in range(8):
                            op("pe", lambda e, kc=kc, j=j, pp=pp: e.matmul(pp[:], Wt[:, kc, 512 + j * 128:512 + (j + 1) * 128], hT[:, kc, cur], start=(kc == 0), stop=(kc == 7)),
                               r=[Wt] + hs, w=[pp])
                    g0 = seg_tok0[0] + st * ST
                    dma("sp", rC[:], cd["ropeC"][:, g0:g0 + ST], w=[rC])
                    dma("sp", rS[:], cd["ropeS"][:, g0:g0 + ST], w=[rS])
                    for (px, pxs, dst) in ((pq, pqs, qa), (pk, pks, ka)):
                        op("dve", lambda e, px=px: e.tensor_tensor(out=tA[:], in0=px[:], in1=rC[:], op=ALU.mult), r=[px, rC], w=[tA])
                        op("dve", lambda e, pxs=pxs: e.tensor_tensor(out=tB[:], in0=pxs[:], in1=rS[:], op=ALU.mult), r=[pxs, rS], w=[tB])
                        op("dve", lambda e, dst=dst: e.tensor_tensor(out=dst[:], in0=tA[:], in1=tB[:], op=ALU.add), r=[tA, tB], w=[dst])
                    op("dve", lambda e: e.tensor_tensor(out=v3(qb[:]), in0=v3(qa[:]), in1=qdec[:, 64 * u:64 * u + 64].unsqueeze(1).to_broadcast([128, 8, 64]), op=ALU.mult),
                       r=[qa, qdec], w=[qb])
                    op("dve", lambda e: e.tensor_tensor(out=v3(kend[:]), in0=v3(ka[:]), in1=kdec[:, 64 * u:64 * u + 64].unsqueeze(1).to_broadcast([128, 8, 64]), op=ALU.mult),
                       r=[ka, kdec], w=[kend])
                chk("g_dec")
                yield
                ptv = ps()
                ptvb = ptv[:].bitcast(BF16)
                for c in range(8):
                    cc = slice(c * 64, (c + 1) * 64)
                    for hh in range(2):
                        hp, hd = HP[hh], HD[hh]
                        op("pe", lambda e, c=c, hp=hp, cc=cc: e.transpose(ptvb[hp, c * 64:(c + 1) * 64], vbf[hp, cc], identb[hp, hp]), r=[vbf, identb], w=[ptv])
                        op("pe", lambda e, c=c, hp=hp, cc=cc: e.transpose(ptvb[hp, 512 + c * 64:512 + (c + 1) * 64], kend[hp, cc], identb[hp, hp]), r=[kend, identb], w=[ptv])
                op("act", lambda e: e.activation(out=Vt[:].rearrange("q a b -> q (a b)"), in_=ptvb[:, 0:512], func=AF.Copy), r=[ptv], w=[Vt])
                op("dve", lambda e: e.tensor_copy(out=kendT[:].rearrange("q a b -> q (a b)"), in_=ptvb[:, 512:1024]), r=[ptv], w=[kendT])
                chk("g_tr")
                yield
                p1 = ps()
                p2 = ps() if not is_ret else None
                for c in range(8):
                    cc = slice(c * 64, (c + 1) * 64)
                    for hh in range(2):
                        hp, hd = HP[hh], HD[hh]
                        op("pe", lambda e, hp=hp, hd=hd, cc=cc: e.matmul(p1[hp, cc], ka[hd, cc], qa[hd, cc], start=True, stop=True), r=[ka, qa], w=[p1])
                        if not is_ret:
                            op("pe", lambda e, hp=hp, hd=hd, cc=cc: e.matmul(p2[hp, cc], kb[hd, cc], qb[hd, cc], start=True, stop=True), r=[kb, qb], w=[p2])
                if not is_ret:
                    op("dve", lambda e: e.tensor_tensor(out=P1[:], in0=v3(p1[:]), in1=m_ap[:, 64:128].unsqueeze(1).to_broadcast([128, 8, 64]), op=ALU.mult), r=[p1, m_ap], w=[P1])
                    op("dve", lambda e: e.tensor_tensor(out=P2[:], in0=v3(p2[:]), in1=m_low[:].unsqueeze(1).to_broadcast([128, 8, 64]), op=ALU.mult), r=[p2, m_low], w=[P2])
                else:
                    op("dve", lambda e: e.tensor_tensor(out=P1[:], in0=v3(p1[:]), in1=retD[:, 64 * u:64 * u + 64].unsqueeze(1).to_broadcast([128, 8, 64]), op=ALU.mult),
                       r=[p1, retD], w=[P1])
                chk("g_sc")
                yield
                pkv = ps()
                for c in range(8):
                    for hh in range(2):
                        hp, hd = HP[hh], HD[hh]
                        op("pe", lambda e, c=c, hp=hp, hd=hd: e.matmul(pkv[hd, c * 64:(c + 1) * 64], kendT[hp, c, :], Vt[hp, c, :], start=True, stop=True), r=[kendT, Vt], w=[pkv])
                for c in range(8):
                    for hh in range(2):
                        hd = HD[hh]
                        op("act", lambda e, c=c, hd=hd: e.activation(out=Sball[hd, c, :], in_=Sst[hd, l, u, :], func=AF.Copy), r=[ssl], w=[Sball])
                        dsc = cdec[hd, u:u + 1] if is_ret else decc[hd, c:c + 1]
                        op("dve", lambda e, c=c, dsc=dsc, hd=hd: e.scalar_tensor_tensor(out=Sst[hd, l, u, :], in0=Sst[hd, l, u, :], scalar=dsc, in1=pkv[hd, c * 64:(c + 1) * 64],
                                                                                       op0=ALU.mult, op1=ALU.add), r=[ssl, pkv, decc, cdec], w=[ssl])
                chk("g_kv")
                yield
                po = ps()
                qi = qa if not is_ret else qb
                for c in range(8):
                    cc = slice(c * 64, (c + 1) * 64)
                    for hh in range(2):
                        hp, hd = HP[hh], HD[hh]
                        oo = po[hp, cc]
                        op("pe", lambda e, oo=oo, hp=hp, c=c: e.matmul(oo, P1[hp, c, :], Vt[hp, c, :], start=True, stop=False), r=[P1, Vt], w=[po])
                        if not is_ret:
                            op("pe", lambda e, oo=oo, hp=hp, c=c: e.matmul(oo, P2[hp, c, :], Vt[hp, c, :], start=False, stop=False), r=[P2, Vt], w=[po])
                        op("pe", lambda e, oo=oo, hd=hd, cc=cc, c=c: e.matmul(oo, qi[hd, cc], Sball[hd, c, :], start=False, stop=True), r=[qi, Sball], w=[po])
                chk("g_o")
                po3 = v3(po[:])
                if is_ret:
                    op("dve", lambda e: e.tensor_reduce(out=st2[:, :, 0], in_=po3, axis=AX.X, op=ALU.add), r=[po], w=[st2])
                    op("dve", lambda e: e.tensor_scalar(out=st2[:, :, 0], in0=st2[:, :, 0], scalar1=1.0 / 64, scalar2=None, op0=ALU.mult), r=[st2], w=[st2])
                    op("dve", lambda e: e.tensor_tensor(out=v3(tA[:]), in0=po3, in1=st2[:, :, 0:1].to_broadcast([128, 8, 64]), op=ALU.subtract), r=[po, st2], w=[tA])
                else:
                    op("act", lambda e: e.activation(out=tA[:], in_=po[:], func=AF.Copy), r=[po], w=[tA])
                op("act", lambda e: e.activation(out=ysq[:], in_=tA[:], func=AF.Square), r=[tA], w=[ysq])
                op("dve", lambda e: e.tensor_reduce(out=st2[:, :, 1], in_=v3(ysq[:]), axis=AX.X, op=ALU.add), r=[ysq], w=[st2])
                op("act", lambda e: e.activation(out=st2[:, :, 1], in_=st2[:, :, 1], func=AF.Ln, bias=epsc[:, 0:1], scale=1.0 / 64), r=[st2, epsc], w=[st2])
                op("act", lambda e: e.activation(out=st2[:, :, 1], in_=st2[:, :, 1], func=AF.Exp, scale=-0.5), r=[st2], w=[st2])
                op("dve", lambda e: e.tensor_tensor(out=yn[:], in0=v3(tA[:]), in1=st2[:, :, 1:2].to_broadcast([128, 8, 64]), op=ALU.mult), r=[tA, st2], w=[yn])
                chk("g_norm")
                yield
                pto = ps()
                ptob = pto[:].bitcast(BF16)
                for c in range(8):
                    for hh in range(2):
                        hp = HP[hh]
                        op("pe", lambda e, c=c, hp=hp: e.transpose(ptob[hp, c * 64:(c + 1) * 64], yn[hp, c, :], identb[hp, hp]), r=[yn, identb], w=[pto])
                if not is_ret:
                    gl = pcol[:, poff[("glng", l)]:poff[("glng", l)] + 1]
                    op("dve", lambda e: e.scalar_tensor_tensor(out=yT[:, yidx, ts], in0=ptob[:, 0:512], scalar=gl, in1=g_fm[:], op0=ALU.mult, op1=ALU.mult),
                       r=[pto, pcol, g_fm], w=[yT.sub((yidx, st))])
                else:
                    op("dve", lambda e: e.tensor_tensor(out=yT[:, yidx, ts], in0=ptob[:, 0:512], in1=g_fm[:], op=ALU.mult), r=[pto, g_fm], w=[yT.sub((yidx, st))])
                yield

        seg_tok0 = [0]
        out_slots = []

        def chk(tag):
            if stop == tag:
                P.stopped = True
        try:
          chk("setup")
          for seg in range(nseg):
              t0 = seg * SEG
              seg_tok0[0] = t0
              with ExitStack() as s1:
                  xst = [P.sb("xst%d" % i, [128, D], F32, s1) for i in range(2)]
                  for tt in range(SEG // 128):
                      xs = xst[tt % 2]
                      dma("sp", xs[:], x_d[t0 + tt * 128:t0 + (tt + 1) * 128, :], w=[xs])
                      for half in range(2):
                          pp = ps()
                          for j in range(4):
                              c = half * 4 + j
                              op("pe", lambda e, c=c, j=j, xs=xs, pp=pp: e.transpose(pp[:, j * 128:(j + 1) * 128], xs[:, c * 128:(c + 1) * 128], identf[:]),
                                 r=[xs, identf], w=[pp])
                          eng = "act" if half == 0 else "dve"
                          if eng == "act":
                              op("act", lambda e, half=half, tt=tt, pp=pp: e.activation(out=xT[:, half * 4:(half + 1) * 4, tt * 128:(tt + 1) * 128],
                                                                                        in_=pp[:].rearrange("q (c t) -> q c t", c=4), func=AF.Copy), r=[pp], w=[xT])
                          else:
                              op("dve", lambda e, half=half, tt=tt, pp=pp: e.tensor_copy(out=xT[:, half * 4:(half + 1) * 4, tt * 128:(tt + 1) * 128],
                                                                                         in_=pp[:].rearrange("q (c t) -> q c t", c=4)), r=[pp], w=[xT])
                  P.add_fence(xst)
              chk("xload")

              for l in range(nlayer):
                  with ExitStack() as s2:
                      if seg == 0:
                          op("dve", lambda e: e.memset(hT[:, :, 0:1], 0.0), w=[hT.sub("c0")])
                      else:
                          op("dve", lambda e: e.tensor_copy(out=hT[:, :, 0:1], in_=hcar[:, l, :].unsqueeze(2)), r=[hcar.sub(l)], w=[hT.sub("c0")])
                      tl = rmsnorm_to_hT(drv[:, l, 0:8], mod[:, l, 0:8], s2)
                      op("dve", lambda e: e.tensor_copy(out=hcar[:, l, :].unsqueeze(2), in_=hT[:, :, SEG:SEG + 1]), r=hslots(), w=[hcar.sub(l)])
                      P.add_fence(tl)
                  chk("norm1")
                  s2b = ExitStack()
                  lw = P.sb("lw", [128, 2, 8, 304], BF16, s2b)
                  dma("sp", lw[:].rearrange("q a k n -> q (a k n)"), lw_d[l], r=[lw_slots[l]], w=[lw])
                  for st in range(SEG // ST):
                      cur = slice(1 + st * ST, 1 + (st + 1) * ST)
                      prv = slice(st * ST, (st + 1) * ST)
                      ts = slice(st * ST, (st + 1) * ST)
                      hs = hslots()
                      for ci, (c0, c1) in enumerate(((0, 128), (128, 256), (256, 304))):
                          m = c1 - c0
                          pp = ps()
                          for kc in range(8):
                              op("pe", lambda e, kc=kc, pp=pp, c0=c0, c1=c1, m=m: e.matmul(pp[0:m, :], lw[:, 0, kc, c0:c1], hT[:, kc, cur], start=(kc == 0), stop=False),
                                 r=[lw] + hs, w=[pp])
                              op("pe", lambda e, kc=kc, pp=pp, c0=c0, c1=c1, m=m: e.matmul(pp[0:m, :], lw[:, 1, kc, c0:c1], hT[:, kc, prv], start=False, stop=(kc == 7)),
                                 r=[lw] + hs, w=[pp])
                          if ci == 0:
                              op("act", lambda e, pp=pp: e.activation(out=mids[0][0:64, ts], in_=pp[0:64, :], func=AF.Tanh), r=[pp], w=[mids[0].sub(st)])
                              op("act", lambda e, pp=pp: e.activation(out=mids[0][64:128, ts], in_=pp[64:128, :], func=AF.Copy), r=[pp], w=[mids[0].sub(st)])
                          elif ci == 1:
                              op("act", lambda e, pp=pp: e.activation(out=mids[1][:, ts], in_=pp[:], func=AF.Sigmoid), r=[pp], w=[mids[1].sub(st)])
                          else:
                              op("act", lambda e, pp=pp: e.activation(out=mids[2][0:48, ts], in_=pp[0:48, :], func=AF.Copy), r=[pp], w=[mids[2].sub(st)])
                  P.add_fence([lw])
                  s2b.close()
                  chk("lora")
                  if MIXMODE == 2:
                      for p in (0, 2):
                          with ExitStack() as s3:
                              tl_ = []
                              run_streams(rwkv_pair(l, p, s3, tl_) + rwkv_pair(l, p + 1, s3, tl_))
                              P.add_fence(tl_)
                      for is_ret in (False, True):
                          with ExitStack() as s3:
                              tl_ = []
                              run_streams([glaret_unit(l, u, is_ret, s3, tl_) for u in range(2)])
                              P.add_fence(tl_)
                  elif MIXMODE == 1:
                      for p in range(4):
                          with ExitStack() as s3:
                              tl_ = []
                              gens = rwkv_pair(l, p, s3, tl_) + [glaret_unit(l, p % 2, p >= 2, s3, tl_)]
                              run_streams(gens, periods=[1, 1, GPER])
                              P.add_fence(tl_)
                          chk("rwkv%d" % p)
                  else:
                      for p in range(4):
                          with ExitStack() as s3:
                              tl_ = []
                              run_streams(rwkv_pair(l, p, s3, tl_))
                              P.add_fence(tl_)
                          chk("rwkv%d" % p)
                      for is_ret in (False, True):
                          with ExitStack() as s3:
                              tl_ = []
                              run_streams([glaret_unit(l, u, is_ret, s3, tl_) for u in range(2)])
                              P.add_fence(tl_)
                  with ExitStack() as s6:
                      ngrp = (NFF + FFG - 1) // FFG
                      wgu = [P.sb("wgu%d" % i, [128, 8, 2, FFG * 128], BF16, s6) for i in range(2)]
                      wdn = [P.sb("wdn%d" % i, [128, FFG, D], BF16, s6) for i in range(2)]
                      sg = [P.sb("sg%d" % i, [128, ST], F32, s6) for i in range(2)]
                      aT = [P.sb("aT%d" % i, [128, FFG, ST], BF16, s6) for i in range(2)]
                      ai = 0

                      def ffn_load(g):
                          f0 = g * FFG
                          nf = min(FFG, NFF - f0)
                          wg_t, wd_t = wgu[g % 2], wdn[g % 2]
                          dma("pool", wg_t[:, :, 0, 0:nf * 128], wg_d[l, :, f0 * 128:(f0 + nf) * 128].rearrange("(kc q) n -> q kc n", q=128), w=[wg_t])
                          dma("pool", wg_t[:, :, 1, 0:nf * 128], wu_d[l, :, f0 * 128:(f0 + nf) * 128].rearrange("(kc q) n -> q kc n", q=128), w=[wg_t])
                          dma("pool", wd_t[:, 0:nf, :], wd_d[l, f0 * 128:(f0 + nf) * 128, :].rearrange("(f q) n -> q f n", q=128), w=[wd_t])
                      ffn_load(0)
                      ffn_load(1)
                      with ExitStack() as s4:
                          wo = P.sb("wo", [128, 8, D], BF16, s4)
                          dma("sp", wo[:].rearrange("q k n -> q (k n)"), wo_d[l], r=[wo_slots[l]], w=[wo])
                          ysl = yT.all_slots()
                          for st in range(SEG // ST):
                              ts = slice(st * ST, (st + 1) * ST)
                              for c in range(8):
                                  pp = ps()
                                  for kc in range(8):
                                      op("pe", lambda e, kc=kc, c=c, pp=pp: e.matmul(pp[:], wo[:, kc, c * 128:(c + 1) * 128], yT[:, kc, ts], start=(kc == 0), stop=(kc == 7)),
                                         r=[wo] + ysl, w=[pp])
                                  op("dve", lambda e, c=c, pp=pp: e.scalar_tensor_tensor(out=xT[:, c, ts], in0=pp[:], scalar=mod[:, l, 16 + c:17 + c], in1=xT[:, c, ts],
                                                                                         op0=ALU.mult, op1=ALU.add), r=[pp, mod, xT], w=[xT])
                          P.add_fence([wo])
                      chk("wout")
                      with ExitStack() as s5:
                          tl = rmsnorm_to_hT(drv[:, l, 8:16], mod[:, l, 24:32], s5)
                          P.add_fence(tl)
                      hs = hslots()
                      for g in range(ngrp):
                          f0 = g * FFG
                          nf = min(FFG, NFF - f0)
                          wg_t, wd_t = wgu[g % 2], wdn[g % 2]
                          if g >= 2:
                              ffn_load(g)
                          for st in range(SEG // ST):
                              tok = slice(1 + st * ST, 1 + (st + 1) * ST)
                              tx = slice(st * ST, (st + 1) * ST)
                              at = aT[ai % 2]
                              ai += 1
                              for fi in range(nf):
                                  pg_, pu_ = ps(), ps()
                                  for kc in range(8):
                                      op("pe", lambda e, kc=kc, fi=fi, pg_=pg_: e.matmul(pg_[:], wg_t[:, kc, 0, fi * 128:(fi + 1) * 128], hT[:, kc, tok],
                                                                                        start=(kc == 0), stop=(kc == 7)), r=[wg_t] + hs, w=[pg_])
                                  for kc in range(8):
                                      op("pe", lambda e, kc=kc, fi=fi, pu_=pu_: e.matmul(pu_[:], wg_t[:, kc, 1, fi * 128:(fi + 1) * 128], hT[:, kc, tok],
                                                                                        start=(kc == 0), stop=(kc == 7)), r=[wg_t] + hs, w=[pu_])
                                  sgt = sg[fi % 2]
                                  op("act", lambda e, sgt=sgt, pg_=pg_: e.activation(out=sgt[:], in_=pg_[:], func=AF.Silu), r=[pg_], w=[sgt])
                                  op("dve", lambda e, sgt=sgt, pu_=pu_, at=at, fi=fi: e.tensor_tensor(out=at[:, fi, :], in0=pu_[:], in1=sgt[:], op=ALU.mult),
                                     r=[pu_, sgt], w=[at.sub(fi)])
                              for c in range(8):
                                  pd = ps()
                                  for fi in range(nf):
                                      op("pe", lambda e, c=c, fi=fi, at=at, pd=pd: e.matmul(pd[:], wd_t[:, fi, c * 128:(c + 1) * 128], at[:, fi, :],
                                                                                           start=(fi == 0), stop=(fi == nf - 1)), r=[wd_t, at.sub(fi)], w=[pd])
                                  op("dve", lambda e, c=c, pd=pd: e.scalar_tensor_tensor(out=xT[:, c, tx], in0=pd[:], scalar=mod[:, l, 40 + c:41 + c], in1=xT[:, c, tx],
                                                                                         op0=ALU.mult, op1=ALU.add), r=[pd, mod, xT], w=[xT])
                      P.add_fence(wgu + wdn + sg + aT)

              with ExitStack() as s7:
                  sqb = P.sb("fsqb", [128, 8, ST], BF16, s7)
                  rstd = P.sb("frstd", [128, ST], F32, s7)
                  tfall = P.sb("tfall", [128, 8, ST], F32, s7)
                  ost = [P.sb("ost%d" % i, [128, D], F32, s7) for i in range(2)]
                  gmf = pc(("nfg",), 8)
                  oi = 0
                  for st in range(SEG // ST):
                      ts = slice(st * ST, (st + 1) * ST)
                      mps = ps()
                      for c in range(8):
                          op("act", lambda e, c=c: e.activation(out=sqb[:, c, :], in_=xT[:, c, ts], func=AF.Square), r=[xT], w=[sqb.sub(c)])
                          op("pe", lambda e, c=c: e.matmul(mps[:], ones_mean[:], sqb[:, c, :], start=(c == 0), stop=(c == 7)),
                             r=[ones_mean, sqb.sub(c)], w=[mps])
                      op("act", lambda e: e.activation(out=rstd[:], in_=mps[:], func=AF.Ln, bias=epsc[:, 0:1], scale=1.0), r=[mps, epsc], w=[rstd])
                      op("act", lambda e: e.activation(out=rstd[:], in_=rstd[:], func=AF.Exp, scale=-0.5), r=[rstd], w=[rstd])
                      for c in range(8):
                          op("dve", lambda e, c=c: e.scalar_tensor_tensor(out=tfall[:, c, :], in0=xT[:, c, ts], scalar=gmf[:, c:c + 1], in1=rstd[:],
                                                                          op0=ALU.mult, op1=ALU.mult), r=[xT, rstd, pcol], w=[tfall])
                      for tt in range(ST // 128):
                          o_t = ost[oi % 2]
                          oi += 1
                          for half in range(2):
                              pp = ps()
                              for j in range(4):
                                  c = half * 4 + j
                                  op("pe", lambda e, c=c, j=j, pp=pp, tt=tt: e.transpose(pp[:, j * 128:(j + 1) * 128], tfall[:, c, tt * 128:(tt + 1) * 128], identf[:]),
                                     r=[tfall, identf], w=[pp])
                              op("act", lambda e, half=half, o_t=o_t, pp=pp: e.activation(out=o_t[:, half * 512:(half + 1) * 512], in_=pp[:], func=AF.Copy), r=[pp], w=[o_t])
                          r0 = t0 + st * ST + tt * 128
                          dma("sp", out_d[r0:r0 + 128, :], o_t[:], r=[o_t])
                  P.add_fence([sqb, rstd, tfall] + ost)

        except _Stop:
            pass
        P.stopped = False
        if dbg_d is not None:
            src = {"yT": yT, "hT": hT, "xT": xT, "vfirst": vfirst, "mod": mod, "midA": mids[0], "midG": mids[1], "midC": mids[2],
                   "Mf": Mf, "Sg": Sg, "Sr": Sr, "drv": drv}[dbg[0]]
            ap = src[:]
            if len(ap.shape) == 3:
                ap = ap.rearrange("q a b -> q (a b)")
            elif len(ap.shape) == 4:
                ap = ap.rearrange("q a b c -> q (a b c)")
            npart = ap.shape[0]
            dma("pool", dbg_d[0:npart, 0:ap.shape[1]], ap, r=src.all_slots())
        sp = P.E["sp"]
        for so in P.dsems:
            if so.count > 0:
                sp.e.wait_ge(so.h, so.count)
    return nc


_CACHE = {}


def _in_map(inputs, b, consts):
    m = {
        "x": np.ascontiguousarray(inputs["x"][b], dtype=np.float32),
        "pcol": _build_pcol(inputs, b),
    }
    for k in ("ada_w", "w_in", "w_out", "rk_mu_rkv", "rk_w1", "rk_a1", "rk_g1", "rk_v1", "gla_a1", "rk_w2", "rk_a2", "rk_g2",
              "rk_v2", "gla_a2", "gla_ln_g", "ffn_w_gate", "ffn_w_up", "ffn_w_down"):
        m[k] = np.ascontiguousarray(inputs[k], dtype=np.float32)
    for k, v in consts.items():
        m["c_" + k] = v
    return m


def kernel(**inputs):
    inputs = {k: np.asarray(v) for k, v in inputs.items()}
    nb = inputs["x"].shape[0]
    if "nc" not in _CACHE:
        _CACHE["nc"] = build_program()
    nc = _CACHE["nc"]
    consts = _consts()
    in_maps = [_in_map(inputs, b, consts) for b in range(nb)]
    res = run_bass_kernel_spmd(nc, in_maps, core_ids=list(range(nb)))
    out = np.stack([np.asarray(res.results[b]["out"], dtype=np.float32) for b in range(nb)], axis=0)
    return out
```

```python
import os
import numpy as np
from contextlib import ExitStack
import concourse.bass as bass
import concourse.mybir as mybir
from concourse.bass_utils import run_bass_kernel_spmd

F32 = mybir.dt.float32
BF16 = mybir.dt.bfloat16
AF = mybir.ActivationFunctionType
ALU = mybir.AluOpType
AX = mybir.AxisListType

D = 1024
SEQ = 4096
DEPTH = 2
DFF = 2816
NFF = DFF // 128
SEG = 1024
ST = 512
CH = 64
EPS = 1e-6
GN_EPS = 64e-5
CDEC = float(np.exp(-0.5))
FFG = 6
GPER = int(os.environ.get('GPER', '1'))
MIXMODE = int(os.environ.get('MIXMODE', '1'))
GIDLE = int(os.environ.get('GIDLE', '32'))
POOLENG = os.environ.get('POOLENG', 'dve')


class SemObj:
    __slots__ = ("h", "count")

    def __init__(self, h):
        self.h = h
        self.count = 0


class Slot:
    __slots__ = ("w", "r", "excl")

    def __init__(self, fence=None):
        self.w = None
        self.r = dict(fence) if fence else {}
        self.excl = False


class Tile:
    def __init__(self, t, fence=None):
        self.t = t
        self.s = Slot(fence)
        self._subs = {}
        self._fence = fence

    def sub(self, key):
        if key not in self._subs:
            self._subs[key] = Slot(self._fence)
        return self._subs[key]

    def all_slots(self):
        return [self.s] + list(self._subs.values())

    def __getitem__(self, k):
        return self.t[k]


class Eng:
    def __init__(self, e, sem, inorder=False):
        self.e = e
        self.sem = sem
        self.known = {}
        self.inorder = inorder


class Prog:
    def __init__(self, nc, es, n_dma_sems=24):
        self.nc = nc
        self.es = es
        mk = lambda n: SemObj(es.enter_context(nc.semaphore(n)))
        self.E = {
            "pe": Eng(nc.tensor, mk("s_pe"), inorder=True),
            "act": Eng(nc.scalar, mk("s_act")),
            "dve": Eng(nc.vector, mk("s_dve")),
            "pool": Eng(nc.gpsimd, mk("s_pool")),
            "sp": Eng(nc.sync, mk("s_sp")),
        }
        self.dsems = [mk("s_dma%d" % i) for i in range(n_dma_sems)]
        self.drr = 0
        self.fence = {}
        self.ninst = 0

    def sb(self, name, shape, dt, es=None):
        es = es or self.es
        self.nsb = getattr(self, "nsb", 0) + 1
        t = es.enter_context(self.nc.sbuf_tensor("sb%d_%s" % (self.nsb, name), list(shape), dt))
        return Tile(t, self.fence)

    def add_fence(self, tiles):
        f = dict(self.fence)
        for tl in tiles:
            for s in tl.all_slots():
                if s.w is not None:
                    f[s.w[0]] = max(f.get(s.w[0], 0), s.w[1])
                for so, v in s.r.items():
                    f[so] = max(f.get(so, 0), v)
        self.fence = f

    def _collect(self, reads, writes):
        need = {}
        for s in reads:
            if s.w is not None:
                need[s.w[0]] = max(need.get(s.w[0], 0), s.w[1])
            if s.excl:
                for so, v in s.r.items():
                    need[so] = max(need.get(so, 0), v)
        for s in writes:
            if s.w is not None:
                need[s.w[0]] = max(need.get(s.w[0], 0), s.w[1])
            for so, v in s.r.items():
                need[so] = max(need.get(so, 0), v)
        return need

    def _waits(self, E, need):
        for so, v in need.items():
            if v <= 0:
                continue
            if so is E.sem and E.inorder:
                continue
            if E.known.get(so, 0) < v:
                E.e.wait_ge(so.h, v)
                E.known[so] = v

    @staticmethod
    def _slots(xs):
        return [x.s if isinstance(x, Tile) else x for x in xs]

    stopped = False

    def op(self, eng, fn, r=(), w=()):
        if self.stopped:
            return None
        E = self.E[eng]
        r = self._slots(r)
        w = self._slots(w)
        self._waits(E, self._collect(r, w))
        inst = fn(E.e)
        E.sem.count += 1
        inst.then_inc(E.sem.h, 1)
        v = E.sem.count
        for s in r:
            s.r[E.sem] = v
        for s in w:
            s.w = (E.sem, v)
            s.r = {}
        self.ninst += 1
        return inst

    def dma(self, eng, out, in_, r=(), w=()):
        if self.stopped:
            return None
        E = self.E[eng]
        r = self._slots(r)
        w = self._slots(w)
        so = self.dsems[self.drr]
        self.drr = (self.drr + 1) % len(self.dsems)
        need = self._collect(r, w)
        if so.count > 0:
            need[so] = max(need.get(so, 0), so.count)
        self._waits(E, need)
        inst = E.e.dma_start(out=out, in_=in_)
        so.count += 16
        inst.then_inc(so.h, 16)
        for s in r:
            s.r[so] = so.count
        for s in w:
            s.w = (so, so.count)
            s.r = {}
        self.ninst += 1
        return inst


def _consts():
    c = {}
    c["ident"] = np.eye(128, dtype=np.float32)
    bd = np.zeros((128, 128), np.float32)
    bd[:64, :64] = 1.0
    bd[64:, 64:] = 1.0
    c["ones_bd"] = bd
    c["ones_mean"] = np.full((128, 128), 1.0 / D, np.float32)
    s = np.arange(64)[:, None]
    t = np.arange(64)[None, :]
    m_ap = np.concatenate([(s < t), (s <= t)], axis=1).astype(np.float32)
    c["m_ap"] = np.concatenate([m_ap, m_ap], axis=0)
    m_low = (t < s).astype(np.float32)
    c["m_low"] = np.concatenate([m_low, m_low], axis=0)
    ge = (t >= s).astype(np.float32)
    lt = (t < s).astype(np.float32)
    ge2 = np.concatenate([ge, ge], axis=1)
    lt2 = np.concatenate([lt, lt], axis=1)
    c["m_ge"] = np.concatenate([ge2, ge2], axis=0)
    c["m_lt"] = np.concatenate([lt2, lt2], axis=0)
    sm = np.ones((128, ST), np.float32)
    sm[:, ::CH] = 0.0
    c["scanmask"] = sm
    lg = np.log1p(-(2.0 ** (-5.0 - np.arange(4, dtype=np.float64))))
    pos = np.arange(64, dtype=np.float64)
    dk = 32
    retD = np.zeros((128, 2, 64), np.float64)
    qdec = np.zeros((128, 2, 64), np.float64)
    kdec = np.zeros((128, 2, 64), np.float64)
    cdec = np.zeros((128, 2), np.float64)
    for u in range(2):
        for hp in range(2):
            h = 2 * u + hp
            dm = np.exp(lg[h] * np.abs(pos[:, None] - pos[None, :])) * dk ** -0.5
            retD[64 * hp:64 * hp + 64, u, :] = dm
            qdec[64 * hp:64 * hp + 32, u, :] = (np.exp(lg[h] * (pos + 1.0)) * dk ** -0.5)[None, :]
            kdec[64 * hp:64 * hp + 32, u, :] = np.exp(lg[h] * (63.0 - pos))[None, :]
            cdec[64 * hp:64 * hp + 32, u] = np.exp(lg[h] * 64.0)
    c["retD"] = retD.reshape(128, 128).astype(np.float32)
    c["qdec"] = qdec.reshape(128, 128).astype(np.float32)
    c["kdec"] = kdec.reshape(128, 128).astype(np.float32)
    c["cdec"] = cdec.astype(np.float32)
    half = 16
    inv_freq = (10000.0 ** (-np.arange(half, dtype=np.float32) / half)).astype(np.float32)
    ang = np.arange(SEQ, dtype=np.float32)[:, None] * inv_freq[None, :]
    cos = np.cos(ang.astype(np.float64)).T
    sin = np.sin(ang.astype(np.float64)).T
    ropeC = np.zeros((128, SEQ), np.float64)
    ropeS = np.zeros((128, SEQ), np.float64)
    for hp in range(2):
        for j in range(2):
            rows = slice(64 * hp + 16 * j, 64 * hp + 16 * j + 16)
            ropeC[rows] = cos
            ropeS[rows] = -sin if j == 0 else sin
    c["ropeC"] = ropeC.astype(np.float32)
    c["ropeS"] = ropeS.astype(np.float32)
    return c


def _pcol_layout():
    off = {}
    n = 0

    def add(name, k):
        nonlocal n
        off[name] = n
        n += k
    for l in range(DEPTH):
        add(("n1g", l), 8)
        add(("n2g", l), 8)
        for j in range(3):
            add(("mux", l, j), 8)
        add(("muv", l), 8)
        for nm in ("w0", "a0", "kk", "ka", "rk", "lng", "lnb", "v0"):
            add((nm, l), 4)
        add(("glab", l), 2)
        add(("glng", l), 1)
        add(("adab", l), 48)
    add(("nfg",), 8)
    add(("cT",), 8)
    return off, n


def _col8(v):
    return np.ascontiguousarray(np.asarray(v, np.float32).reshape(8, 128).T)


def _col4(v):
    return np.ascontiguousarray(np.asarray(v, np.float32).reshape(4, 128).T)


def _build_pcol(inp, b):
    off, n = _pcol_layout()
    t = np.zeros((128, n), np.float32)
    for l in range(DEPTH):
        t[:, off[("n1g", l)]:off[("n1g", l)] + 8] = _col8(inp["norm1_g"][l])
        t[:, off[("n2g", l)]:off[("n2g", l)] + 8] = _col8(inp["norm2_g"][l])
        for j in range(3):
            t[:, off[("mux", l, j)]:off[("mux", l, j)] + 8] = _col8(inp["rk_mu_x"][l, j])
        if l >= 1:
            t[:, off[("muv", l)]:off[("muv", l)] + 8] = _col8(inp["rk_mu_v"][l - 1])
            t[:, off[("v0", l)]:off[("v0", l)] + 4] = _col4(inp["rk_v0"][l - 1])
        for nm, key in (("w0", "rk_w0"), ("a0", "rk_a0"), ("kk", "rk_k_k"), ("ka", "rk_k_a"),
                        ("lng", "rk_ln_g"), ("lnb", "rk_ln_b")):
            t[:, off[(nm, l)]:off[(nm, l)] + 4] = _col4(inp[key][l])
        t[:, off[("rk", l)]:off[("rk", l)] + 4] = _col4(np.asarray(inp["rk_r_k"][l]).reshape(512))
        gab = np.asarray(inp["gla_ab"][l], np.float32)
        for u in range(2):
            for hh in range(2):
                t[64 * hh:64 * hh + 32, off[("glab", l)] + u] = gab[64 * u + 32 * hh:64 * u + 32 * hh + 32]
        lg_ = np.asarray(inp["gla_ln_g"][l], np.float32)
        t[:, off[("glng", l)]] = np.concatenate([lg_, lg_])
        t[:, off[("adab", l)]:off[("adab", l)] + 48] = np.asarray(inp["ada_b"][l], np.float32).reshape(48, 128).T
    t[:, off[("nfg",)]:off[("nfg",)] + 8] = _col8(inp["norm_f_g"])
    t[:, off[("cT",)]:off[("cT",)] + 8] = _col8(inp["c"][b])
    return t


class _Stop(Exception):
    pass


def build_program(nseg=SEQ // SEG, nlayer=DEPTH, dbg=None, stop=None):
    nc = bass.Bass("TRN2", target_bir_lowering=False)
    ntok = nseg * SEG
    poff, pn = _pcol_layout()
    cst = _consts()

    def din(name, shape):
        return nc.dram_tensor(name, list(shape), F32, kind="ExternalInput").ap()

    x_d = din("x", [SEQ, D])
    pcol_d = din("pcol", [128, pn])
    ada_w = din("ada_w", [DEPTH, D, 6 * D])
    w_in = din("w_in", [DEPTH, D, 3072])
    w_out = din("w_out", [DEPTH, D, D])
    mu_rkv = din("rk_mu_rkv", [DEPTH, 3, 512])
    rk_w1 = din("rk_w1", [DEPTH, D, 64])
    rk_a1 = din("rk_a1", [DEPTH, D, 64])
    rk_g1 = din("rk_g1", [DEPTH, D, 128])
    rk_v1 = din("rk_v1", [DEPTH - 1, D, 32])
    gla_a1 = din("gla_a1", [DEPTH, D, 16])
    rk_w2 = din("rk_w2", [DEPTH, 64, 512])
    rk_a2 = din("rk_a2", [DEPTH, 64, 512])
    rk_g2 = din("rk_g2", [DEPTH, 128, 512])
    rk_v2 = din("rk_v2", [DEPTH - 1, 32, 512])
    gla_a2 = din("gla_a2", [DEPTH, 16, 128])
    gla_lng = din("gla_ln_g", [DEPTH, 64])
    wg_d = din("ffn_w_gate", [DEPTH, D, DFF])
    wu_d = din("ffn_w_up", [DEPTH, D, DFF])
    wd_d = din("ffn_w_down", [DEPTH, DFF, D])
    cd = {k: din("c_" + k, v.shape) for k, v in cst.items()}
    out_d = nc.dram_tensor("out", [SEQ, D], F32, kind="ExternalOutput").ap()
    wab_d = nc.dram_tensor("wab_scr", [DEPTH * 4, 128, 8 * 768], BF16, kind="Internal").ap()
    lw_d = nc.dram_tensor("lw_scr", [DEPTH, 128, 2 * 8 * 304], BF16, kind="Internal").ap()
    wo_d = nc.dram_tensor("wo_scr", [DEPTH, 128, 8 * D], BF16, kind="Internal").ap()
    dbg_d = None
    if dbg is not None:
        dbg_d = nc.dram_tensor("dbg", [128, dbg[1]], F32, kind="ExternalOutput").ap()

    es = ExitStack()
    with es:
        P = Prog(nc, es)
        op, dma = P.op, P.dma

        psb = [Tile(es.enter_context(nc.psum_tensor("ps%d" % i, [128, 512], F32))) for i in range(8)]
        for t_ in psb:
            t_.s.excl = True
        prr = [0]

        held = set()

        def ps(hold=False):
            for _ in range(16):
                i = prr[0]
                prr[0] = (prr[0] + 1) % 8
                if i not in held:
                    if hold:
                        held.add(i)
                    return psb[i]
            raise RuntimeError("no free psum bank")

        def release(t):
            held.discard(psb.index(t))

        xT = P.sb("xT", [128, 8, SEG], F32)
        hT = P.sb("hT", [128, 8, SEG + 1], BF16)
        yT = P.sb("yT", [128, 8, SEG], BF16)
        vfirst = P.sb("vfirst", [128, 4, SEG], BF16)
        pcol = P.sb("pcol", [128, pn], F32)
        drv = P.sb("drv", [128, DEPTH, 64], F32)
        mod = P.sb("mod", [128, DEPTH, 48], F32)
        identf = P.sb("identf", [128, 128], F32)
        identb = P.sb("identb", [128, 128], BF16)
        ones_bd = P.sb("ones_bd", [128, 128], BF16)
        ones_mean = P.sb("ones_mean", [128, 128], BF16)
        m_ap = P.sb("m_ap", [128, 128], F32)
        m_low = P.sb("m_low", [128, 64], F32)
        m_ge = P.sb("m_ge", [128, 128], F32)
        m_lt = P.sb("m_lt", [128, 128], F32)
        scanmask = P.sb("scanmask", [128, ST], F32)
        retD = P.sb("retD", [128, 128], F32)
        qdec = P.sb("qdec", [128, 128], F32)
        kdec = P.sb("kdec", [128, 128], F32)
        cdec = P.sb("cdec", [128, 2], F32)
        up_gl = P.sb("up_gl", [48, DEPTH, 2, 128], BF16)
        epsc = P.sb("epsc", [128, 4], F32)
        glng = P.sb("glng", [128, DEPTH, 64], F32)
        wab_slots = [Slot() for _ in range(DEPTH * 4)]
        lw_slots = [Slot() for _ in range(DEPTH)]
        wo_slots = [Slot() for _ in range(DEPTH)]
        up_wa = P.sb("up_wa", [128, DEPTH, 512], BF16)
        up_g = P.sb("up_g", [128, DEPTH, 512], BF16)
        up_c = P.sb("up_c", [48, DEPTH, 512], BF16)
        Mf = P.sb("Mf", [128, DEPTH, 4, 64], F32)
        Mb = P.sb("Mb", [128, DEPTH, 4, 64], BF16)
        Sg = P.sb("Sg", [128, DEPTH, 2, 64], F32)
        Sr = P.sb("Sr", [128, DEPTH, 2, 64], F32)
        hcar = P.sb("hcar", [128, DEPTH, 8], BF16)
        mids = [P.sb("midA", [128, SEG], BF16), P.sb("midG", [128, SEG], BF16), P.sb("midC", [48, SEG], BF16)]

        def pc(key, k=1, j=0):
            o = poff[key] + j
            return pcol[:, o:o + k]

        dma("sp", pcol[:], pcol_d, w=[pcol])
        dma("sp", identf[:], cd["ident"], w=[identf])
        dma("pool", identb[:], cd["ident"], w=[identb])
        dma("pool", ones_bd[:], cd["ones_bd"], w=[ones_bd])
        dma("pool", ones_mean[:], cd["ones_mean"], w=[ones_mean])
        for tl, nm in ((m_ap, "m_ap"), (m_low, "m_low"), (m_ge, "m_ge"), (m_lt, "m_lt"), (scanmask, "scanmask"),
                       (retD, "retD"), (qdec, "qdec"), (kdec, "kdec"), (cdec, "cdec")):
            dma("sp", tl[:], cd[nm], w=[tl])
        for l in range(DEPTH):
            dma("sp", glng[:, l, :], gla_lng[l:l + 1, :].partition_broadcast(128), w=[glng])
        op("dve", lambda e: e.memset(epsc[:, 0:1], EPS), w=[epsc])
        op("dve", lambda e: e.memset(epsc[:, 1:2], GN_EPS), w=[epsc])
        op("dve", lambda e: e.memset(epsc[:, 2:3], 1e-24), w=[epsc])
        op("dve", lambda e: e.memset(epsc[:, 3:4], 1.0), w=[epsc])
        for tl in (Mf, Mb, Sg, Sr, hcar, up_gl):
            op("dve", lambda e, tl=tl: e.memset(tl[:], 0.0), w=[tl])

        for l in range(nlayer):
            dma("pool", up_wa[0:64, l, :], rk_w2[l], w=[up_wa])
            dma("pool", up_wa[64:128, l, :], rk_a2[l], w=[up_wa])
            dma("pool", up_g[:, l, :], rk_g2[l], w=[up_g])
            if l >= 1:
                dma("pool", up_c[0:32, l, :], rk_v2[l - 1], w=[up_c])
            for u_ in range(2):
                for hh_ in range(2):
                    dma("pool", up_gl[32:48, l, u_, 64 * hh_:64 * hh_ + 32], gla_a2[l, :, 64 * u_ + 32 * hh_:64 * u_ + 32 * hh_ + 32], w=[up_gl])

        with ExitStack() as s0:
            condb = P.sb("condb", [128, 8], BF16, s0)
            omu = P.sb("omu", [128, DEPTH, 4, 8], F32, s0)
            ldwf = P.sb("ldwf", [128, 8, 304], F32, s0)
            adaw = [P.sb("adaw%d" % i, [128, 8, 512], BF16, s0) for i in range(2)]
            lwab = P.sb("lwab", [128, 2, 8, 304], BF16, s0)
            wst = P.sb("wst", [128, 8, 384], F32, s0)
            murow = P.sb("murow", [128, 384], F32, s0)
            omurow = P.sb("omurow", [128, 384], F32, s0)
            WABs = P.sb("WABs", [128, 8, 2, 384], BF16, s0)
            scoped = [condb, omu, ldwf, lwab, wst, murow, omurow, WABs] + adaw
            op("act", lambda e: e.activation(out=condb[:], in_=pc(("cT",), 8), func=AF.Silu), r=[pcol], w=[condb])
            for l in range(nlayer):
                mps = ps()
                for piece in range(12):
                    aw = adaw[piece % 2]
                    dma("pool", aw[:], ada_w[l, :, piece * 512:(piece + 1) * 512].rearrange("(kc p) n -> p kc n", p=128), w=[aw])
                    for j in range(4):
                        jc = piece * 4 + j
                        for kc in range(8):
                            op("pe", lambda e, kc=kc, j=j, jc=jc, aw=aw: e.matmul(
                                mps[:, jc:jc + 1], aw[:, kc, j * 128:(j + 1) * 128], condb[:, kc:kc + 1],
                                start=(kc == 0), stop=(kc == 7)), r=[aw, condb], w=[mps])
                op("dve", lambda e, l=l, mps=mps: e.tensor_tensor(out=mod[:, l, :], in0=mps[:, 0:48], in1=pc(("adab", l), 48), op=ALU.add),
                   r=[mps, pcol], w=[mod])
                op("dve", lambda e, l=l: e.scalar_tensor_tensor(out=drv[:, l, 0:8], in0=mod[:, l, 8:16], scalar=1.0, in1=pc(("n1g", l), 8),
                                                                op0=ALU.add, op1=ALU.mult), r=[mod, pcol], w=[drv])
                op("dve", lambda e, l=l: e.scalar_tensor_tensor(out=drv[:, l, 8:16], in0=mod[:, l, 32:40], scalar=1.0, in1=pc(("n2g", l), 8),
                                                                op0=ALU.add, op1=ALU.mult), r=[mod, pcol], w=[drv])
                op("dve", lambda e, l=l: e.tensor_scalar(out=drv[:, l, 16:20], in0=pc(("ka", l), 4), scalar1=-1.0, scalar2=1.0,
                                                         op0=ALU.mult, op1=ALU.add), r=[pcol], w=[drv])
                for j in range(4):
                    src = pc(("mux", l, j), 8) if j < 3 else pc(("muv", l), 8)
                    op("dve", lambda e, l=l, j=j, src=src: e.tensor_scalar(out=omu[:, l, j, :], in0=src, scalar1=-1.0, scalar2=1.0,
                                                                           op0=ALU.mult, op1=ALU.add), r=[pcol], w=[omu])
                op("dve", lambda e: e.memset(ldwf[:], 0.0), w=[ldwf])
                rr = lambda a: a.rearrange("(kc p) n -> p kc n", p=128)
                dma("sp", ldwf[:, :, 0:64], rr(rk_w1[l]), w=[ldwf])
                dma("sp", ldwf[:, :, 64:128], rr(rk_a1[l]), w=[ldwf])
                dma("sp", ldwf[:, :, 128:256], rr(rk_g1[l]), w=[ldwf])
                if l >= 1:
                    dma("sp", ldwf[:, :, 256:288], rr(rk_v1[l - 1]), w=[ldwf])
                dma("sp", ldwf[:, :, 288:304], rr(gla_a1[l]), w=[ldwf])
                for kc in range(8):
                    for j, (c0, c1) in enumerate(((0, 64), (64, 128), (128, 256), (256, 288))):
                        mu_ap = (pc(("mux", l, j), 8) if j < 3 else pc(("muv", l), 8))[:, kc:kc + 1]
                        op("dve", lambda e, kc=kc, c0=c0, c1=c1, mu_ap=mu_ap, l=l: e.tensor_scalar(
                            out=lwab[:, 1, kc, c0:c1], in0=ldwf[:, kc, c0:c1], scalar1=mu_ap, scalar2=None, op0=ALU.mult),
                            r=[ldwf, pcol], w=[lwab])
                        op("dve", lambda e, kc=kc, c0=c0, c1=c1, j=j, l=l: e.tensor_scalar(
                            out=lwab[:, 0, kc, c0:c1], in0=ldwf[:, kc, c0:c1], scalar1=omu[:, l, j, kc:kc + 1], scalar2=None, op0=ALU.mult),
                            r=[ldwf, omu], w=[lwab])
                    op("dve", lambda e, kc=kc, l=l: e.tensor_copy(out=lwab[:, 0, kc, 288:304], in_=ldwf[:, kc, 288:304]), r=[ldwf], w=[lwab])
                    op("dve", lambda e, kc=kc, l=l: e.memset(lwab[:, 1, kc, 288:304], 0.0), w=[lwab])
                dma("sp", lw_d[l], lwab[:].rearrange("q a k n -> q (a k n)"), r=[lwab], w=[lw_slots[l]])
                for hf in range(2):
                    aw = adaw[hf]
                    dma("pool", aw[:], w_out[l, :, hf * 512:(hf + 1) * 512].rearrange("(kc q) n -> q kc n", q=128), w=[aw])
                    dma("sp", wo_d[l].rearrange("q (k n) -> q k n", k=8)[:, :, hf * 512:(hf + 1) * 512], aw[:], r=[aw], w=[wo_slots[l]])
                for p in range(4):
                    for j in range(3):
                        dma("sp", wst[:, :, j * 128:(j + 1) * 128],
                            w_in[l, :, j * 512 + p * 128:j * 512 + (p + 1) * 128].rearrange("(kc q) n -> q kc n", q=128), w=[wst])
                        dma("sp", murow[:, j * 128:(j + 1) * 128], mu_rkv[l, j:j + 1, p * 128:(p + 1) * 128].partition_broadcast(128), w=[murow])
                    op("dve", lambda e: e.tensor_scalar(out=omurow[:], in0=murow[:], scalar1=-1.0, scalar2=1.0, op0=ALU.mult, op1=ALU.add),
                       r=[murow], w=[omurow])
                    op("dve", lambda e: e.tensor_tensor(out=WABs[:, :, 1, :], in0=wst[:], in1=murow[:].unsqueeze(1).to_broadcast([128, 8, 384]), op=ALU.mult),
                       r=[wst, murow], w=[WABs])
                    op("dve", lambda e: e.tensor_tensor(out=WABs[:, :, 0, :], in0=wst[:], in1=omurow[:].unsqueeze(1).to_broadcast([128, 8, 384]), op=ALU.mult),
                       r=[wst, omurow], w=[WABs])
                    dma("sp", wab_d[l * 4 + p], WABs[:].rearrange("q k a n -> q (k a n)"), r=[WABs], w=[wab_slots[l * 4 + p]])
            P.add_fence(scoped)

        def rmsnorm_to_hT(gm, sh, es_l):
            sqb = P.sb("sqb", [128, 8, ST], BF16, es_l)
            rstd = P.sb("rstd", [128, ST], F32, es_l)
            tmpf = [P.sb("tmpf%d" % i, [128, ST], F32, es_l) for i in range(2)]
            for st in range(SEG // ST):
                ts = slice(st * ST, (st + 1) * ST)
                mps = ps()
                for c in range(8):
                    op("act", lambda e, c=c: e.activation(out=sqb[:, c, :], in_=xT[:, c, ts], func=AF.Square), r=[xT], w=[sqb.sub(c)])
                    op("pe", lambda e, c=c: e.matmul(mps[:], ones_mean[:], sqb[:, c, :], start=(c == 0), stop=(c == 7)),
                       r=[ones_mean, sqb.sub(c)], w=[mps])
                op("act", lambda e: e.activation(out=rstd[:], in_=mps[:], func=AF.Ln, bias=epsc[:, 0:1], scale=1.0), r=[mps, epsc], w=[rstd])
                op("act", lambda e: e.activation(out=rstd[:], in_=rstd[:], func=AF.Exp, scale=-0.5), r=[rstd], w=[rstd])
                for c in range(8):
                    tf = tmpf[c % 2]
                    op("dve", lambda e, c=c, tf=tf: e.scalar_tensor_tensor(out=tf[:], in0=xT[:, c, ts], scalar=gm[:, c:c + 1], in1=rstd[:],
                                                                          op0=ALU.mult, op1=ALU.mult), r=[xT, rstd, drv, pcol], w=[tf])
                    dst = hT[:, c, 1 + st * ST:1 + (st + 1) * ST]
                    if sh is not None:
                        op("act", lambda e, c=c, tf=tf, dst=dst: e.activation(out=dst, in_=tf[:], func=AF.Identity, bias=sh[:, c:c + 1], scale=1.0),
                           r=[tf, mod], w=[hT.sub(st)])
                    else:
                        op("act", lambda e, tf=tf, dst=dst: e.activation(out=dst, in_=tf[:], func=AF.Copy), r=[tf], w=[hT.sub(st)])
            return [sqb, rstd] + tmpf

        def hslots():
            return [hT.sub(i) for i in range(SEG // ST)] + [hT.sub("c0")]

        STR = int(os.environ.get('STR', '256'))
        NCR = STR // CH

        def run_streams(gens, periods=None):
            live = list(gens)
            per = dict((id(g_), (periods[i] if periods else 1)) for i, g_ in enumerate(gens))
            rnd = 0
            while live:
                for g_ in list(live):
                    if rnd % per[id(g_)] != 0 and len(live) > 1:
                        continue
                    try:
                        next(g_)
                    except StopIteration:
                        live.remove(g_)
                rnd += 1

        def rwkv_pair(l, p, es_u, tiles):
            def T(name, shape, dt):
                t = P.sb(name, shape, dt, es_u)
                tiles.append(t)
                return t
            WAB = T("WAB", [128, 8, 2, 384], BF16)
            dma("sp", WAB[:].rearrange("q k a n -> q (k a n)"), wab_d[l * 4 + p], r=[wab_slots[l * 4 + p]], w=[WAB])
            cs = slice(p * 128, (p + 1) * 128)
            col = lambda nm: pcol[:, poff[(nm, l)] + p:poff[(nm, l)] + p + 1]
            HP = (slice(0, 64), slice(64, 128))

            def v3(ap):
                return ap.rearrange("q (c t) -> q c t", c=NCR)

            def stream(sidx):
                f = lambda n: T(n, [128, STR], F32)
                sig, cum, a_t, g_t, r_t, k_t, v_t, kk_t, kt_t, tA, tB, eG, eI, eX, eE, bon = [f(n) for n in
                    ("sig", "cum", "a_t", "g_t", "r_t", "k_t", "v_t", "kk_t", "kt_t", "tA", "tB", "eG", "eI", "eX", "eE", "bon")]
                tbf = T("tbf", [128, STR], BF16)
                RK = T("RK", [128, NCR, 2, 64], BF16)
                LK = T("LK", [128, NCR, 2, 64], BF16)
                EF = T("EF", [128, 2, STR], BF16)
                vbf = T("vbf", [128, STR], BF16)
                Et = T("Et", [128, NCR, 2, 64], BF16)
                Vt = T("Vt", [128, NCR, 64], BF16)
                APs = T("APs", [128, NCR, 128], BF16)
                BQs = T("BQs", [128, NCR, 128], BF16)
                Apl = [T("Apl%d" % i, [128, NCR, 64], BF16) for i in range(2)]
                ATr = [T("ATr%d" % i, [128, NCR, 64], BF16) for i in range(2)]
                X = [T("X%d" % i, [128, NCR, 128], BF16) for i in range(2)]
                Gt = T("Gt", [128, NCR, 64], BF16)
                Yt = T("Yt", [128, NCR, 64], BF16)
                ysq = T("ysq", [128, STR], F32)
                st1 = T("st1", [128, NCR, 4], F32)
                yn = T("yn", [128, NCR, 64], BF16)
                yield
                for st in range(sidx, SEG // STR, 2):
                    ts = slice(st * STR, (st + 1) * STR)
                    cur = slice(1 + st * STR, 1 + (st + 1) * STR)
                    prv = slice(st * STR, (st + 1) * STR)
                    st5 = (st * STR) // ST
                    hs = hslots()
                    for j, dst in enumerate((r_t, k_t, v_t)):
                        pp = ps()
                        for kc in range(8):
                            op("pe", lambda e, j=j, pp=pp, kc=kc: e.matmul(pp[:, 0:STR], WAB[:, kc, 0, j * 128:(j + 1) * 128], hT[:, kc, cur],
                                                                          start=(kc == 0), stop=False), r=[WAB] + hs, w=[pp])
                            op("pe", lambda e, j=j, pp=pp, kc=kc: e.matmul(pp[:, 0:STR], WAB[:, kc, 1, j * 128:(j + 1) * 128], hT[:, kc, prv],
                                                                          start=False, stop=(kc == 7)), r=[WAB] + hs, w=[pp])
                        op("act", lambda e, pp=pp, dst=dst: e.activation(out=dst[:], in_=pp[:, 0:STR], func=AF.Copy), r=[pp], w=[dst])
                        yield
                    pw, pa, pg = ps(), ps(), ps()
                    op("pe", lambda e: e.matmul(pw[:, 0:STR], up_wa[0:64, l, cs], mids[0][0:64, ts], start=True, stop=True), r=[up_wa, mids[0].sub(st5)], w=[pw])
                    op("pe", lambda e: e.matmul(pa[:, 0:STR], up_wa[64:128, l, cs], mids[0][64:128, ts], start=True, stop=True), r=[up_wa, mids[0].sub(st5)], w=[pa])
                    op("pe", lambda e: e.matmul(pg[:, 0:STR], up_g[:, l, cs], mids[1][:, ts], start=True, stop=True), r=[up_g, mids[1].sub(st5)], w=[pg])
                    op("act", lambda e: e.activation(out=sig[:], in_=pw[:, 0:STR], func=AF.Sigmoid, bias=col("w0"), scale=1.0), r=[pw, pcol], w=[sig])
                    op("act", lambda e: e.activation(out=a_t[:], in_=pa[:, 0:STR], func=AF.Sigmoid, bias=col("a0"), scale=1.0), r=[pa, pcol], w=[a_t])
                    op("act", lambda e: e.activation(out=g_t[:], in_=pg[:, 0:STR], func=AF.Copy), r=[pg], w=[g_t])
                    if l >= 1:
                        pvg = ps()
                        op("pe", lambda e: e.matmul(pvg[:, 0:STR], up_c[0:32, l, cs], mids[2][0:32, ts], start=True, stop=True), r=[up_c, mids[2].sub(st5)], w=[pvg])
                        op("act", lambda e: e.activation(out=tA[:], in_=pvg[:, 0:STR], func=AF.Sigmoid, bias=col("v0"), scale=1.0), r=[pvg, pcol], w=[tA])
                        yield
                        op(POOLENG, lambda e: e.tensor_tensor(out=tB[:], in0=vfirst[:, p, ts], in1=v_t[:], op=ALU.subtract), r=[vfirst.sub((p, st)), v_t], w=[tB])
                        op(POOLENG, lambda e: e.tensor_tensor(out=tB[:], in0=tB[:], in1=tA[:], op=ALU.mult), r=[tB, tA], w=[tB])
                        op(POOLENG, lambda e: e.tensor_tensor(out=v_t[:], in0=v_t[:], in1=tB[:], op=ALU.add), r=[v_t, tB], w=[v_t])
                    else:
                        yield
                        op("act", lambda e: e.activation(out=vfirst[:, p, ts], in_=v_t[:], func=AF.Copy), r=[v_t], w=[vfirst.sub((p, st))])
                    op("act", lambda e: e.activation(out=vbf[:], in_=v_t[:], func=AF.Copy), r=[v_t], w=[vbf])
                    yield
                    op("dve", lambda e: e.tensor_tensor_scan(out=cum[:], data0=scanmask[:, 0:STR], data1=sig[:], initial=0.0, op0=ALU.mult, op1=ALU.add),
                       r=[scanmask, sig], w=[cum])
                    op(POOLENG, lambda e: e.tensor_tensor(out=tA[:], in0=cum[:], in1=sig[:], op=ALU.subtract), r=[cum, sig], w=[tA])
                    op(POOLENG, lambda e: e.tensor_tensor(out=v3(tB[:]), in0=v3(cum[:])[:, :, 63:64].to_broadcast([128, NCR, 64]), in1=v3(cum[:]),
                                                        op=ALU.subtract), r=[cum], w=[tB])
                    yield
                    op("act", lambda e: e.activation(out=eG[:], in_=cum[:], func=AF.Exp, scale=-CDEC), r=[cum], w=[eG])
                    op("act", lambda e: e.activation(out=eI[:], in_=cum[:], func=AF.Exp, scale=CDEC), r=[cum], w=[eI])
                    op("act", lambda e: e.activation(out=eX[:], in_=tA[:], func=AF.Exp, scale=-CDEC), r=[tA], w=[eX])
                    op("act", lambda e: e.activation(out=eE[:], in_=tB[:], func=AF.Exp, scale=-CDEC), r=[tB], w=[eE])
                    op("act", lambda e: e.activation(out=kk_t[:], in_=k_t[:], func=AF.Copy, scale=col("kk")), r=[k_t, pcol], w=[kk_t])
                    op("act", lambda e: e.activation(out=tbf[:], in_=kk_t[:], func=AF.Square), r=[kk_t], w=[tbf])
                    pss = ps()
                    op("pe", lambda e: e.matmul(pss[:, 0:STR], ones_bd[:], tbf[:], start=True, stop=True), r=[ones_bd, tbf], w=[pss])
                    op("dve", lambda e: e.tensor_scalar(out=tA[:], in0=pss[:, 0:STR], scalar1=1e-24, scalar2=None, op0=ALU.max), r=[pss], w=[tA])
                    yield
                    op("act", lambda e: e.activation(out=tA[:], in_=tA[:], func=AF.Ln), r=[tA], w=[tA])
                    op("act", lambda e: e.activation(out=tA[:], in_=tA[:], func=AF.Exp, scale=-0.5), r=[tA], w=[tA])
                    op("act", lambda e: e.activation(out=tB[:], in_=a_t[:], func=AF.Identity, scale=col("ka"), bias=drv[:, l, 16 + p:17 + p]),
                       r=[a_t, pcol, drv], w=[tB])
                    op(POOLENG, lambda e: e.tensor_tensor(out=kt_t[:], in0=k_t[:], in1=tB[:], op=ALU.mult), r=[k_t, tB], w=[kt_t])
                    op("dve", lambda e: e.scalar_tensor_tensor(out=tbf[:], in0=r_t[:], scalar=col("rk"), in1=kt_t[:], op0=ALU.mult, op1=ALU.mult),
                       r=[r_t, kt_t, pcol], w=[tbf])
                    pbn = ps()
                    op("pe", lambda e: e.matmul(pbn[:, 0:STR], ones_bd[:], tbf[:], start=True, stop=True), r=[ones_bd, tbf], w=[pbn])
                    op("dve", lambda e: e.tensor_tensor(out=bon[:], in0=pbn[:, 0:STR], in1=v_t[:], op=ALU.mult), r=[pbn, v_t], w=[bon])
                    yield
                    op("dve", lambda e: e.tensor_tensor(out=kk_t[:], in0=kk_t[:], in1=tA[:], op=ALU.mult), r=[kk_t, tA], w=[kk_t])
                    op(POOLENG, lambda e: e.tensor_tensor(out=tA[:], in0=kk_t[:], in1=a_t[:], op=ALU.mult), r=[kk_t, a_t], w=[tA])
                    op("dve", lambda e: e.tensor_tensor(out=RK[:, :, 0, :], in0=v3(kk_t[:]), in1=v3(eX[:]), op=ALU.mult), r=[kk_t, eX], w=[RK])
                    op("dve", lambda e: e.tensor_tensor(out=RK[:, :, 1, :], in0=v3(r_t[:]), in1=v3(eG[:]), op=ALU.mult), r=[r_t, eG], w=[RK])
                    yield
                    op("dve", lambda e: e.scalar_tensor_tensor(out=LK[:, :, 0, :], in0=v3(tA[:]), scalar=-1.0, in1=v3(eI[:]), op0=ALU.mult, op1=ALU.mult),
                       r=[tA, eI], w=[LK])
                    op("dve", lambda e: e.tensor_tensor(out=LK[:, :, 1, :], in0=v3(kt_t[:]), in1=v3(eI[:]), op=ALU.mult), r=[kt_t, eI], w=[LK])
                    op("dve", lambda e: e.scalar_tensor_tensor(out=EF[:, 0, :], in0=tA[:], scalar=-1.0, in1=eE[:], op0=ALU.mult, op1=ALU.mult),
                       r=[tA, eE], w=[EF])
                    op("dve", lambda e: e.tensor_tensor(out=EF[:, 1, :], in0=kt_t[:], in1=eE[:], op=ALU.mult), r=[kt_t, eE], w=[EF])
                    yield
                    pt1, pt2 = ps(), ps()
                    pt1b = pt1[:].bitcast(BF16)
                    pt2b = pt2[:].bitcast(BF16)
                    for c in range(NCR):
                        for h in range(2):
                            hp = HP[h]
                            cc = slice(c * 64, (c + 1) * 64)
                            op("pe", lambda e, c=c, hp=hp, cc=cc: e.transpose(pt1b[hp, c * 128:c * 128 + 64], EF[hp, 0, cc], identb[hp, hp]),
                               r=[EF, identb], w=[pt1])
                            op("pe", lambda e, c=c, hp=hp, cc=cc: e.transpose(pt1b[hp, c * 128 + 64:c * 128 + 128], EF[hp, 1, cc], identb[hp, hp]),
                               r=[EF, identb], w=[pt1])
                            op("pe", lambda e, c=c, hp=hp, cc=cc: e.transpose(pt2b[hp, c * 64:c * 64 + 64], vbf[hp, cc], identb[hp, hp]),
                               r=[vbf, identb], w=[pt2])
                            op("pe", lambda e, c=c, hp=hp: e.transpose(pt2b[hp, 512 + c * 64:512 + c * 64 + 64], RK[hp, c, 0, :], identb[hp, hp]),
                               r=[RK, identb], w=[pt2])
                    op("act", lambda e: e.activation(out=Et[:].rearrange("q c j s -> q (c j s)"), in_=pt1b[:, 0:NCR * 128], func=AF.Copy), r=[pt1], w=[Et])
                    op("dve", lambda e: e.tensor_copy(out=Vt[:].rearrange("q c s -> q (c s)"), in_=pt2b[:, 0:NCR * 64]), r=[pt2], w=[Vt])
                    op("dve", lambda e: e.tensor_copy(out=X[0][:, :, 0:64], in_=pt2b[:, 512:512 + NCR * 64].rearrange("q (c s) -> q c s", c=NCR)), r=[pt2], w=[X[0].sub("k")])
                    yield
                    pap, pbq, pal = ps(), ps(), ps()
                    for c in range(NCR):
                        for h in range(2):
                            hp = HP[h]
                            op("pe", lambda e, c=c, hp=hp: e.matmul(
                                pap[hp, c * 128:(c + 1) * 128], LK[hp, c, 0, :], RK[hp, c, :, :].rearrange("q j s -> q (j s)"),
                                start=True, stop=True), r=[LK, RK], w=[pap])
                            op("pe", lambda e, c=c, hp=hp: e.matmul(
                                pbq[hp, c * 128:(c + 1) * 128], LK[hp, c, 1, :], RK[hp, c, :, :].rearrange("q j s -> q (j s)"),
                                start=True, stop=True), r=[LK, RK], w=[pbq])
                            op("pe", lambda e, c=c, hp=hp: e.matmul(pal[hp, c * 64:(c + 1) * 64], RK[hp, c, 0, :], LK[hp, c, 0, :],
                                                                    start=True, stop=True), r=[LK, RK], w=[pal])
                    mapb = m_ap[:].unsqueeze(1).to_broadcast([128, NCR, 128])
                    op("dve", lambda e: e.tensor_tensor(out=APs[:], in0=pap[:, 0:NCR * 128].rearrange("q (c s) -> q c s", c=NCR), in1=mapb, op=ALU.mult), r=[pap, m_ap], w=[APs])
                    op("dve", lambda e: e.tensor_tensor(out=BQs[:], in0=pbq[:, 0:NCR * 128].rearrange("q (c s) -> q c s", c=NCR), in1=mapb, op=ALU.mult), r=[pbq, m_ap], w=[BQs])
                    op("dve", lambda e: e.tensor_tensor(out=Apl[0][:], in0=v3(pal[:, 0:NCR * 64]), in1=m_low[:].unsqueeze(1).to_broadcast([128, NCR, 64]), op=ALU.mult),
                       r=[pal, m_low], w=[Apl[0]])
                    yield
                    pbv = ps()
                    for c in range(NCR):
                        for h in range(2):
                            hp = HP[h]
                            op("pe", lambda e, c=c, hp=hp: e.matmul(pbv[hp, c * 64:(c + 1) * 64], BQs[hp, c, 0:64], Vt[hp, c, :], start=True, stop=True),
                               r=[BQs, Vt], w=[pbv])
                    op("act", lambda e: e.activation(out=X[0][:, :, 64:128], in_=v3(pbv[:, 0:NCR * 64]), func=AF.Copy), r=[pbv], w=[X[0].sub("v")])
                    yield
                    xc = 0
                    ac = 0

                    def atr(lev, ac_, hp, c):
                        return APs[hp, c, 0:64] if lev == 0 else ATr[ac_][hp, c, :]
                    for lev in range(6):
                        px = ps()
                        xi, xo = X[xc], X[1 - xc]
                        asl = [APs] if lev == 0 else [ATr[ac]]
                        for c in range(NCR):
                            for h in range(2):
                                hp = HP[h]
                                op("pe", lambda e, c=c, hp=hp, xi=xi, ac=ac, lev=lev, px=px: e.matmul(
                                    px[hp, c * 128:(c + 1) * 128], atr(lev, ac, hp, c), xi[hp, c, :], start=True, stop=True),
                                   r=asl + [xi.sub("k"), xi.sub("v")], w=[px])
                        op("dve", lambda e, xi=xi, xo=xo, px=px: e.tensor_tensor(
                            out=xo[:], in0=px[:, 0:NCR * 128].rearrange("q (c s) -> q c s", c=NCR), in1=xi[:], op=ALU.add),
                           r=[px, xi.sub("k"), xi.sub("v")], w=[xo.sub("k"), xo.sub("v")])
                        xc = 1 - xc
                        if lev < 5:
                            pq2 = ps()
                            for c in range(NCR):
                                for h in range(2):
                                    hp = HP[h]
                                    if lev < 4:
                                        op("pe", lambda e, c=c, hp=hp, ac=ac, lev=lev, pq2=pq2: e.matmul(pq2[hp, c * 64:(c + 1) * 64], atr(lev, ac, hp, c), Apl[ac][hp, c, :],
                                                                                                    start=True, stop=True), r=asl + [Apl[ac]], w=[pq2])
                                    op("pe", lambda e, c=c, hp=hp, ac=ac, lev=lev, pq2=pq2: e.matmul(pq2[hp, 256 + c * 64:256 + (c + 1) * 64], Apl[ac][hp, c, :], atr(lev, ac, hp, c),
                                                                                                start=True, stop=True), r=asl + [Apl[ac]], w=[pq2])
                            nac = 1 - ac
                            if lev < 4:
                                op("act", lambda e, nac=nac, pq2=pq2: e.activation(out=Apl[nac][:], in_=v3(pq2[:, 0:NCR * 64]), func=AF.Copy), r=[pq2], w=[Apl[nac]])
                            op("act", lambda e, nac=nac, pq2=pq2: e.activation(out=ATr[nac][:], in_=v3(pq2[:, 256:256 + NCR * 64]), func=AF.Copy), r=[pq2], w=[ATr[nac]])
                            ac = nac
                        yield
                    XF = X[xc]
                    xfs = [XF.sub("k"), XF.sub("v")]
                    pgy = ps()
                    for c in range(NCR):
                        for h in range(2):
                            hp = HP[h]
                            op("pe", lambda e, c=c, hp=hp: e.matmul(pgy[hp, c * 64:(c + 1) * 64], XF[hp, c, 0:64], Et[hp, c, 0, :], start=True, stop=True),
                               r=xfs + [Et], w=[pgy])
                            op("pe", lambda e, c=c, hp=hp: e.matmul(pgy[hp, 256 + c * 64:256 + (c + 1) * 64], XF[hp, c, 0:64], APs[hp, c, 64:128], start=True, stop=True),
                               r=xfs + [APs], w=[pgy])
                    op("act", lambda e: e.activation(out=Gt[:], in_=v3(pgy[:, 0:NCR * 64]), func=AF.Copy), r=[pgy], w=[Gt])
                    op("dve", lambda e: e.tensor_tensor(out=Yt[:], in0=v3(pgy[:, 256:256 + NCR * 64]), in1=RK[:, :, 1, :], op=ALU.add), r=[pgy, RK], w=[Yt])
                    yield
                    py = ps(hold=True)
                    msl = Mf.sub((l, p))
                    mbs = Mb.sub((l, p))
                    for c in range(NCR):
                        pm = ps()
                        for h in range(2):
                            hp = HP[h]
                            yo = py[hp, c * 64:(c + 1) * 64]
                            op("pe", lambda e, c=c, hp=hp, yo=yo: e.matmul(yo, APs[hp, c, 64:128], XF[hp, c, 64:128], start=True, stop=False),
                               r=[APs] + xfs, w=[py])
                            op("pe", lambda e, c=c, hp=hp, yo=yo: e.matmul(yo, BQs[hp, c, 64:128], Vt[hp, c, :], start=False, stop=False),
                               r=[BQs, Vt], w=[py])
                            op("pe", lambda e, c=c, hp=hp, yo=yo: e.matmul(yo, Yt[hp, c, :], Mb[hp, l, p, :], start=False, stop=True),
                               r=[Yt, mbs], w=[py])
                            mo = pm[hp, 0:64]
                            op("pe", lambda e, c=c, hp=hp, mo=mo: e.matmul(mo, Et[hp, c, 0, :], XF[hp, c, 64:128], start=True, stop=False),
                               r=[Et] + xfs, w=[pm])
                            op("pe", lambda e, c=c, hp=hp, mo=mo: e.matmul(mo, Et[hp, c, 1, :], Vt[hp, c, :], start=False, stop=False),
                               r=[Et, Vt], w=[pm])
                            op("pe", lambda e, c=c, hp=hp, mo=mo: e.matmul(mo, Gt[hp, c, :], Mb[hp, l, p, :], start=False, stop=True),
                               r=[Gt, mbs], w=[pm])
                        op("dve", lambda e, c=c, pm=pm: e.scalar_tensor_tensor(out=Mb[:, l, p, :], in0=Mf[:, l, p, :], scalar=eG[:, c * 64 + 63:c * 64 + 64],
                                                                               in1=pm[:, 0:64], op0=ALU.mult, op1=ALU.add), r=[msl, eG, pm], w=[mbs])
                        op("dve", lambda e, c=c, pm=pm: e.scalar_tensor_tensor(out=Mf[:, l, p, :], in0=Mf[:, l, p, :], scalar=eG[:, c * 64 + 63:c * 64 + 64],
                                                                               in1=pm[:, 0:64], op0=ALU.mult, op1=ALU.add), r=[msl, eG, pm], w=[msl])
                    release(py)
                    py3 = v3(py[:, 0:NCR * 64])
                    op("dve", lambda e: e.tensor_reduce(out=st1[:, :, 0], in_=py3, axis=AX.X, op=ALU.add), r=[py], w=[st1])
                    op("act", lambda e: e.activation(out=ysq[:], in_=py[:, 0:STR], func=AF.Square), r=[py], w=[ysq])
                    op("dve", lambda e: e.tensor_reduce(out=st1[:, :, 1], in_=v3(ysq[:]), axis=AX.X, op=ALU.add), r=[ysq], w=[st1])
                    op("dve", lambda e: e.tensor_scalar(out=st1[:, :, 0], in0=st1[:, :, 0], scalar1=1.0 / 64, scalar2=None, op0=ALU.mult), r=[st1], w=[st1])
                    op("dve", lambda e: e.tensor_tensor(out=st1[:, :, 2], in0=st1[:, :, 0], in1=st1[:, :, 0], op=ALU.mult), r=[st1], w=[st1])
                    op("dve", lambda e: e.scalar_tensor_tensor(out=st1[:, :, 1], in0=st1[:, :, 1], scalar=1.0 / 64, in1=st1[:, :, 2], op0=ALU.mult, op1=ALU.subtract),
                       r=[st1], w=[st1])
                    op("act", lambda e: e.activation(out=st1[:, :, 1], in_=st1[:, :, 1], func=AF.Ln, bias=epsc[:, 1:2], scale=1.0), r=[st1, epsc], w=[st1])
                    op("act", lambda e: e.activation(out=st1[:, :, 1], in_=st1[:, :, 1], func=AF.Exp, scale=-0.5), r=[st1], w=[st1])
                    op("dve", lambda e: e.tensor_tensor(out=v3(ysq[:]), in0=py3, in1=st1[:, :, 0:1].to_broadcast([128, NCR, 64]), op=ALU.subtract),
                       r=[py, st1], w=[ysq])
                    op("dve", lambda e: e.tensor_tensor(out=yn[:], in0=v3(ysq[:]), in1=st1[:, :, 1:2].to_broadcast([128, NCR, 64]), op=ALU.mult),
                       r=[ysq, st1], w=[yn])
                    yield
                    pto = ps()
                    ptob = pto[:].bitcast(BF16)
                    for c in range(NCR):
                        for h in range(2):
                            hp = HP[h]
                            op("pe", lambda e, c=c, hp=hp: e.transpose(ptob[hp, c * 64:(c + 1) * 64], yn[hp, c, :], identb[hp, hp]), r=[yn, identb], w=[pto])
                    op("act", lambda e: e.activation(out=tA[:], in_=ptob[:, 0:STR], func=AF.Identity, scale=col("lng"), bias=col("lnb")), r=[pto, pcol], w=[tA])
                    op("dve", lambda e: e.tensor_tensor(out=tA[:], in0=tA[:], in1=bon[:], op=ALU.add), r=[tA, bon], w=[tA])
                    op("dve", lambda e: e.tensor_tensor(out=yT[:, p, ts], in0=tA[:], in1=g_t[:], op=ALU.mult), r=[tA, g_t], w=[yT.sub((p, st))])
                    yield
            return [stream(0), stream(1)]

        def glaret_unit(l, u, is_ret, es_u, tiles):
            def T(name, shape, dt):
                t = P.sb(name, shape, dt, es_u)
                tiles.append(t)
                return t
            yidx = 4 + (2 if is_ret else 0) + u
            base = 2304 if is_ret else 1536
            qc0 = base + 64 * u
            kc0 = base + 128 + 64 * u
            vc0 = base + 256 + 128 * u
            gc0 = base + 512 + 128 * u
            ncol = 768 if is_ret else 512
            Wt = T("Wt", [128, 8, ncol], BF16)
            rr = lambda c0, n: w_in[l, :, c0:c0 + n].rearrange("(kc q) n -> q kc n", q=128)
            op("dve", lambda e: e.memset(Wt[:], 0.0), w=[Wt])
            for hh in range(2):
                dma("pool", Wt[:, :, 64 * hh:64 * hh + 32], rr(qc0 + 32 * hh, 32), w=[Wt])
                dma("pool", Wt[:, :, 128 + 64 * hh:128 + 64 * hh + 32], rr(kc0 + 32 * hh, 32), w=[Wt])
            dma("pool", Wt[:, :, 256:384], rr(vc0, 128), w=[Wt])
            dma("pool", Wt[:, :, 384:512], rr(gc0, 128), w=[Wt])
            if is_ret:
                for j, c0 in enumerate((qc0, kc0)):
                    for hh in range(2):
                        for jj in range(2):
                            dma("pool", Wt[:, :, 512 + 128 * j + 64 * hh + 16 * jj:512 + 128 * j + 64 * hh + 16 * jj + 16],
                                rr(c0 + 32 * hh + 16 * (1 - jj), 16), w=[Wt])
            for _ in range(GIDLE):
                yield
            f = lambda n: T(n, [128, ST], F32)
            tA, tB, g_fm = [f(n) for n in ("gtA", "gtB", "g_fm")]
            if not is_ret:
                e1, e2, e3 = [f(n) for n in ("ge1", "ge2", "ge3")]
            if is_ret:
                rC = T("rC", [128, ST], F32)
                rS = T("rS", [128, ST], F32)
            qa = T("qa", [128, ST], BF16)
            qb = T("qb", [128, ST], BF16)
            ka = T("ka", [128, ST], BF16)
            kb = T("kb", [128, ST], BF16)
            kend = T("kend", [128, ST], BF16)
            vbf = T("gvbf", [128, ST], BF16)
            kendT = T("kendT", [128, 8, 64], BF16)
            Vt = T("gVt", [128, 8, 64], BF16)
            P1 = T("P1", [128, 8, 64], BF16)
            P2 = T("P2", [128, 8, 64], BF16)
            Sball = T("Sball", [128, 8, 64], BF16)
            decc = T("decc", [128, 8], F32)
            ysq = T("gysq", [128, ST], F32)
            st2 = T("st2", [128, 8, 4], F32)
            yn = T("gyn", [128, 8, 64], BF16)
            Sst = (Sr if is_ret else Sg)
            ssl = Sst.sub((l, u))
            HP = (slice(0, 64), slice(64, 128))
            HD = HP

            def v3(ap):
                return ap.rearrange("q (c t) -> q c t", c=8)

            for st in range(SEG // ST):
                ts = slice(st * ST, (st + 1) * ST)
                cur = slice(1 + st * ST, 1 + (st + 1) * ST)
                hs = hslots()
                pq, pk, pv, pg = ps(), ps(), ps(), ps()
                for j, pp in enumerate((pq, pk, pv, pg)):
                    for kc in range(8):
                        op("pe", lambda e, kc=kc, j=j, pp=pp: e.matmul(pp[:], Wt[:, kc, j * 128:(j + 1) * 128], hT[:, kc, cur], start=(kc == 0), stop=(kc == 7)),
                           r=[Wt] + hs, w=[pp])
                op("act", lambda e: e.activation(out=vbf[:], in_=pv[:], func=AF.Copy), r=[pv], w=[vbf])
                op("act", lambda e: e.activation(out=g_fm[:], in_=pg[:], func=AF.Silu), r=[pg], w=[g_fm])
                chk("g_proj")
                if not is_ret:
                    pz = ps()
                    op("pe", lambda e: e.matmul(pz[:], up_gl[32:48, l, u, :], mids[2][32:48, ts], start=True, stop=True),
                       r=[up_gl, mids[2].sub(st)], w=[pz])
                    gb = pcol[:, poff[("glab", l)] + u:poff[("glab", l)] + u + 1]
                    op("act", lambda e: e.activation(out=tA[:], in_=pz[:], func=AF.Sigmoid, bias=gb, scale=1.0), r=[pz, pcol], w=[tA])
                    op("act", lambda e: e.activation(out=tA[:], in_=tA[:], func=AF.Ln), r=[tA], w=[tA])
                    op("dve", lambda e: e.tensor_tensor_scan(out=tB[:], data0=scanmask[:], data1=tA[:], initial=0.0, op0=ALU.mult, op1=ALU.add),
                       r=[scanmask, tA], w=[tB])
                    op("act", lambda e: e.activation(out=e1[:], in_=tB[:], func=AF.Exp, scale=1.0 / 16), r=[tB], w=[e1])
                    op("act", lambda e: e.activation(out=e2[:], in_=tB[:], func=AF.Exp, scale=-1.0 / 16), r=[tB], w=[e2])
                    op("dve", lambda e: e.tensor_tensor(out=v3(tA[:]), in0=v3(tB[:])[:, :, 63:64].to_broadcast([128, 8, 64]), in1=v3(tB[:]), op=ALU.subtract),
                       r=[tB], w=[tA])
                    op("act", lambda e: e.activation(out=e3[:], in_=tA[:], func=AF.Exp, scale=1.0 / 16), r=[tA], w=[e3])
                    op("dve", lambda e: e.tensor_copy(out=decc[:], in_=v3(e1[:])[:, :, 63]), r=[e1], w=[decc])
                    sc = 32 ** -0.5
                    op("dve", lambda e: e.scalar_tensor_tensor(out=qa[:], in0=pq[:], scalar=sc, in1=e1[:], op0=ALU.mult, op1=ALU.mult), r=[pq, e1], w=[qa])
                    op("dve", lambda e: e.scalar_tensor_tensor(out=qb[:], in0=pq[:], scalar=sc, in1=e2[:], op0=ALU.mult, op1=ALU.mult), r=[pq, e2], w=[qb])
                    op("dve", lambda e: e.tensor_tensor(out=ka[:], in0=pk[:], in1=e2[:], op=ALU.mult), r=[pk, e2], w=[ka])
                    op("dve", lambda e: e.tensor_tensor(out=kb[:], in0=pk[:], in1=e1[:], op=ALU.mult), r=[pk, e1], w=[kb])
                    op("dve", lambda e: e.tensor_tensor(out=kend[:], in0=pk[:], in1=e3[:], op=ALU.mult), r=[pk, e3], w=[kend])
                else:
                    pqs, pks = ps(), ps()
                    for j, pp in enumerate((pqs, pks)):
                        for kc in range(8):
                            op("pe", lambda e, kc=kc, j=j, pp=pp: e.matmul(pp[:], Wt[:, kc, 512 + j * 128:512 + (j + 1) * 128], hT[:, kc, cur], start=(kc == 0), stop=(kc == 7)),
                               r=[Wt] + hs, w=[pp])
                    g0 = seg_tok0[0] + st * ST
                    dma("sp", rC[:], cd["ropeC"][:, g0:g0 + ST], w=[rC])
                    dma("sp", rS[:], cd["ropeS"][:, g0:g0 + ST], w=[rS])
                    for (px, pxs, dst) in ((pq, pqs, qa), (pk, pks, ka)):
                        op("dve", lambda e, px=px: e.tensor_tensor(out=tA[:], in0=px[:], in1=rC[:], op=ALU.mult), r=[px, rC], w=[tA])
                        op("dve", lambda e, pxs=pxs: e.tensor_tensor(out=tB[:], in0=pxs[:], in1=rS[:], op=ALU.mult), r=[pxs, rS], w=[tB])
                        op("dve", lambda e, dst=dst: e.tensor_tensor(out=dst[:], in0=tA[:], in1=tB[:], op=ALU.add), r=[tA, tB], w=[dst])
                    op("dve", lambda e: e.tensor_tensor(out=v3(qb[:]), in0=v3(qa[:]), in1=qdec[:, 64 * u:64 * u + 64].unsqueeze(1).to_broadcast([128, 8, 64]), op=ALU.mult),
                       r=[qa, qdec], w=[qb])
                    op("dve", lambda e: e.tensor_tensor(out=v3(kend[:]), in0=v3(ka[:]), in1=kdec[:, 64 * u:64 * u + 64].unsqueeze(1).to_broadcast([128, 8, 64]), op=ALU.mult),
                       r=[ka, kdec], w=[kend])
                chk("g_dec")
                yield
                ptv = ps()
                ptvb = ptv[:].bitcast(BF16)
                for c in range(8):
                    cc = slice(c * 64, (c + 1) * 64)
                    for hh in range(2):
                        hp, hd = HP[hh], HD[hh]
                        op("pe", lambda e, c=c, hp=hp, cc=cc: e.transpose(ptvb[hp, c * 64:(c + 1) * 64], vbf[hp, cc], identb[hp, hp]), r=[vbf, identb], w=[ptv])
                        op("pe", lambda e, c=c, hp=hp, cc=cc: e.transpose(ptvb[hp, 512 + c * 64:512 + (c + 1) * 64], kend[hp, cc], identb[hp, hp]), r=[kend, identb], w=[ptv])
                op("act", lambda e: e.activation(out=Vt[:].rearrange("q a b -> q (a b)"), in_=ptvb[:, 0:512], func=AF.Copy), r=[ptv], w=[Vt])
                op("dve", lambda e: e.tensor_copy(out=kendT[:].rearrange("q a b -> q (a b)"), in_=ptvb[:, 512:1024]), r=[ptv], w=[kendT])
                chk("g_tr")
                yield
                p1 = ps()
                p2 = ps() if not is_ret else None
                for c in range(8):
                    cc = slice(c * 64, (c + 1) * 64)
                    for hh in range(2):
                        hp, hd = HP[hh], HD[hh]
                        op("pe", lambda e, hp=hp, hd=hd, cc=cc: e.matmul(p1[hp, cc], ka[hd, cc], qa[hd, cc], start=True, stop=True), r=[ka, qa], w=[p1])
                        if not is_ret:
                            op("pe", lambda e, hp=hp, hd=hd, cc=cc: e.matmul(p2[hp, cc], kb[hd, cc], qb[hd, cc], start=True, stop=True), r=[kb, qb], w=[p2])
                if not is_ret:
                    op("dve", lambda e: e.tensor_tensor(out=P1[:], in0=v3(p1[:]), in1=m_ap[:, 64:128].unsqueeze(1).to_broadcast([128, 8, 64]), op=ALU.mult), r=[p1, m_ap], w=[P1])
                    op("dve", lambda e: e.tensor_tensor(out=P2[:], in0=v3(p2[:]), in1=m_low[:].unsqueeze(1).to_broadcast([128, 8, 64]), op=ALU.mult), r=[p2, m_low], w=[P2])
                else:
                    op("dve", lambda e: e.tensor_tensor(out=P1[:], in0=v3(p1[:]), in1=retD[:, 64 * u:64 * u + 64].unsqueeze(1).to_broadcast([128, 8, 64]), op=ALU.mult),
                       r=[p1, retD], w=[P1])
                chk("g_sc")
                yield
                pkv = ps()
                for c in range(8):
                    for hh in range(2):
                        hp, hd = HP[hh], HD[hh]
                        op("pe", lambda e, c=c, hp=hp, hd=hd: e.matmul(pkv[hd, c * 64:(c + 1) * 64], kendT[hp, c, :], Vt[hp, c, :], start=True, stop=True), r=[kendT, Vt], w=[pkv])
                for c in range(8):
                    for hh in range(2):
                        hd = HD[hh]
                        op("act", lambda e, c=c, hd=hd: e.activation(out=Sball[hd, c, :], in_=Sst[hd, l, u, :], func=AF.Copy), r=[ssl], w=[Sball])
                        dsc = cdec[hd, u:u + 1] if is_ret else decc[hd, c:c + 1]
                        op("dve", lambda e, c=c, dsc=dsc, hd=hd: e.scalar_tensor_tensor(out=Sst[hd, l, u, :], in0=Sst[hd, l, u, :], scalar=dsc, in1=pkv[hd, c * 64:(c + 1) * 64],
                                                                                       op0=ALU.mult, op1=ALU.add), r=[ssl, pkv, decc, cdec], w=[ssl])
                chk("g_kv")
                yield
                po = ps()
                qi = qa if not is_ret else qb
                for c in range(8):
                    cc = slice(c * 64, (c + 1) * 64)
                    for hh in range(2):
                        hp, hd = HP[hh], HD[hh]
                        oo = po[hp, cc]
                        op("pe", lambda e, oo=oo, hp=hp, c=c: e.matmul(oo, P1[hp, c, :], Vt[hp, c, :], start=True, stop=False), r=[P1, Vt], w=[po])
                        if not is_ret:
                            op("pe", lambda e, oo=oo, hp=hp, c=c: e.matmul(oo, P2[hp, c, :], Vt[hp, c, :], start=False, stop=False), r=[P2, Vt], w=[po])
                        op("pe", lambda e, oo=oo, hd=hd, cc=cc, c=c: e.matmul(oo, qi[hd, cc], Sball[hd, c, :], start=False, stop=True), r=[qi, Sball], w=[po])
                chk("g_o")
                po3 = v3(po[:])
                if is_ret:
                    op("dve", lambda e: e.tensor_reduce(out=st2[:, :, 0], in_=po3, axis=AX.X, op=ALU.add), r=[po], w=[st2])
                    op("dve", lambda e: e.tensor_scalar(out=st2[:, :, 0], in0=st2[:, :, 0], scalar1=1.0 / 64, scalar2=None, op0=ALU.mult), r=[st2], w=[st2])
                    op("dve", lambda e: e.tensor_tensor(out=v3(tA[:]), in0=po3, in1=st2[:, :, 0:1].to_broadcast([128, 8, 64]), op=ALU.subtract), r=[po, st2], w=[tA])
                else:
                    op("act", lambda e: e.activation(out=tA[:], in_=po[:], func=AF.Copy), r=[po], w=[tA])
                op("act", lambda e: e.activation(out=ysq[:], in_=tA[:], func=AF.Square), r=[tA], w=[ysq])
                op("dve", lambda e: e.tensor_reduce(out=st2[:, :, 1], in_=v3(ysq[:]), axis=AX.X, op=ALU.add), r=[ysq], w=[st2])
                op("act", lambda e: e.activation(out=st2[:, :, 1], in_=st2[:, :, 1], func=AF.Ln, bias=epsc[:, 0:1], scale=1.0 / 64), r=[st2, epsc], w=[st2])
                op("act", lambda e: e.activation(out=st2[:, :, 1], in_=st2[:, :, 1], func=AF.Exp, scale=-0.5), r=[st2], w=[st2])
                op("dve", lambda e: e.tensor_tensor(out=yn[:], in0=v3(tA[:]), in1=st2[:, :, 1:2].to_broadcast([128, 8, 64]), op=ALU.mult), r=[tA, st2], w=[yn])
                chk("g_norm")
                yield
                pto = ps()
                ptob = pto[:].bitcast(BF16)
                for c in range(8):
                    for hh in range(2):
                        hp = HP[hh]
                        op("pe", lambda e, c=c, hp=hp: e.transpose(ptob[hp, c * 64:(c + 1) * 64], yn[hp, c, :], identb[hp, hp]), r=[yn, identb], w=[pto])
                if not is_ret:
                    gl = pcol[:, poff[("glng", l)]:poff[("glng", l)] + 1]
                    op("dve", lambda e: e.scalar_tensor_tensor(out=yT[:, yidx, ts], in0=ptob[:, 0:512], scalar=gl, in1=g_fm[:], op0=ALU.mult, op1=ALU.mult),
                       r=[pto, pcol, g_fm], w=[yT.sub((yidx, st))])
                else:
                    op("dve", lambda e: e.tensor_tensor(out=yT[:, yidx, ts], in0=ptob[:, 0:512], in1=g_fm[:], op=ALU.mult), r=[pto, g_fm], w=[yT.sub((yidx, st))])
                yield

        seg_tok0 = [0]
        out_slots = []

        def chk(tag):
            if stop == tag:
                P.stopped = True
        try:
          chk("setup")
          for seg in range(nseg):
              t0 = seg * SEG
              seg_tok0[0] = t0
              with ExitStack() as s1:
                  xst = [P.sb("xst%d" % i, [128, D], F32, s1) for i in range(2)]
                  for tt in range(SEG // 128):
                      xs = xst[tt % 2]
                      dma("sp", xs[:], x_d[t0 + tt * 128:t0 + (tt + 1) * 128, :], w=[xs])
                      for half in range(2):
                          pp = ps()
                          for j in range(4):
                              c = half * 4 + j
                              op("pe", lambda e, c=c, j=j, xs=xs, pp=pp: e.transpose(pp[:, j * 128:(j + 1) * 128], xs[:, c * 128:(c + 1) * 128], identf[:]),
                                 r=[xs, identf], w=[pp])
                          eng = "act" if half == 0 else "dve"
                          if eng == "act":
                              op("act", lambda e, half=half, tt=tt, pp=pp: e.activation(out=xT[:, half * 4:(half + 1) * 4, tt * 128:(tt + 1) * 128],
                                                                                        in_=pp[:].rearrange("q (c t) -> q c t", c=4), func=AF.Copy), r=[pp], w=[xT])
                          else:
                              op("dve", lambda e, half=half, tt=tt, pp=pp: e.tensor_copy(out=xT[:, half * 4:(half + 1) * 4, tt * 128:(tt + 1) * 128],
                                                                                         in_=pp[:].rearrange("q (c t) -> q c t", c=4)), r=[pp], w=[xT])
                  P.add_fence(xst)
              chk("xload")

              for l in range(nlayer):
                  with ExitStack() as s2:
                      if seg == 0:
                          op("dve", lambda e: e.memset(hT[:, :, 0:1], 0.0), w=[hT.sub("c0")])
                      else:
                          op("dve", lambda e: e.tensor_copy(out=hT[:, :, 0:1], in_=hcar[:, l, :].unsqueeze(2)), r=[hcar.sub(l)], w=[hT.sub("c0")])
                      tl = rmsnorm_to_hT(drv[:, l, 0:8], mod[:, l, 0:8], s2)
                      op("dve", lambda e: e.tensor_copy(out=hcar[:, l, :].unsqueeze(2), in_=hT[:, :, SEG:SEG + 1]), r=hslots(), w=[hcar.sub(l)])
                      P.add_fence(tl)
                  chk("norm1")
                  s2b = ExitStack()
                  lw = P.sb("lw", [128, 2, 8, 304], BF16, s2b)
                  dma("sp", lw[:].rearrange("q a k n -> q (a k n)"), lw_d[l], r=[lw_slots[l]], w=[lw])
                  for st in range(SEG // ST):
                      cur = slice(1 + st * ST, 1 + (st + 1) * ST)
                      prv = slice(st * ST, (st + 1) * ST)
                      ts = slice(st * ST, (st + 1) * ST)
                      hs = hslots()
                      for ci, (c0, c1) in enumerate(((0, 128), (128, 256), (256, 304))):
                          m = c1 - c0
                          pp = ps()
                          for kc in range(8):
                              op("pe", lambda e, kc=kc, pp=pp, c0=c0, c1=c1, m=m: e.matmul(pp[0:m, :], lw[:, 0, kc, c0:c1], hT[:, kc, cur], start=(kc == 0), stop=False),
                                 r=[lw] + hs, w=[pp])
                              op("pe", lambda e, kc=kc, pp=pp, c0=c0, c1=c1, m=m: e.matmul(pp[0:m, :], lw[:, 1, kc, c0:c1], hT[:, kc, prv], start=False, stop=(kc == 7)),
                                 r=[lw] + hs, w=[pp])
                          if ci == 0:
                              op("act", lambda e, pp=pp: e.activation(out=mids[0][0:64, ts], in_=pp[0:64, :], func=AF.Tanh), r=[pp], w=[mids[0].sub(st)])
                              op("act", lambda e, pp=pp: e.activation(out=mids[0][64:128, ts], in_=pp[64:128, :], func=AF.Copy), r=[pp], w=[mids[0].sub(st)])
                          elif ci == 1:
                              op("act", lambda e, pp=pp: e.activation(out=mids[1][:, ts], in_=pp[:], func=AF.Sigmoid), r=[pp], w=[mids[1].sub(st)])
                          else:
                              op("act", lambda e, pp=pp: e.activation(out=mids[2][0:48, ts], in_=pp[0:48, :], func=AF.Copy), r=[pp], w=[mids[2].sub(st)])
                  P.add_fence([lw])
                  s2b.close()
                  chk("lora")
                  if MIXMODE == 2:
                      for p in (0, 2):
                          with ExitStack() as s3:
                              tl_ = []
                              run_streams(rwkv_pair(l, p, s3, tl_) + rwkv_pair(l, p + 1, s3, tl_))
                              P.add_fence(tl_)
                      for is_ret in (False, True):
                          with ExitStack() as s3:
                              tl_ = []
                              run_streams([glaret_unit(l, u, is_ret, s3, tl_) for u in range(2)])
                              P.add_fence(tl_)
                  elif MIXMODE == 1:
                      for p in range(4):
                          with ExitStack() as s3:
                              tl_ = []
                              gens = rwkv_pair(l, p, s3, tl_) + [glaret_unit(l, p % 2, p >= 2, s3, tl_)]
                              run_streams(gens, periods=[1, 1, GPER])
                              P.add_fence(tl_)
                          chk("rwkv%d" % p)
                  else:
                      for p in range(4):
                          with ExitStack() as s3:
                              tl_ = []
                              run_streams(rwkv_pair(l, p, s3, tl_))
                              P.add_fence(tl_)
                          chk("rwkv%d" % p)
                      for is_ret in (False, True):
                          with ExitStack() as s3:
                              tl_ = []
                              run_streams([glaret_unit(l, u, is_ret, s3, tl_) for u in range(2)])
                              P.add_fence(tl_)
                  with ExitStack() as s6:
                      ngrp = (NFF + FFG - 1) // FFG
                      wgu = [P.sb("wgu%d" % i, [128, 8, 2, FFG * 128], BF16, s6) for i in range(2)]
                      wdn = [P.sb("wdn%d" % i, [128, FFG, D], BF16, s6) for i in range(2)]
                      sg = [P.sb("sg%d" % i, [128, ST], F32, s6) for i in range(2)]
                      aT = [P.sb("aT%d" % i, [128, FFG, ST], BF16, s6) for i in range(2)]
                      ai = 0

                      def ffn_load(g):
                          f0 = g * FFG
                          nf = min(FFG, NFF - f0)
                          wg_t, wd_t = wgu[g % 2], wdn[g % 2]
                          dma("pool", wg_t[:, :, 0, 0:nf * 128], wg_d[l, :, f0 * 128:(f0 + nf) * 128].rearrange("(kc q) n -> q kc n", q=128), w=[wg_t])
                          dma("pool", wg_t[:, :, 1, 0:nf * 128], wu_d[l, :, f0 * 128:(f0 + nf) * 128].rearrange("(kc q) n -> q kc n", q=128), w=[wg_t])
                          dma("pool", wd_t[:, 0:nf, :], wd_d[l, f0 * 128:(f0 + nf) * 128, :].rearrange("(f q) n -> q f n", q=128), w=[wd_t])
                      ffn_load(0)
                      ffn_load(1)
                      with ExitStack() as s4:
                          wo = P.sb("wo", [128, 8, D], BF16, s4)
                          dma("sp", wo[:].rearrange("q k n -> q (k n)"), wo_d[l], r=[wo_slots[l]], w=[wo])
                          ysl = yT.all_slots()
                          for st in range(SEG // ST):
                              ts = slice(st * ST, (st + 1) * ST)
                              for c in range(8):
                                  pp = ps()
                                  for kc in range(8):
                                      op("pe", lambda e, kc=kc, c=c, pp=pp: e.matmul(pp[:], wo[:, kc, c * 128:(c + 1) * 128], yT[:, kc, ts], start=(kc == 0), stop=(kc == 7)),
                                         r=[wo] + ysl, w=[pp])
                                  op("dve", lambda e, c=c, pp=pp: e.scalar_tensor_tensor(out=xT[:, c, ts], in0=pp[:], scalar=mod[:, l, 16 + c:17 + c], in1=xT[:, c, ts],
                                                                                         op0=ALU.mult, op1=ALU.add), r=[pp, mod, xT], w=[xT])
                          P.add_fence([wo])
                      chk("wout")
                      with ExitStack() as s5:
                          tl = rmsnorm_to_hT(drv[:, l, 8:16], mod[:, l, 24:32], s5)
                          P.add_fence(tl)
                      hs = hslots()
                      for g in range(ngrp):
                          f0 = g * FFG
                          nf = min(FFG, NFF - f0)
                          wg_t, wd_t = wgu[g % 2], wdn[g % 2]
                          if g >= 2:
                              ffn_load(g)
                          for st in range(SEG // ST):
                              tok = slice(1 + st * ST, 1 + (st + 1) * ST)
                              tx = slice(st * ST, (st + 1) * ST)
                              at = aT[ai % 2]
                              ai += 1
                              for fi in range(nf):
                                  pg_, pu_ = ps(), ps()
                                  for kc in range(8):
                                      op("pe", lambda e, kc=kc, fi=fi, pg_=pg_: e.matmul(pg_[:], wg_t[:, kc, 0, fi * 128:(fi + 1) * 128], hT[:, kc, tok],
                                                                                        start=(kc == 0), stop=(kc == 7)), r=[wg_t] + hs, w=[pg_])
                                  for kc in range(8):
                                      op("pe", lambda e, kc=kc, fi=fi, pu_=pu_: e.matmul(pu_[:], wg_t[:, kc, 1, fi * 128:(fi + 1) * 128], hT[:, kc, tok],
                                                                                        start=(kc == 0), stop=(kc == 7)), r=[wg_t] + hs, w=[pu_])
                                  sgt = sg[fi % 2]
                                  op("act", lambda e, sgt=sgt, pg_=pg_: e.activation(out=sgt[:], in_=pg_[:], func=AF.Silu), r=[pg_], w=[sgt])
                                  op("dve", lambda e, sgt=sgt, pu_=pu_, at=at, fi=fi: e.tensor_tensor(out=at[:, fi, :], in0=pu_[:], in1=sgt[:], op=ALU.mult),
                                     r=[pu_, sgt], w=[at.sub(fi)])
                              for c in range(8):
                                  pd = ps()
                                  for fi in range(nf):
                                      op("pe", lambda e, c=c, fi=fi, at=at, pd=pd: e.matmul(pd[:], wd_t[:, fi, c * 128:(c + 1) * 128], at[:, fi, :],
                                                                                           start=(fi == 0), stop=(fi == nf - 1)), r=[wd_t, at.sub(fi)], w=[pd])
                                  op("dve", lambda e, c=c, pd=pd: e.scalar_tensor_tensor(out=xT[:, c, tx], in0=pd[:], scalar=mod[:, l, 40 + c:41 + c], in1=xT[:, c, tx],
                                                                                         op0=ALU.mult, op1=ALU.add), r=[pd, mod, xT], w=[xT])
                      P.add_fence(wgu + wdn + sg + aT)

              with ExitStack() as s7:
                  sqb = P.sb("fsqb", [128, 8, ST], BF16, s7)
                  rstd = P.sb("frstd", [128, ST], F32, s7)
                  tfall = P.sb("tfall", [128, 8, ST], F32, s7)
                  ost = [P.sb("ost%d" % i, [128, D], F32, s7) for i in range(2)]
                  gmf = pc(("nfg",), 8)
                  oi = 0
                  for st in range(SEG // ST):
                      ts = slice(st * ST, (st + 1) * ST)
                      mps = ps()
                      for c in range(8):
                          op("act", lambda e, c=c: e.activation(out=sqb[:, c, :], in_=xT[:, c, ts], func=AF.Square), r=[xT], w=[sqb.sub(c)])
                          op("pe", lambda e, c=c: e.matmul(mps[:], ones_mean[:], sqb[:, c, :], start=(c == 0), stop=(c == 7)),
                             r=[ones_mean, sqb.sub(c)], w=[mps])
                      op("act", lambda e: e.activation(out=rstd[:], in_=mps[:], func=AF.Ln, bias=epsc[:, 0:1], scale=1.0), r=[mps, epsc], w=[rstd])
                      op("act", lambda e: e.activation(out=rstd[:], in_=rstd[:], func=AF.Exp, scale=-0.5), r=[rstd], w=[rstd])
                      for c in range(8):
                          op("dve", lambda e, c=c: e.scalar_tensor_tensor(out=tfall[:, c, :], in0=xT[:, c, ts], scalar=gmf[:, c:c + 1], in1=rstd[:],
                                                                          op0=ALU.mult, op1=ALU.mult), r=[xT, rstd, pcol], w=[tfall])
                      for tt in range(ST // 128):
                          o_t = ost[oi % 2]
                          oi += 1
                          for half in range(2):
                              pp = ps()
                              for j in range(4):
                                  c = half * 4 + j
                                  op("pe", lambda e, c=c, j=j, pp=pp, tt=tt: e.transpose(pp[:, j * 128:(j + 1) * 128], tfall[:, c, tt * 128:(tt + 1) * 128], identf[:]),
                                     r=[tfall, identf], w=[pp])
                              op("act", lambda e, half=half, o_t=o_t, pp=pp: e.activation(out=o_t[:, half * 512:(half + 1) * 512], in_=pp[:], func=AF.Copy), r=[pp], w=[o_t])
                          r0 = t0 + st * ST + tt * 128
                          dma("sp", out_d[r0:r0 + 128, :], o_t[:], r=[o_t])
                  P.add_fence([sqb, rstd, tfall] + ost)

        except _Stop:
            pass
        P.stopped = False
        if dbg_d is not None:
            src = {"yT": yT, "hT": hT, "xT": xT, "vfirst": vfirst, "mod": mod, "midA": mids[0], "midG": mids[1], "midC": mids[2],
                   "Mf": Mf, "Sg": Sg, "Sr": Sr, "drv": drv}[dbg[0]]
            ap = src[:]
            if len(ap.shape) == 3:
                ap = ap.rearrange("q a b -> q (a b)")
            elif len(ap.shape) == 4:
                ap = ap.rearrange("q a b c -> q (a b c)")
            npart = ap.shape[0]
            dma("pool", dbg_d[0:npart, 0:ap.shape[1]], ap, r=src.all_slots())
        sp = P.E["sp"]
        for so in P.dsems:
            if so.count > 0:
                sp.e.wait_ge(so.h, so.count)
    return nc


_CACHE = {}


def _in_map(inputs, b, consts):
    m = {
        "x": np.ascontiguousarray(inputs["x"][b], dtype=np.float32),
        "pcol": _build_pcol(inputs, b),
    }
    for k in ("ada_w", "w_in", "w_out", "rk_mu_rkv", "rk_w1", "rk_a1", "rk_g1", "rk_v1", "gla_a1", "rk_w2", "rk_a2", "rk_g2",
              "rk_v2", "gla_a2", "gla_ln_g", "ffn_w_gate", "ffn_w_up", "ffn_w_down"):
        m[k] = np.ascontiguousarray(inputs[k], dtype=np.float32)
    for k, v in consts.items():
        m["c_" + k] = v
    return m


def kernel(**inputs):
    inputs = {k: np.asarray(v) for k, v in inputs.items()}
    nb = inputs["x"].shape[0]
    if "nc" not in _CACHE:
        _CACHE["nc"] = build_program()
    nc = _CACHE["nc"]
    consts = _consts()
    in_maps = [_in_map(inputs, b, consts) for b in range(nb)]
    res = run_bass_kernel_spmd(nc, in_maps, core_ids=list(range(nb)))
    out = np.stack([np.asarray(res.results[b]["out"], dtype=np.float32) for b in range(nb)], axis=0)
    return out
```
